# Optimizing a Trainium2 kernel written in Bass

```python
import math
import jax
import jax.numpy as jnp
from jax import lax
import numpy as np


D_MODEL = 1024
BATCH = 4
SEQ = 4096
DEPTH = 4

GRID_W = 64
CTX_LEN = 256
HEAD_DIM = 64
NA_HEADS = 16
NA_WIDTH = NA_HEADS * HEAD_DIM
WIN_R = 8
WIN_C = 16
QB_C = 16
KB_C = QB_C + WIN_C
ROPE_BASE = 10000.0
ATTN_SCALE = HEAD_DIM ** -0.5
SSD_D_INNER = 2 * D_MODEL
SSD_HEADDIM = 64
SSD_HEADS = SSD_D_INNER // SSD_HEADDIM
SSD_GROUPS = 4
SSD_STATE = 128
SSD_CONV = 5
SSD_CHUNK = 128
SSD_CONV_DIM = SSD_D_INNER + 2 * SSD_GROUPS * SSD_STATE
D_FF = 2816
N_BRANCH = 2
N_MOD = 9
EPS = 1e-6
IN_SIZES = (NA_WIDTH, NA_WIDTH, NA_WIDTH, SSD_D_INNER, SSD_CONV_DIM, 2 * SSD_HEADS, N_BRANCH * D_MODEL)
N_IN = 3 * NA_WIDTH + SSD_D_INNER + SSD_CONV_DIM + 2 * SSD_HEADS + N_BRANCH * D_MODEL

kernel_name = 'hybrid_na_ssd_macaron_dit'


def rmsnorm(x, g):
    xf = x.astype(jnp.float32)
    y = xf * lax.rsqrt(jnp.mean(xf * xf, axis=-1, keepdims=True) + EPS)
    return (y * g.astype(jnp.float32)).astype(x.dtype)


def modulate(h, shift, scale):
    return h * (1.0 + scale) + shift


def adaln(v, w, b):
    m = jax.nn.silu(v) @ w + b
    return m.reshape(m.shape[:-1] + (N_MOD, D_MODEL))


def swiglu(h, w_gate, w_up, w_down):
    return (jax.nn.silu(h @ w_gate) * (h @ w_up)) @ w_down


def ffn_sublayer(s, mod, i, g_norm, w_gate, w_up, w_down):
    h = modulate(rmsnorm(s, g_norm), mod[:, :, i], mod[:, :, i + 1])
    return s + 0.5 * mod[:, :, i + 2] * swiglu(h, w_gate, w_up, w_down)


def split_in(p):
    offs = np.cumsum(IN_SIZES)[:-1].tolist()
    return jnp.split(p, offs, axis=-1)


def rope_2d(n_tokens):
    t = jnp.arange(n_tokens, dtype=jnp.int32)
    row = (t // GRID_W).astype(jnp.float32)
    col = (t % GRID_W).astype(jnp.float32)
    n_freq = HEAD_DIM // 4
    inv = ROPE_BASE ** (-jnp.arange(n_freq, dtype=jnp.float32) / n_freq)
    ang = jnp.concatenate([row[:, None] * inv, col[:, None] * inv], axis=-1)[:, None, :]
    return jnp.cos(ang), jnp.sin(ang)


def apply_rope(x, cos, sin):
    half = x.shape[-1] // 2
    xf = x.astype(jnp.float32)
    x1, x2 = xf[..., :half], xf[..., half:]
    return jnp.concatenate([x1 * cos - x2 * sin, x1 * sin + x2 * cos], axis=-1).astype(x.dtype)


def heads(t, n_heads, dim):
    return t.reshape(t.shape[0], t.shape[1], n_heads, dim)


def neighbourhood_attention(q, k, v, k_ctx, v_ctx, rpb):
    bsz, n_lat, n_h, hd = q.shape
    rows = n_lat // GRID_W
    win_r = min(WIN_R, rows)
    n_cb = GRID_W // QB_C
    r_idx = np.arange(rows)
    r0 = np.clip(r_idx - win_r // 2, 0, rows - win_r)
    dr = r0[:, None] + np.arange(win_r)[None, :] - r_idx[:, None]
    key_cols = np.clip(np.arange(n_cb) * QB_C - WIN_C // 2, 0, GRID_W - KB_C)[:, None] + np.arange(KB_C)[None, :]
    q_cols = np.arange(GRID_W).reshape(n_cb, QB_C)
    s_col = np.clip(q_cols - WIN_C // 2, 0, GRID_W - WIN_C)
    valid = (key_cols[:, None, :] >= s_col[..., None]) & (key_cols[:, None, :] < s_col[..., None] + WIN_C)
    dc_idx = np.clip(key_cols[:, None, :] - q_cols[..., None], -(WIN_C - 1), WIN_C - 1) + (WIN_C - 1)
    n_win = win_r * KB_C

    kg = k.reshape(bsz, rows, GRID_W, n_h, hd)
    vg = v.reshape(bsz, rows, GRID_W, n_h, hd)
    q_rows = jnp.moveaxis(q.reshape(bsz, rows, n_cb, QB_C, n_h, hd), 1, 0)

    def row_block(args):
        q_r, r0_r, dr_r = args
        k_rows = lax.dynamic_slice_in_dim(kg, r0_r, win_r, axis=1)
        v_rows = lax.dynamic_slice_in_dim(vg, r0_r, win_r, axis=1)
        k_blk = k_rows[:, :, key_cols]
        v_blk = v_rows[:, :, key_cols]
        s_win = jnp.einsum('bnqhd,binkhd->bhnqik', q_r, k_blk).astype(jnp.float32) * ATTN_SCALE
        bias = rpb[:, dr_r + (WIN_R - 1)][:, :, dc_idx]
        s_win = s_win + jnp.transpose(bias, (0, 2, 3, 1, 4)).astype(jnp.float32)[None]
        s_win = jnp.where(valid[:, :, None, :], s_win, -jnp.inf)
        s_ctx = jnp.einsum('bnqhd,bchd->bhnqc', q_r, k_ctx).astype(jnp.float32) * ATTN_SCALE
        s_all = jnp.concatenate([s_win.reshape(bsz, n_h, n_cb, QB_C, n_win), s_ctx], axis=-1)
        p = jax.nn.softmax(s_all, axis=-1).astype(v.dtype)
        p_win = p[..., :n_win].reshape(bsz, n_h, n_cb, QB_C, win_r, KB_C)
        o = jnp.einsum('bhnqik,binkhd->bnqhd', p_win, v_blk) + jnp.einsum('bhnqc,bchd->bnqhd', p[..., n_win:], v_ctx)
        return o.reshape(bsz, GRID_W, n_h, hd)

    out = lax.map(row_block, (q_rows, jnp.asarray(r0, jnp.int32), jnp.asarray(dr, jnp.int32)))
    return jnp.moveaxis(out, 0, 1).reshape(bsz, n_lat, n_h, hd)


def context_attention(q, k, v):
    s = jnp.einsum('bqhd,bkhd->bhqk', q, k).astype(jnp.float32) * ATTN_SCALE
    p = jax.nn.softmax(s, axis=-1).astype(v.dtype)
    return jnp.einsum('bhqk,bkhd->bqhd', p, v)


def attention_branch(q_l, k_l, v_l, q_c, k_c, v_c, q_norm, k_norm, rpb, cos, sin, need_ctx):
    bsz, n_lat, _ = q_l.shape
    hh = lambda t: heads(t, NA_HEADS, HEAD_DIM)
    ql = apply_rope(rmsnorm(hh(q_l), q_norm), cos, sin)
    kl = apply_rope(rmsnorm(hh(k_l), k_norm), cos, sin)
    kc = rmsnorm(hh(k_c), k_norm)
    vc = hh(v_c)
    o_l = neighbourhood_attention(ql, kl, hh(v_l), kc, vc, rpb).reshape(bsz, n_lat, NA_WIDTH)
    o_c = None
    if need_ctx:
        o_c = context_attention(rmsnorm(hh(q_c), q_norm), kc, vc).reshape(bsz, q_c.shape[1], NA_WIDTH)
    return o_l, o_c


def dwconv_centred(x, w, b):
    k_w, ch = w.shape
    out = lax.conv_general_dilated(x, w[:, None, :].astype(x.dtype), window_strides=(1,),
                                   padding=[(k_w // 2, k_w // 2)],
                                   dimension_numbers=('NWC', 'WIO', 'NWC'), feature_group_count=ch)
    return out + b


def ssd_chunked(x, dt, a, bm, cm, state0, want_y):
    bsz, n_tok, n_h, p_dim = x.shape
    n_g, n_s = bm.shape[2], bm.shape[3]
    hg = n_h // n_g
    q_len = math.gcd(n_tok, SSD_CHUNK)
    nc = n_tok // q_len
    xf = x.astype(jnp.float32).reshape(bsz, nc, q_len, n_g, hg, p_dim)
    dtc = dt.astype(jnp.float32).reshape(bsz, nc, q_len, n_g, hg)
    bc = bm.astype(jnp.float32).reshape(bsz, nc, q_len, n_g, n_s)
    cc = cm.astype(jnp.float32).reshape(bsz, nc, q_len, n_g, n_s)
    a_cum = jnp.cumsum(dtc * a.reshape(n_g, hg), axis=2)
    a_last = a_cum[:, :, -1]
    xdt = xf * dtc[..., None]
    states = jnp.einsum('bcjgn,bcjghp->bcghpn', bc, xdt * jnp.exp(a_last[:, :, None] - a_cum)[..., None])

    def step(s, inp):
        decay, st = inp
        return s * decay[..., None, None] + st, s

    s_final, s_enter = lax.scan(step, state0.astype(jnp.float32),
                                (jnp.moveaxis(jnp.exp(a_last), 1, 0), jnp.moveaxis(states, 1, 0)))
    if not want_y:
        return None, s_final
    s_enter = jnp.moveaxis(s_enter, 0, 1)
    at = jnp.moveaxis(a_cum, 2, -1)
    tri = np.tril(np.ones((q_len, q_len), dtype=bool))
    l_mat = jnp.exp(jnp.where(tri, at[..., :, None] - at[..., None, :], -jnp.inf))
    cb = jnp.einsum('bcign,bcjgn->bcgij', cc, bc)
    y_diag = jnp.einsum('bcghij,bcjghp->bcighp', cb[:, :, :, None] * l_mat, xdt)
    y_off = jnp.einsum('bcign,bcghpn->bcighp', cc, s_enter) * jnp.exp(a_cum)[..., None]
    y = (y_diag + y_off).reshape(bsz, n_tok, n_h, p_dim).astype(x.dtype)
    return y, s_final


def ssd_branch(z_l, xbc_l, dt_l, z_c, xbc_c, dt_c, conv_w, conv_b, dt_bias, a_log, d_skip, norm_w, need_ctx):
    a = -jnp.exp(a_log.astype(jnp.float32))

    def prep(xbc, dt_raw):
        bsz, n_tok, _ = xbc.shape
        u = jax.nn.silu(dwconv_centred(xbc, conv_w, conv_b))
        xs, bm, cm = jnp.split(u, [SSD_D_INNER, SSD_D_INNER + SSD_GROUPS * SSD_STATE], axis=-1)
        dt = jax.nn.softplus(dt_raw.astype(jnp.float32).reshape(bsz, n_tok, 2, SSD_HEADS) + dt_bias.astype(jnp.float32))
        return (heads(xs, SSD_HEADS, SSD_HEADDIM), heads(bm, SSD_GROUPS, SSD_STATE),
                heads(cm, SSD_GROUPS, SSD_STATE), dt)

    xl, bl, cl, dtl = prep(xbc_l, dt_l)
    xc, bcx, ccx, dtc = prep(xbc_c, dt_c)
    bsz = xl.shape[0]
    s0 = jnp.zeros((bsz, SSD_GROUPS, SSD_HEADS // SSD_GROUPS, SSD_HEADDIM, SSD_STATE), jnp.float32)
    flip = lambda t: jnp.flip(t, axis=1)
    yc_f, sc_f = ssd_chunked(xc, dtc[:, :, 0], a[0], bcx, ccx, s0, need_ctx)
    yl_f, _ = ssd_chunked(xl, dtl[:, :, 0], a[0], bl, cl, sc_f, True)
    yc_b, sc_b = ssd_chunked(flip(xc), flip(dtc[:, :, 1]), a[1], flip(bcx), flip(ccx), s0, need_ctx)
    yl_b, _ = ssd_chunked(flip(xl), flip(dtl[:, :, 1]), a[1], flip(bl), flip(cl), sc_b, True)

    def finish(y_f, y_b_rev, xs, z):
        b_, n_tok = xs.shape[0], xs.shape[1]
        y = y_f + flip(y_b_rev) + d_skip[:, None] * xs
        y = (y.reshape(b_, n_tok, SSD_D_INNER) * jax.nn.silu(z)).astype(jnp.float32)
        yg = y.reshape(b_, n_tok, SSD_GROUPS, SSD_D_INNER // SSD_GROUPS)
        yg = yg * lax.rsqrt(jnp.mean(yg * yg, axis=-1, keepdims=True) + EPS)
        return (yg.reshape(b_, n_tok, SSD_D_INNER) * norm_w.astype(jnp.float32)).astype(xs.dtype)

    y_l = finish(yl_f, yl_b, xl, z_l)
    y_c = finish(yc_f, yc_b, xc, z_c) if need_ctx else None
    return y_l, y_c


def hybrid_mixer(h_l, h_c, w_in, q_norm, k_norm, na_rpb, na_w_o, ssd_conv_w, ssd_conv_b, ssd_dt_bias,
                 ssd_a_log, ssd_d, ssd_norm, ssd_w_o, w_out, cos, sin, need_ctx):
    q_l, k_l, v_l, z_l, xbc_l, dt_l, g_l = split_in(h_l @ w_in)
    q_c, k_c, v_c, z_c, xbc_c, dt_c, g_c = split_in(h_c @ w_in)
    a_l, a_c = attention_branch(q_l, k_l, v_l, q_c, k_c, v_c, q_norm, k_norm, na_rpb, cos, sin, need_ctx)
    s_l, s_c = ssd_branch(z_l, xbc_l, dt_l, z_c, xbc_c, dt_c, ssd_conv_w, ssd_conv_b, ssd_dt_bias,
                          ssd_a_log, ssd_d, ssd_norm, need_ctx)

    def merge(a_o, s_o, g):
        g = jax.nn.sigmoid(g.astype(jnp.float32)).astype(a_o.dtype)
        g_a, g_s = jnp.split(g, N_BRANCH, axis=-1)
        return (g_a * (a_o @ na_w_o) + g_s * (s_o @ ssd_w_o)) @ w_out

    y_l = merge(a_l, s_l, g_l)
    y_c = merge(a_c, s_c, g_c) if need_ctx else None
    return y_l, y_c


def setup_inputs(seed: int = 0) -> dict:
    key = jax.random.key(seed)
    ks = iter(jax.random.split(key, 40))
    f32 = jnp.float32

    def nrm(shape, scale):
        return jax.random.normal(next(ks), shape, f32) * scale

    def gain(shape):
        return 1.0 + nrm(shape, 0.05)

    L = DEPTH
    dt0 = jnp.exp(jax.random.uniform(next(ks), (L, 2, SSD_HEADS), f32, math.log(1e-3), math.log(1e-1)))
    dt_bias = dt0 + jnp.log(-jnp.expm1(-dt0))
    a_log = jnp.log(jax.random.uniform(next(ks), (L, 2, SSD_HEADS), f32, 1.0, 16.0))
    return {
        'x': nrm((BATCH, SEQ, D_MODEL), 1.0),
        'c': nrm((BATCH, D_MODEL), 1.0),
        'ctx': nrm((BATCH, CTX_LEN, D_MODEL), 1.0),
        'c_ctx': nrm((D_MODEL,), 1.0),
        'w_ada': nrm((L, D_MODEL, N_MOD * D_MODEL), 0.5 * D_MODEL ** -0.5),
        'b_ada': nrm((L, N_MOD * D_MODEL), 0.05),
        'norm_ffn1': gain((L, D_MODEL)),
        'ffn1_w_gate': nrm((L, D_MODEL, D_FF), D_MODEL ** -0.5),
        'ffn1_w_up': nrm((L, D_MODEL, D_FF), D_MODEL ** -0.5),
        'ffn1_w_down': nrm((L, D_FF, D_MODEL), D_FF ** -0.5),
        'norm_mix': gain((L, D_MODEL)),
        'w_in': nrm((L, D_MODEL, N_IN), D_MODEL ** -0.5),
        'q_norm': gain((L, HEAD_DIM)),
        'k_norm': gain((L, HEAD_DIM)),
        'na_rpb': nrm((L, NA_HEADS, 2 * WIN_R - 1, 2 * WIN_C - 1), 0.1),
        'na_w_o': nrm((L, NA_WIDTH, D_MODEL), NA_WIDTH ** -0.5),
        'ssd_conv_w': nrm((L, SSD_CONV, SSD_CONV_DIM), SSD_CONV ** -0.5),
        'ssd_conv_b': nrm((L, SSD_CONV_DIM), 0.02),
        'ssd_dt_bias': dt_bias,
        'ssd_a_log': a_log,
        'ssd_d': gain((L, SSD_HEADS)),
        'ssd_norm': gain((L, SSD_D_INNER)),
        'ssd_w_o': nrm((L, SSD_D_INNER, D_MODEL), SSD_D_INNER ** -0.5),
        'w_out': nrm((L, D_MODEL, D_MODEL), D_MODEL ** -0.5),
        'norm_ffn2': gain((L, D_MODEL)),
        'ffn2_w_gate': nrm((L, D_MODEL, D_FF), D_MODEL ** -0.5),
        'ffn2_w_up': nrm((L, D_MODEL, D_FF), D_MODEL ** -0.5),
        'ffn2_w_down': nrm((L, D_FF, D_MODEL), D_FF ** -0.5),
    }


def reference(x, c, ctx, c_ctx, w_ada, b_ada, norm_ffn1, ffn1_w_gate, ffn1_w_up, ffn1_w_down,
              norm_mix, w_in, q_norm, k_norm, na_rpb, na_w_o, ssd_conv_w, ssd_conv_b,
              ssd_dt_bias, ssd_a_log, ssd_d, ssd_norm, ssd_w_o, w_out,
              norm_ffn2, ffn2_w_gate, ffn2_w_up, ffn2_w_down):
    cos, sin = rope_2d(x.shape[1])
    for l in range(DEPTH):
        need_ctx = l < DEPTH - 1
        mod_l = adaln(c, w_ada[l], b_ada[l])[:, None]
        mod_c = adaln(c_ctx, w_ada[l], b_ada[l])[None, None]
        x = ffn_sublayer(x, mod_l, 0, norm_ffn1[l], ffn1_w_gate[l], ffn1_w_up[l], ffn1_w_down[l])
        ctx = ffn_sublayer(ctx, mod_c, 0, norm_ffn1[l], ffn1_w_gate[l], ffn1_w_up[l], ffn1_w_down[l])
        h_l = modulate(rmsnorm(x, norm_mix[l]), mod_l[:, :, 3], mod_l[:, :, 4])
        h_c = modulate(rmsnorm(ctx, norm_mix[l]), mod_c[:, :, 3], mod_c[:, :, 4])
        y_l, y_c = hybrid_mixer(h_l, h_c, w_in[l], q_norm[l], k_norm[l], na_rpb[l], na_w_o[l],
                                ssd_conv_w[l], ssd_conv_b[l], ssd_dt_bias[l], ssd_a_log[l], ssd_d[l],
                                ssd_norm[l], ssd_w_o[l], w_out[l], cos, sin, need_ctx)
        x = x + mod_l[:, :, 5] * y_l
        x = ffn_sublayer(x, mod_l, 6, norm_ffn2[l], ffn2_w_gate[l], ffn2_w_up[l], ffn2_w_down[l])
        if need_ctx:
            ctx = ctx + mod_c[:, :, 5] * y_c
            ctx = ffn_sublayer(ctx, mod_c, 6, norm_ffn2[l], ffn2_w_gate[l], ffn2_w_up[l], ffn2_w_down[l])
    return x
```

```python
import contextlib
import numpy as np
import concourse.bass as bass
import concourse.mybir as mybir
from concourse.bass_utils import run_bass_kernel_spmd

F32 = mybir.dt.float32
BF16 = mybir.dt.bfloat16
ALU = mybir.AluOpType
AF = mybir.ActivationFunctionType

SEM_ROT = 30000
N_DMA_SEMS = 20


class Buf:
    __slots__ = ("name", "last_w", "readers")

    def __init__(self, name=""):
        self.name = name
        self.last_w = None
        self.readers = []


class Op:
    __slots__ = ("eng", "fn", "deps", "ndma", "ticket", "sem", "semval", "signal", "idx")


class Sched:
    ENGS = ("pe", "act", "dve", "pool", "sp")
    uid = 0

    def __init__(self, nc):
        Sched.uid += 1
        self.nc = nc
        self.ops = []
        self.dma_rr = {e: 0 for e in self.ENGS}
        self.dma_last = {}

    def add(self, eng, fn, reads=(), writes=(), ndma=0):
        op = Op()
        op.eng = eng
        op.fn = fn
        op.ndma = ndma
        op.signal = False
        op.ticket = None
        op.sem = None
        op.semval = None
        op.idx = len(self.ops)
        deps = {}
        for b in reads:
            w = b.last_w
            if w is not None:
                deps[w.idx] = w
        for b in writes:
            w = b.last_w
            if w is not None and (w.eng != eng or ndma or w.ndma):
                deps[w.idx] = w
            for r in b.readers:
                if r.eng != eng or ndma or r.ndma:
                    deps[r.idx] = r
        if eng == "pe":
            deps = {k: v for k, v in deps.items() if v.eng != "pe" or v.ndma}
        if ndma:
            slot = self.dma_rr[eng]
            self.dma_rr[eng] = (slot + 1) % N_DMA_SEMS
            prev = self.dma_last.get((eng, slot))
            if prev is not None:
                deps[prev.idx] = prev
            self.dma_last[(eng, slot)] = op
            op.sem = (eng, slot)
        for b in reads:
            b.readers.append(op)
        for b in writes:
            b.last_w = op
            b.readers = []
        op.deps = list(deps.values())
        for d in op.deps:
            d.signal = True
        self.ops.append(op)
        return op

    def emit(self):
        nc = self.nc
        ops = self.ops
        counts = {e: 0 for e in self.ENGS}
        dma_tot = {}
        for op in ops:
            if op.ndma:
                tot = dma_tot.get(op.sem, 0) + 16 * op.ndma
                dma_tot[op.sem] = tot
                op.semval = tot
            elif op.signal:
                counts[op.eng] += 1
                op.ticket = counts[op.eng]
        nsem_eng = {e: max(1, -(-counts[e] // SEM_ROT)) for e in self.ENGS}
        all_sems = []
        with contextlib.ExitStack() as st:
            tick_sems = {}
            for e, n in nsem_eng.items():
                tick_sems[e] = [nc.alloc_semaphore(name="tk%d_%s_%d" % (Sched.uid, e, i)) for i in range(n)]
                all_sems += tick_sems[e]
            dma_sems = {}
            for key in dma_tot:
                dma_sems[key] = nc.alloc_semaphore(name="dm%d_%s_%d" % (Sched.uid, key[0], key[1]))
                all_sems.append(dma_sems[key])
            self.all_sems = all_sems
            block = st.enter_context(nc.Block())

            def tick(op):
                t = op.ticket - 1
                return tick_sems[op.eng][t // SEM_ROT], (t % SEM_ROT) + 1, t // SEM_ROT

            per_eng = {e: [] for e in self.ENGS}
            for op in ops:
                per_eng[op.eng].append(op)

            def run_engine(ename, eng):
                waited = {}
                for op in per_eng[ename]:
                    for d in op.deps:
                        if d.ndma:
                            key = ("d",) + d.sem
                            sem, val = dma_sems[d.sem], d.semval
                        else:
                            sem, val, r = tick(d)
                            key = ("t", d.eng, r)
                        if waited.get(key, 0) >= val:
                            continue
                        waited[key] = val
                        eng.wait_ge(sem, val)
                    ins = op.fn(eng)
                    if op.ndma:
                        ins.then_inc(dma_sems[op.sem], 16)
                    elif op.signal:
                        sem, _, _ = tick(op)
                        ins.then_inc(sem, 1)

            @block.tensor
            def _(e):
                run_engine("pe", e)

            @block.scalar
            def _(e):
                run_engine("act", e)

            @block.vector
            def _(e):
                run_engine("dve", e)

            @block.gpsimd
            def _(e):
                run_engine("pool", e)

            @block.sync
            def _(e):
                run_engine("sp", e)


class T:
    def __init__(self, t, name=""):
        self.t = t
        self.b = Buf(name)

    def __getitem__(self, k):
        return self.t[k]


class TS(T):
    def __init__(self, t, c0, w):
        self.t = t
        self.b = Buf()
        self.c0 = c0
        self.w = w

    def __getitem__(self, k):
        ps, cs = k
        a = self.c0 + (cs.start or 0)
        b = self.c0 + (self.w if cs.stop is None else cs.stop)
        return self.t[ps, a:b]


def _bufs(lst):
    return [x.b if isinstance(x, T) else x for x in lst]


class Rot:
    def __init__(self, items):
        self.items = items
        self.i = 0

    def next(self):
        x = self.items[self.i % len(self.items)]
        self.i += 1
        return x


class Phase:
    uid = 0

    def __init__(self, nc, name):
        self.nc = nc
        self.name = name
        self.S = Sched(nc)
        self.st = contextlib.ExitStack()
        self.stores = []

    def __enter__(self):
        return self

    def __exit__(self, et, ev, tb):
        if et is not None:
            self.st.close()
            return False
        if self.stores:
            self.S.add("sp", lambda e: e.nop(), reads=self.stores)
        with self.nc.named_scope(self.name):
            self.S.emit()
        self.st.close()
        self.nc.all_engine_barrier()
        self.nc.clear_and_free_semaphores(self.S.all_sems)
        self.nc.all_engine_barrier()
        return False

    def _nm(self, p):
        Phase.uid += 1
        return "%s_%s%d" % (self.name, p, Phase.uid)

    def sb(self, shape, dt=F32):
        nm = self._nm("s")
        return T(self.st.enter_context(self.nc.sbuf_tensor(nm, list(shape), dt)), nm)

    def ps(self, shape=(128, 512), dt=F32):
        nm = self._nm("p")
        return T(self.st.enter_context(self.nc.psum_tensor(nm, list(shape), dt)), nm)

    def ps_slots(self, nbanks, w):
        out = []
        for _ in range(nbanks):
            t = self.ps().t
            out += [TS(t, c0, w) for c0 in range(0, 512, w)]
        return out

    def op(self, eng, meth, R, W, *a, **kw):
        return self.S.add(eng, lambda e: getattr(e, meth)(*a, **kw), _bufs(R), _bufs(W))

    def mm(self, out, lhsT, rhs, start, stop, R, W):
        return self.S.add("pe", lambda e: e.matmul(out, lhsT=lhsT, rhs=rhs, start=start, stop=stop),
                          _bufs(R), _bufs(W))

    def dma(self, eng, out, in_, R=(), W=()):
        return self.S.add(eng, lambda e: e.dma_start(out=out, in_=in_), _bufs(R), _bufs(W), ndma=1)

    def store(self, eng, out, in_, R=()):
        b = Buf("st")
        self.stores.append(b)
        return self.S.add(eng, lambda e: e.dma_start(out=out, in_=in_), _bufs(R), [b], ndma=1)


DEPTH = 4
D = 1024
NCH = 8
NL = 4096
CT = 256
NT = NL + CT
DFF = 2816
NFF = 22
NIN = 10304
OQ, OK_, OV, OZ, OX, ODT, OG = 0, 1024, 2048, 3072, 5120, 8192, 8256
EPS = 1e-6
BIG = 32768.0
TILES = [(i * 512, 512, 0) for i in range(8)] + [(NL, CT, 1)]
CHUNKS_F = [NL, NL + 128] + [i * 128 for i in range(32)]
CHUNKS_B = [NL + 128, NL] + [i * 128 for i in range(31, -1, -1)]


class K:
    pass


def dram(nc, name, shape, dt, kind="Internal"):
    return nc.dram_tensor(name, list(shape), dt, kind=kind).ap()


def ph_load_x(k):
    with Phase(k.nc, "ldx") as P:
        ident = P.sb([128, 128]);
        P.dma("sp", ident[:], k.ident, W=[ident])
        xin = Rot([P.sb([128, 4, D]) for _ in range(2)])
        stg = Rot([P.sb([128, NCH, 512]) for _ in range(2)])
        pss = Rot([P.ps() for _ in range(4)])
        cp = Rot(["dve", "act"])
        for (t0, n, v) in TILES:
            nb = n // 128
            xi = xin.next()
            src = k.x[t0:t0 + n, :] if v == 0 else k.ctx[:, :]
            P.dma("sp", xi[:, 0:nb, :], src.rearrange("(b p) d -> p b d", p=128), W=[xi])
            sg = stg.next()
            for b in range(nb):
                for half in range(2):
                    ps = pss.next()
                    for c4 in range(4):
                        c = half * 4 + c4
                        P.mm(ps[:, c4 * 128:(c4 + 1) * 128], xi[:, b, c * 128:(c + 1) * 128], ident[:],
                             True, True, [xi, ident], [ps])
                    e = cp.next()
                    dst = sg[:, half * 4:(half + 1) * 4, b * 128:(b + 1) * 128]
                    srcp = ps[:, :].rearrange("p (c t) -> p c t", c=4)
                    if e == "dve":
                        P.op("dve", "tensor_copy", [ps], [sg], out=dst, in_=srcp)
                    else:
                        P.op("act", "activation", [ps], [sg], out=dst, in_=srcp, func=AF.Copy)
            P.store("sp", k.XR[:, :, t0:t0 + n].rearrange("c p t -> p c t"), sg[:, :, 0:n], R=[sg])


def ph_store_out(k):
    with Phase(k.nc, "sto") as P:
        ident = P.sb([128, 128])
        P.dma("sp", ident[:], k.ident, W=[ident])
        xin = Rot([P.sb([128, NCH, 512]) for _ in range(2)])
        stg = Rot([P.sb([128, 4, D]) for _ in range(2)])
        pss = Rot([P.ps() for _ in range(4)])
        cp = Rot(["dve", "act"])
        for (t0, n, v) in TILES[:8]:
            xi = xin.next()
            P.dma("sp", xi[:], k.XR[:, :, t0:t0 + n].rearrange("c p t -> p c t"), W=[xi])
            sg = stg.next()
            for b in range(4):
                for half in range(2):
                    ps = pss.next()
                    for c4 in range(4):
                        c = half * 4 + c4
                        P.mm(ps[:, c4 * 128:(c4 + 1) * 128], xi[:, c, b * 128:(b + 1) * 128], ident[:],
                             True, True, [xi, ident], [ps])
                    e = cp.next()
                    dst = sg[:, b, half * 512:(half + 1) * 512]
                    if e == "dve":
                        P.op("dve", "tensor_copy", [ps], [sg], out=dst, in_=ps[:, :])
                    else:
                        P.op("act", "activation", [ps], [sg], out=dst, in_=ps[:, :], func=AF.Copy)
            P.store("sp", k.out[t0:t0 + n, :].rearrange("(b p) d -> p b d", p=128), sg[:], R=[sg])


def ph_adaln(k, l):
    with Phase(k.nc, "ada%d" % l) as P:
        cv = P.sb([128, NCH, 2])
        sv = P.sb([128, NCH, 2])
        P.dma("sp", cv[:], k.cvec, W=[cv])
        P.op("act", "activation", [cv], [sv], out=sv[:], in_=cv[:], func=AF.Silu)
        bada = P.sb([128, 72])
        P.dma("sp", bada[:], k.b_ada[l], W=[bada])
        gn = P.sb([128, 3, NCH])
        P.dma("sp", gn[:, 0, :], k.norm_ffn1[l], W=[gn])
        P.dma("sp", gn[:, 1, :], k.norm_mix[l], W=[gn])
        P.dma("sp", gn[:, 2, :], k.norm_ffn2[l], W=[gn])
        wb = Rot([P.sb([128, NCH, 512]) for _ in range(3)])
        pm = P.ps([128, 144])
        wsrc = k.w_ada[l].rearrange("(c p) n -> p c n", p=128)
        for blk in range(18):
            w = wb.next()
            P.dma("sp", w[:], wsrc[:, :, blk * 512:(blk + 1) * 512], W=[w])
            for s in range(4):
                ch = blk * 4 + s
                for kc in range(NCH):
                    P.mm(pm[:, ch * 2:ch * 2 + 2], w[:, kc, s * 128:(s + 1) * 128], sv[:, kc, :],
                         kc == 0, kc == NCH - 1, [w, sv], [pm])
        mod = P.sb([128, 72, 2])
        P.op("dve", "tensor_tensor", [pm, bada], [mod], out=mod[:],
             in0=pm[:, :].rearrange("p (c v) -> p c v", v=2),
             in1=bada[:, :].unsqueeze(2).to_broadcast([128, 72, 2]), op=ALU.add)
        ms = P.sb([128, 9, NCH, 2])
        for i, (gi, scale_idx, shift_idx, gate_idx, gmul) in enumerate(
                [(0, 1, 0, 2, 0.5), (1, 4, 3, 5, 1.0), (2, 7, 6, 8, 0.5)]):
            tmp = P.sb([128, NCH, 2])
            P.op("dve", "tensor_scalar", [mod], [tmp], out=tmp[:], in0=mod[:, scale_idx * 8:scale_idx * 8 + 8, :],
                 scalar1=1.0, scalar2=32.0, op0=ALU.add, op1=ALU.mult)
            P.op("dve", "tensor_tensor", [tmp, gn], [ms], out=ms[:, 3 * i, :, :], in0=tmp[:],
                 in1=gn[:, gi, :].unsqueeze(2).to_broadcast([128, NCH, 2]), op=ALU.mult)
            P.op("dve", "tensor_copy", [mod], [ms], out=ms[:, 3 * i + 1, :, :],
                 in_=mod[:, shift_idx * 8:shift_idx * 8 + 8, :])
            P.op("dve", "tensor_scalar", [mod], [ms], out=ms[:, 3 * i + 2, :, :],
                 in0=mod[:, gate_idx * 8:gate_idx * 8 + 8, :], scalar1=gmul, scalar2=0.0, op0=ALU.mult, op1=ALU.add)
        P.store("sp", k.MODS, ms[:].rearrange("p a c v -> p (a c v)"), R=[ms])


def rsqrt_psum(P, C, ss, r, n, eps_eff, np_=128):
    sqv = C["sqv"].next()
    P.op("act", "activation", [ss, C["epsb"]], [sqv], out=sqv[0:np_, 0:n], in_=ss[0:np_, 0:n], func=AF.Sqrt,
         bias=C["epsb"][0:np_, 0:1], scale=1.0)
    P.op("dve", "reciprocal", [sqv], [r], out=r[0:np_, 0:n], in_=sqv[0:np_, 0:n])


def norm_mod(P, k, C, xt, n, v, slotA, ht):
    ss = C["ps_ss"].next()
    for c in range(NCH):
        sq = C["sq"].next()
        P.op("act", "activation", [xt], [sq], out=sq[:, 0:n], in_=xt[:, c, 0:n], func=AF.Square)
        P.mm(ss[:, 0:n], C["ones"][:], sq[:, 0:n], c == 0, c == NCH - 1, [C["ones"], sq], [ss])
    r = C["r"].next()
    rsqrt_psum(P, C, ss, r, n, float(D * EPS))
    ms = C["ms"]
    for c in range(NCH):
        tm = C["tm"].next()
        P.op("dve", "tensor_tensor", [xt, r], [tm], out=tm[:, 0:n], in0=xt[:, c, 0:n], in1=r[:, 0:n], op=ALU.mult)
        P.op("act", "activation", [tm, ms], [ht], out=ht[:, c, 0:n], in_=tm[:, 0:n], func=AF.Identity,
             scale=ms[:, slotA, c, v:v + 1], bias=ms[:, slotA + 1, c, v:v + 1])


def norm_ctx(P, k):
    C = {}
    C["ones"] = P.sb([128, 128], BF16)
    P.dma("pool", C["ones"][:], k.ones, W=[C["ones"]])
    C["ms"] = P.sb([128, 9, NCH, 2])
    P.dma("sp", C["ms"][:].rearrange("p a c v -> p (a c v)"), k.MODS, W=[C["ms"]])
    C["sq"] = Rot([P.sb([128, 512], BF16) for _ in range(2)])
    C["r"] = Rot([P.sb([128, 512]) for _ in range(1)])
    C["tm"] = Rot([P.sb([128, 512]) for _ in range(2)])
    C["ps_ss"] = Rot([P.ps() for _ in range(1)])
    C["sqv"] = C["tm"]
    C["epsb"] = P.sb([128, 1])
    P.op("dve", "memset", [], [C["epsb"]], C["epsb"][:], float(D * EPS))
    return C


def ph_ffn(k, l, sub, skip_ctx=False):
    wg_d, wu_d, wd_d = (k.ffn1_w_gate, k.ffn1_w_up, k.ffn1_w_down) if sub == 1 else \
        (k.ffn2_w_gate, k.ffn2_w_up, k.ffn2_w_down)
    slotA = 0 if sub == 1 else 6
    with Phase(k.nc, "ffn%d_%d" % (sub, l)) as P:
        C = norm_ctx(P, k)
        wg = P.sb([128, NCH, DFF], BF16)
        wu = P.sb([128, NCH, DFF], BF16)
        wd = P.sb([128, NFF, D], BF16)
        wgb = [Buf(), Buf()]
        wub = [Buf(), Buf()]
        wdb = [Buf(), Buf()]
        gs = wg_d[l].rearrange("(c p) f -> p c f", p=128)
        us = wu_d[l].rearrange("(c p) f -> p c f", p=128)
        ds = wd_d[l].rearrange("(f p) d -> p f d", p=128)
        HF = DFF // 2
        for hf in range(2):
            for c in range(NCH):
                P.dma("pool", wg[:, c, hf * HF:(hf + 1) * HF], gs[:, c, hf * HF:(hf + 1) * HF], W=[wgb[hf]])
            for c in range(NCH):
                P.dma("pool", wu[:, c, hf * HF:(hf + 1) * HF], us[:, c, hf * HF:(hf + 1) * HF], W=[wub[hf]])
        for hf in range(2):
            for f in range(11):
                ff = hf * 11 + f
                P.dma("pool", wd[:, ff, :], ds[:, ff, :], W=[wdb[hf]])
        xts = Rot([P.sb([128, NCH, 512]) for _ in range(2)])
        hts = Rot([P.sb([128, NCH, 512], BF16) for _ in range(1)])
        ats = Rot([P.sb([128, NFF, 512], BF16) for _ in range(1)])
        sgs = Rot([P.sb([128, 512]) for _ in range(2)])
        pgs = Rot([P.ps() for _ in range(2)])
        pus = Rot([P.ps() for _ in range(2)])
        pys = Rot([P.ps() for _ in range(2)])
        tiles = TILES[:8] if skip_ctx else TILES
        for (t0, n, v) in tiles:
            xt = xts.next()
            P.dma("sp", xt[:, :, 0:n], k.XR[:, :, t0:t0 + n].rearrange("c p t -> p c t"), W=[xt])
            ht = hts.next()
            norm_mod(P, k, C, xt, n, v, slotA, ht)
            at = ats.next()
            for f in range(NFF):
                hf = f // 11
                pg = pgs.next()
                pu = pus.next()
                for c in range(NCH):
                    P.mm(pg[:, 0:n], wg[:, c, f * 128:(f + 1) * 128], ht[:, c, 0:n], c == 0, c == NCH - 1,
                         [wgb[hf], ht], [pg])
                for c in range(NCH):
                    P.mm(pu[:, 0:n], wu[:, c, f * 128:(f + 1) * 128], ht[:, c, 0:n], c == 0, c == NCH - 1,
                         [wub[hf], ht], [pu])
                sg = sgs.next()
                P.op("act", "activation", [pg], [sg], out=sg[:, 0:n], in_=pg[:, 0:n], func=AF.Silu)
                P.op("dve", "tensor_tensor", [sg, pu], [at], out=at[:, f, 0:n], in0=sg[:, 0:n], in1=pu[:, 0:n],
                     op=ALU.mult)
            for dc in range(NCH):
                py = pys.next()
                for f in range(NFF):
                    P.mm(py[:, 0:n], wd[:, f, dc * 128:(dc + 1) * 128], at[:, f, 0:n], f == 0, f == NFF - 1,
                         [wdb[f // 11], at], [py])
                P.op("dve", "scalar_tensor_tensor", [py, xt, C["ms"]], [xt], out=xt[:, dc, 0:n], in0=py[:, 0:n],
                     scalar=C["ms"][:, slotA + 2, dc, v:v + 1], in1=xt[:, dc, 0:n], op0=ALU.mult, op1=ALU.add)
            P.store("sp", k.XR[:, :, t0:t0 + n].rearrange("c p t -> p c t"), xt[:, :, 0:n], R=[xt])


def ph_hmix(k, l):
    with Phase(k.nc, "hmix%d" % l) as P:
        C = norm_ctx(P, k)
        xts = Rot([P.sb([128, NCH, 512]) for _ in range(2)])
        hts = Rot([P.sb([128, NCH, 512], BF16) for _ in range(2)])
        for (t0, n, v) in TILES:
            xt = xts.next()
            P.dma("sp", xt[:, :, 0:n], k.XR[:, :, t0:t0 + n].rearrange("c p t -> p c t"), W=[xt])
            ht = hts.next()
            norm_mod(P, k, C, xt, n, v, 3, ht)
            P.store("sp", k.HS[:, :, t0:t0 + n].rearrange("c p t -> p c t"), ht[:, :, 0:n], R=[ht])


def load_w_cols(P, wd, l_ap, c0, ncols, splits=1):
    b = Buf()
    src = l_ap.rearrange("(c p) f -> p c f", p=128)
    for c in range(NCH):
        P.dma("pool", wd[:, c, 0:ncols], src[:, c, c0:c0 + ncols], W=[b])
    return b


def ph_qkv(k, l):
    with Phase(k.nc, "qkv%d" % l) as P:
        w = P.sb([128, NCH, 3072], BF16)
        wb = load_w_cols(P, w, k.w_in[l], 0, 3072)
        blk = P.sb([128, 128], BF16)
        P.dma("pool", blk[:], k.blk, W=[blk])
        rot = P.sb([128, 128])
        P.dma("sp", rot[:], k.rot, W=[rot])
        cosT = P.sb([128, NL])
        sinS = P.sb([128, NL])
        P.dma("sp", cosT[:], k.cosT, W=[cosT])
        P.dma("sp", sinS[:], k.sinS, W=[sinS])
        gq = P.sb([128, 2])
        g8 = P.sb([128, 2])
        P.dma("sp", gq[:, 0:1], k.q_norm[l], W=[gq])
        P.dma("sp", gq[:, 1:2], k.k_norm[l], W=[gq])
        P.op("dve", "tensor_scalar", [gq], [g8], out=g8[:], in0=gq[:], scalar1=8.0, scalar2=0.0, op0=ALU.mult,
             op1=ALU.add)
        C = {"tm": Rot([P.sb([128, 512]) for _ in range(4)]), "epsb": P.sb([128, 1])}
        C["sqv"] = C["tm"]
        P.op("dve", "memset", [], [C["epsb"]], C["epsb"][:], float(64 * EPS))
        hts = Rot([P.sb([128, NCH, 512], BF16) for _ in range(2)])
        qst = Rot([P.sb([128, 8, 512], BF16) for _ in range(2)])
        kst = Rot([P.sb([128, 8, 512], BF16) for _ in range(2)])
        vst = Rot([P.sb([128, 16, 65], BF16) for _ in range(2)])
        for vv in vst.items:
            P.op("pool", "memset", [], [vv], vv[:], 1.0)
        sqs = Rot([P.sb([128, 512], BF16) for _ in range(4)])
        rs = Rot([P.sb([128, 512]) for _ in range(4)])
        qns = Rot([P.sb([128, 512]) for _ in range(4)])
        t1s = Rot([P.sb([128, 512]) for _ in range(4)])
        t2s = Rot([P.sb([128, 512]) for _ in range(4)])
        pqs = Rot([P.ps() for _ in range(3)])
        pss = Rot([P.ps() for _ in range(2)])
        prs = Rot([P.ps() for _ in range(2)])
        pvs = Rot([P.ps() for _ in range(1)])
        for (t0, n, v) in TILES:
            ht = hts.next()
            P.dma("sp", ht[:, :, 0:n], k.HS[:, :, t0:t0 + n].rearrange("c p t -> p c t"), W=[ht])
            for qi, (st_rot, dst) in enumerate([(qst, k.Q), (kst, k.KK)]):
                sg = st_rot.next()
                for hp in range(8):
                    col = qi * 1024 + hp * 128
                    pq = pqs.next()
                    for c in range(NCH):
                        P.mm(pq[:, 0:n], w[:, c, col:col + 128], ht[:, c, 0:n], c == 0, c == NCH - 1, [wb, ht], [pq])
                    sq = sqs.next()
                    P.op("act", "activation", [pq], [sq], out=sq[:, 0:n], in_=pq[:, 0:n], func=AF.Square)
                    ss = pss.next()
                    P.mm(ss[:, 0:n], blk[:], sq[:, 0:n], True, True, [blk, sq], [ss])
                    r = rs.next()
                    rsqrt_psum(P, C, ss, r, n, 0.0)
                    qn = qns.next()
                    P.op("dve", "scalar_tensor_tensor", [pq, r, g8], [qn], out=qn[:, 0:n], in0=pq[:, 0:n],
                         scalar=g8[:, qi:qi + 1], in1=r[:, 0:n], op0=ALU.mult, op1=ALU.mult)
                    if v == 1:
                        P.op("act", "activation", [qn], [sg], out=sg[:, hp, 0:n], in_=qn[:, 0:n], func=AF.Copy)
                        continue
                    pr = prs.next()
                    P.mm(pr[:, 0:n], rot[:], qn[:, 0:n], True, True, [rot, qn], [pr])
                    t1 = t1s.next()
                    P.op("pool", "tensor_tensor", [qn, cosT], [t1], out=t1[:, 0:n], in0=qn[:, 0:n],
                         in1=cosT[:, t0:t0 + n], op=ALU.mult)
                    t2 = t2s.next()
                    P.op("dve", "tensor_tensor", [pr, sinS], [t2], out=t2[:, 0:n], in0=pr[:, 0:n],
                         in1=sinS[:, t0:t0 + n], op=ALU.mult)
                    P.op("pool", "tensor_tensor", [t1, t2], [sg], out=sg[:, hp, 0:n], in0=t1[:, 0:n], in1=t2[:, 0:n],
                         op=ALU.add)
                P.store("sp", dst[:, :, t0:t0 + n].rearrange("c p t -> p c t"), sg[:, :, 0:n], R=[sg])
            for b in range(n // 128):
                vs = vst.next()
                for half in range(2):
                    pv = pvs.next()
                    for c in range(NCH):
                        P.mm(pv[:, :], ht[:, c, b * 128:(b + 1) * 128], w[:, c, 2048 + half * 512:2048 + (half + 1) * 512],
                             c == 0, c == NCH - 1, [wb, ht], [pv])
                    P.op("act", "activation", [pv], [vs], out=vs[:, half * 8:(half + 1) * 8, 0:64],
                         in_=pv[:, :].rearrange("p (h e) -> p h e", e=64), func=AF.Copy)
                P.store("sp", k.V[t0 + b * 128:t0 + (b + 1) * 128, :], vs[:].rearrange("p h e -> p (h e)"), R=[vs])


def ph_zdt(k, l):
    with Phase(k.nc, "zdt%d" % l) as P:
        w = P.sb([128, NCH, 2048 + 64], BF16)
        wb = load_w_cols(P, w, k.w_in[l], OZ, 2048)
        wb2 = Buf()
        src = k.w_in[l].rearrange("(c p) f -> p c f", p=128)
        P.dma("pool", w[:, :, 2048:2112], src[:, :, ODT:ODT + 64], W=[wb2])
        hts = Rot([P.sb([128, NCH, 512], BF16) for _ in range(2)])
        zst = Rot([P.sb([128, 2048]) for _ in range(2)])
        dst_ = Rot([P.sb([128, 64]) for _ in range(2)])
        pzs = Rot([P.ps() for _ in range(4)])
        pds = Rot([P.ps([128, 64]) for _ in range(2)])
        cp = Rot(["act", "dve"])
        for (t0, n, v) in TILES:
            ht = hts.next()
            P.dma("sp", ht[:, :, 0:n], k.HS[:, :, t0:t0 + n].rearrange("c p t -> p c t"), W=[ht])
            for b in range(n // 128):
                zs = zst.next()
                for q4 in range(4):
                    pz = pzs.next()
                    for c in range(NCH):
                        P.mm(pz[:, :], ht[:, c, b * 128:(b + 1) * 128], w[:, c, q4 * 512:(q4 + 1) * 512],
                             c == 0, c == NCH - 1, [wb, ht], [pz])
                    if cp.next() == "act":
                        P.op("act", "activation", [pz], [zs], out=zs[:, q4 * 512:(q4 + 1) * 512], in_=pz[:, :],
                             func=AF.Copy)
                    else:
                        P.op("dve", "tensor_copy", [pz], [zs], out=zs[:, q4 * 512:(q4 + 1) * 512], in_=pz[:, :])
                P.store("sp", k.Z[t0 + b * 128:t0 + (b + 1) * 128, :], zs[:], R=[zs])
                pd = pds.next()
                for c in range(NCH):
                    P.mm(pd[:, :], ht[:, c, b * 128:(b + 1) * 128], w[:, c, 2048:2112], c == 0, c == NCH - 1,
                         [wb2, ht], [pd])
                ds = dst_.next()
                P.op("dve", "tensor_copy", [pd], [ds], out=ds[:], in_=pd[:, :])
                P.store("sp", k.DT[t0 + b * 128:t0 + (b + 1) * 128, :], ds[:], R=[ds])


def ph_proj_fm(k, l, name, c0, nchunks, dst, sigmoid):
    with Phase(k.nc, "%s%d" % (name, l)) as P:
        w = P.sb([128, NCH, nchunks * 128], BF16)
        wb = load_w_cols(P, w, k.w_in[l], c0, nchunks * 128)
        hts = Rot([P.sb([128, NCH, 512], BF16) for _ in range(2)])
        stg = Rot([P.sb([128, 8, 512]) for _ in range(2)])
        pps = Rot([P.ps() for _ in range(4)])
        cp = Rot(["act", "dve"])
        for (t0, n, v) in TILES:
            ht = hts.next()
            P.dma("sp", ht[:, :, 0:n], k.HS[:, :, t0:t0 + n].rearrange("c p t -> p c t"), W=[ht])
            for g8 in range(nchunks // 8):
                sg = stg.next()
                for j in range(8):
                    ch = g8 * 8 + j
                    pp = pps.next()
                    for c in range(NCH):
                        P.mm(pp[:, 0:n], w[:, c, ch * 128:(ch + 1) * 128], ht[:, c, 0:n], c == 0, c == NCH - 1,
                             [wb, ht], [pp])
                    if sigmoid:
                        P.op("act", "activation", [pp], [sg], out=sg[:, j, 0:n], in_=pp[:, 0:n], func=AF.Sigmoid)
                    elif cp.next() == "act":
                        P.op("act", "activation", [pp], [sg], out=sg[:, j, 0:n], in_=pp[:, 0:n], func=AF.Copy)
                    else:
                        P.op("dve", "tensor_copy", [pp], [sg], out=sg[:, j, 0:n], in_=pp[:, 0:n])
                P.store("sp", dst[g8 * 8:(g8 + 1) * 8, :, t0:t0 + n].rearrange("c p t -> p c t"), sg[:, :, 0:n], R=[sg])


def ph_conv(k, l):
    with Phase(k.nc, "conv%d" % l) as P:
        ident = P.sb([128, 128])
        P.dma("sp", ident[:], k.ident, W=[ident])
        identb = P.sb([128, 128], BF16)
        P.dma("pool", identb[:], k.ident, W=[identb])
        cw = P.sb([128, 24, 5])
        cb = P.sb([128, 24])
        P.dma("sp", cw[:], k.conv_w[l], W=[cw])
        P.dma("sp", cb[:], k.conv_b[l], W=[cb])
        xins = Rot([P.sb([128, 8, 516]) for _ in range(2)])
        accs = Rot([P.sb([128, 512]) for _ in range(8)])
        us = Rot([P.sb([128, 512]) for _ in range(8)])
        xst = Rot([P.sb([128, 4, 2048]) for _ in range(1)])
        bfs = Rot([P.sb([128, 4, 512], BF16) for _ in range(2)])
        bts = Rot([P.sb([128, 4, 512], BF16) for _ in range(2)])
        pts = Rot([P.ps() for _ in range(3)])
        ptb = Rot([P.ps([128, 512], BF16) for _ in range(2)])
        tapeng = Rot(["dve"])
        cp = Rot(["dve", "act"])
        for (t0, n, v) in TILES:
            seg0, seg1 = (0, NL) if v == 0 else (NL, NT)
            nb = n // 128
            xs_t = xst.next()
            for g in range(3):
                xin = xins.next()
                lo = max(t0 - 2, seg0)
                hi = min(t0 + n + 2, seg1)
                if lo > t0 - 2:
                    P.op("pool", "memset", [], [xin], xin[:, :, 0:2], 0.0)
                if hi < t0 + n + 2:
                    P.op("pool", "memset", [], [xin], xin[:, :, n + 2:n + 4], 0.0)
                P.dma("sp", xin[:, :, lo - (t0 - 2):hi - (t0 - 2)],
                      k.XBC[g * 8:(g + 1) * 8, :, lo:hi].rearrange("c p t -> p c t"), W=[xin])
                if g == 2:
                    bf = bfs.next()
                    cf = bfs.next()
                    bt = bts.next()
                for quad in range(2):
                    ul = []
                    accl = [accs.next() for _ in range(4)]
                    for j4 in range(4):
                        j = quad * 4 + j4
                        ch = g * 8 + j
                        P.op("act", "activation", [xin, cw, cb], [accl[j4]], out=accl[j4][:, 0:n], in_=xin[:, j, 0:n],
                             func=AF.Identity, scale=cw[:, ch, 0:1], bias=cb[:, ch:ch + 1])
                    for kk in range(1, 5):
                        for j4 in range(4):
                            j = quad * 4 + j4
                            ch = g * 8 + j
                            P.op("dve", "scalar_tensor_tensor", [xin, cw, accl[j4]], [accl[j4]], out=accl[j4][:, 0:n],
                                 in0=xin[:, j, kk:kk + n], scalar=cw[:, ch, kk:kk + 1], in1=accl[j4][:, 0:n],
                                 op0=ALU.mult, op1=ALU.add)
                    for j4 in range(4):
                        acc = accl[j4]
                        if g < 2:
                            u = us.next()
                            P.op("act", "activation", [acc], [u], out=u[:, 0:n], in_=acc[:, 0:n], func=AF.Silu)
                            ul.append(u)
                        elif quad == 0:
                            P.op("act", "activation", [acc], [bf], out=bf[:, j4, 0:n], in_=acc[:, 0:n], func=AF.Silu)
                        else:
                            P.op("act", "activation", [acc], [cf], out=cf[:, j4, 0:n], in_=acc[:, 0:n], func=AF.Silu)
                    if g < 2:
                        q16 = g * 2 + quad
                        for b in range(nb):
                            pt = pts.next()
                            for j4 in range(4):
                                P.mm(pt[:, j4 * 128:(j4 + 1) * 128], ul[j4][:, b * 128:(b + 1) * 128], ident[:],
                                     True, True, [ul[j4], ident], [pt])
                            if cp.next() == "dve":
                                P.op("dve", "tensor_copy", [pt], [xs_t], out=xs_t[:, b, q16 * 512:(q16 + 1) * 512],
                                     in_=pt[:, :])
                            else:
                                P.op("act", "activation", [pt], [xs_t], out=xs_t[:, b, q16 * 512:(q16 + 1) * 512],
                                     in_=pt[:, :], func=AF.Copy)
                    elif quad == 0:
                        for b in range(nb):
                            pb = ptb.next()
                            for j4 in range(4):
                                P.S.add("pe", (lambda e, o=pb[:, j4 * 128:(j4 + 1) * 128], i=bf[:, j4, b * 128:(b + 1) * 128]:
                                               e.transpose(o, i, identb[:])), _bufs([bf, identb]), _bufs([pb]))
                            P.op("dve", "tensor_copy", [pb], [bt], out=bt[:, b, :], in_=pb[:, :])
                        P.store("sp", k.BF[:, :, t0:t0 + n].rearrange("c p t -> p c t"), bf[:, :, 0:n], R=[bf])
                        P.store("sp", k.BT[t0:t0 + n, :].rearrange("(b p) f -> p b f", p=128), bt[:, 0:nb, :], R=[bt])
                    else:
                        P.store("sp", k.CF[:, :, t0:t0 + n].rearrange("c p t -> p c t"), cf[:, :, 0:n], R=[cf])
            P.store("sp", k.XS[t0:t0 + n, :].rearrange("(b p) f -> p b f", p=128), xs_t[:, 0:nb, :], R=[xs_t])


def _qrange(j):
    qlo = 0 if j <= 3 else 2 * j - 4
    qhi = 63 if j >= 28 else 2 * j + 5
    return qlo, qhi


def ph_attn(k, l, need_ctx):
    with Phase(k.nc, "attn%d" % l) as P:
        mf = P.sb([128, 1024], BF16)
        mi = P.sb([128, 1024], BF16)
        P.dma("pool", mf[:], k.maskF, W=[mf])
        P.dma("pool", mi[:], k.maskI, W=[mi])
        kts = Rot([P.sb([128, NT], BF16) for _ in range(2)])
        qts = Rot([P.sb([128, NT], BF16) for _ in range(2)])
        vts = Rot([P.sb([128, 34, 2, 65], BF16) for _ in range(2)])
        aos = Rot([P.sb([128, 34, 128], BF16) for _ in range(2)])
        PT = P.sb([128, 32, 768], BF16)
        PTb = [Buf() for _ in range(32)]
        PC = P.sb([128, 2, NT], BF16)
        PCb = [Buf() for _ in range(9)]
        rbs = Rot([P.sb([128, 1024]) for _ in range(2)])
        ees = Rot([P.sb([128, 1024]) for _ in range(2)])
        efs = Rot([P.sb([128, 1024], BF16) for _ in range(2)])
        eis = Rot([P.sb([128, 1024], BF16) for _ in range(2)])
        exs = Rot([P.sb([128, 512]) for _ in range(3)])
        rcs = Rot([P.sb([128, 8]) for _ in range(2)])
        pss = Rot([P.ps() for _ in range(4)])
        pos = Rot([P.ps() for _ in range(3)])
        nqb = 34 if need_ctx else 32
        for hp in range(8):
            kt = kts.next()
            qt = qts.next()
            vt = vts.next()
            ao = aos.next()
            P.dma("sp", kt[:], k.KK[hp], W=[kt])
            P.dma("sp", qt[:], k.Q[hp], W=[qt])
            P.dma("sp", vt[:].rearrange("p b a e -> p b (a e)"),
                  k.V[:, hp * 130:(hp + 1) * 130].rearrange("(b p) f -> p b f", p=128), W=[vt])
            for a in range(2):
                h = hp * 2 + a
                pr = slice(a * 64, a * 64 + 64)
                rb = rbs.next()
                P.dma("sp", rb[:], k.RB[l, h], W=[rb])
                ee = ees.next()
                P.op("act", "activation", [rb], [ee], out=ee[:], in_=rb[:], func=AF.Exp)
                ef = efs.next()
                ei = eis.next()
                P.op("pool", "tensor_tensor", [ee, mf], [ef], out=ef[:], in0=ee[:], in1=mf[:], op=ALU.mult)
                P.op("pool", "tensor_tensor", [ee, mi], [ei], out=ei[:], in0=ee[:], in1=mi[:], op=ALU.mult)
                for j in range(32):
                    qlo, qhi = _qrange(j)
                    nq = (qhi - qlo + 1) * 64
                    off = 0
                    while off < nq:
                        ln = min(512, nq - off)
                        ps = pss.next()
                        P.mm(ps[:, 0:ln], kt[pr, j * 128:(j + 1) * 128], qt[pr, qlo * 64 + off:qlo * 64 + off + ln],
                             True, True, [kt, qt], [ps])
                        ex = exs.next()
                        P.op("act", "activation", [ps], [ex], out=ex[:, 0:ln], in_=ps[:, 0:ln], func=AF.Exp, scale=0.125)
                        r0 = qlo + off // 64
                        r1 = r0 + ln // 64
                        qr = r0
                        while qr < r1:
                            full = (qr <= 3 or qr >= 61)
                            qe = qr
                            while qe < r1 and ((qe <= 3 or qe >= 61) == full):
                                qe += 1
                            tab = ef if full else ei
                            s0 = 7 - 2 * j + qr
                            cnt = (qe - qr) * 64
                            assert 0 <= s0 and s0 * 64 + cnt <= 1024
                            P.op("dve", "tensor_tensor", [ex, tab], [PTb[j]],
                                 out=PT[:, j, (qr - qlo) * 64:(qr - qlo) * 64 + cnt],
                                 in0=ex[:, (qr - r0) * 64:(qr - r0) * 64 + cnt], in1=tab[:, s0 * 64:s0 * 64 + cnt],
                                 op=ALU.mult)
                            qr = qe
                        off += ln
                for ti, (t0, n, v) in enumerate(TILES if need_ctx else TILES[:8]):
                    for cc in range(2):
                        ps = pss.next()
                        P.mm(ps[:, 0:n], kt[pr, NL + cc * 128:NL + (cc + 1) * 128], qt[pr, t0:t0 + n], True, True,
                             [kt, qt], [ps])
                        P.op("act", "activation", [ps], [PCb[ti]], out=PC[:, cc, t0:t0 + n], in_=ps[:, 0:n],
                             func=AF.Exp, scale=0.125)
                for g0 in range(0, nqb, 7):
                    grp = list(range(g0, min(nqb, g0 + 7)))
                    po = pos.next()
                    for si, i in enumerate(grp):
                        mms = []
                        if i < 32:
                            for jj in range(32):
                                qlo, qhi = _qrange(jj)
                                if qlo <= 2 * i and 2 * i + 1 <= qhi:
                                    mms.append((PT[:, jj, (2 * i - qlo) * 64:(2 * i - qlo) * 64 + 128],
                                                vt[:, jj, a, :], PTb[jj]))
                            ti = i // 4
                        else:
                            ti = 8
                        for cc in range(2):
                            mms.append((PC[:, cc, i * 128:(i + 1) * 128], vt[:, 32 + cc, a, :], PCb[ti]))
                        for mi_, (lh, rh, bb) in enumerate(mms):
                            P.mm(po[:, si * 65:(si + 1) * 65], lh, rh, mi_ == 0, mi_ == len(mms) - 1, [bb, vt], [po])
                    ng = len(grp)
                    rc = rcs.next()
                    pov = po[:, 0:ng * 65].rearrange("p (s e) -> p s e", e=65)
                    P.op("dve", "reciprocal", [po], [rc], out=rc[:, 0:ng], in_=pov[:, :, 64])
                    P.op("dve", "tensor_tensor", [po, rc], [ao], out=ao[:, g0:g0 + ng, a * 64:(a + 1) * 64],
                         in0=pov[:, :, 0:64], in1=rc[:, 0:ng].unsqueeze(2).to_broadcast([128, ng, 64]), op=ALU.mult)
            P.store("sp", k.AO[0:nqb * 128, hp * 128:(hp + 1) * 128].rearrange("(b p) f -> p b f", p=128),
                    ao[:, 0:nqb, :], R=[ao])


def ssd_consts(P, k, l):
    C = {}
    for nm, src in [("triF", k.triF), ("triB", k.triB), ("nm2F", k.nm2F), ("nm2B", k.nm2B), ("negI", k.negI),
                    ("ones", k.ones), ("ntriF", k.ntriF), ("ntriB", k.ntriB)]:
        C[nm] = P.sb([128, 128], BF16)
        P.dma("pool", C[nm][:], src, W=[C[nm]])
    C["dtb"] = P.sb([128, 64])
    C["alog"] = P.sb([128, 64])
    C["abc"] = P.sb([128, 64])
    C["one"] = P.sb([128, 1])
    P.op("dve", "memset", [], [C["one"]], C["one"][:], 1.0)
    P.dma("sp", C["dtb"][:], k.dt_bias_bc[l], W=[C["dtb"]])
    P.dma("sp", C["alog"][:], k.a_log_bc[l], W=[C["alog"]])
    P.op("act", "activation", [C["alog"]], [C["abc"]], out=C["abc"][:], in_=C["alog"][:], func=AF.Exp)
    P.op("dve", "tensor_scalar", [C["abc"]], [C["abc"]], out=C["abc"][:], in0=C["abc"][:], scalar1=-1.0, scalar2=0.0,
         op0=ALU.mult, op1=ALU.add)
    return C


NBLK = NT // 128


def ssd_prep_all(P, k, C, pbanks):
    W_ = NBLK * 64

    def big(dt_=F32):
        return P.sb([128, NBLK, 64], dt_)

    def bcv(t):
        return t[:, :].unsqueeze(1).to_broadcast([128, NBLK, 64])

    def flat(t):
        return t[:].rearrange("p c f -> p (c f)")

    tmp = big()
    P.dma("sp", tmp[:], k.DT.rearrange("(c p) f -> p c f", p=128), W=[tmp])
    P.op("dve", "tensor_tensor", [tmp, C["dtb"]], [tmp], out=tmp[:], in0=tmp[:], in1=bcv(C["dtb"]), op=ALU.add)
    P.op("act", "activation", [tmp], [tmp], out=flat(tmp), in_=flat(tmp), func=AF.Exp)
    dtv = big()
    P.op("act", "activation", [tmp, C["one"]], [dtv], out=flat(dtv), in_=flat(tmp), func=AF.Ln, bias=C["one"][:, 0:1],
         scale=1.0)
    P.op("dve", "tensor_tensor", [dtv, C["abc"]], [tmp], out=tmp[:], in0=dtv[:], in1=bcv(C["abc"]), op=ALU.mult)
    hi = big(BF16)
    lo = big(BF16)
    P.op("dve", "tensor_copy", [tmp], [hi], out=hi[:], in_=tmp[:])
    P.op("dve", "tensor_tensor", [tmp, hi], [lo], out=lo[:], in0=tmp[:], in1=hi[:], op=ALU.subtract)
    acum = big()
    alast = big()
    pb = Rot(pbanks)
    for d, tri in ((0, C["triF"]), (1, C["triB"])):
        for c0 in range(0, NBLK, 16):
            c1 = min(NBLK, c0 + 16)
            ncol = (c1 - c0) * 32
            ps = pb.next()
            P.mm(ps[:, 0:ncol], tri[:], hi[:, c0:c1, d * 32:(d + 1) * 32], True, False, [tri, hi], [ps])
            P.mm(ps[:, 0:ncol], tri[:], lo[:, c0:c1, d * 32:(d + 1) * 32], False, True, [tri, lo], [ps])
            P.op("dve", "tensor_copy", [ps], [acum], out=acum[:, c0:c1, d * 32:(d + 1) * 32],
                 in_=ps[:, 0:ncol].rearrange("p (c f) -> p c f", f=32))
    for o0 in range(0, W_, 512):
        o1 = min(W_, o0 + 512)
        ps = pb.next()
        P.mm(ps[:, 0:o1 - o0], C["ones"][:], flat(hi)[:, o0:o1], True, False, [C["ones"], hi], [ps])
        P.mm(ps[:, 0:o1 - o0], C["ones"][:], flat(lo)[:, o0:o1], False, True, [C["ones"], lo], [ps])
        P.op("dve", "tensor_copy", [ps], [alast], out=flat(alast)[:, o0:o1], in_=ps[:, 0:o1 - o0])
    eacum = big()
    P.op("act", "activation", [acum], [eacum], out=flat(eacum), in_=flat(acum), func=AF.Exp)
    sv = big()
    P.op("dve", "tensor_tensor", [alast, acum], [sv], out=sv[:], in0=alast[:], in1=acum[:], op=ALU.subtract)
    P.op("act", "activation", [sv], [sv], out=flat(sv), in_=flat(sv), func=AF.Exp)
    P.op("dve", "tensor_tensor", [sv, dtv], [sv], out=sv[:], in0=sv[:], in1=dtv[:], op=ALU.mult)
    ealast = big()
    P.op("act", "activation", [alast], [ealast], out=flat(ealast), in_=flat(alast), func=AF.Exp)
    return dict(dtv=dtv, hi=hi, eacum=eacum, sv=sv, ealast=ealast)


def _bc(ap32, n=32):
    return ap32.unsqueeze(2).to_broadcast([128, n, 64])


def ph_ssd_a(k, l, need_ctx):
    with Phase(k.nc, "ssda%d" % l) as P:
        C = ssd_consts(P, k, l)
        plp = [P.ps() for _ in range(3)]
        A = ssd_prep_all(P, k, C, plp[:2])
        plp = Rot(plp)
        xss = Rot([P.sb([128, 2048]) for _ in range(3)])
        bts = Rot([P.sb([128, 512], BF16) for _ in range(2)])
        bfs = Rot([P.sb([128, 4, 128], BF16) for _ in range(2)])
        cfs = Rot([P.sb([128, 4, 128], BF16) for _ in range(2)])
        xdf = Rot([P.sb([128, 2048], BF16) for _ in range(2)])
        xdb = Rot([P.sb([128, 2048], BF16) for _ in range(2)])
        xsf = Rot([P.sb([128, 2048], BF16) for _ in range(2)])
        yts = Rot([P.sb([128, 2048]) for _ in range(2)])
        lxs = Rot([P.sb([128, 512]) for _ in range(3)])
        wts = Rot([P.sb([128, 512], BF16) for _ in range(3)])
        tms = Rot([P.sb([128, 512]) for _ in range(2)])
        Sf = P.sb([128, 2048])
        Sfb = P.sb([128, 2048], BF16)
        Sg = [Buf() for _ in range(4)]
        Sgb = [Buf() for _ in range(4)]
        P.op("pool", "memset", [], Sg, Sf[:], 0.0)
        P.op("pool", "memset", [], Sgb, Sfb[:], 0.0)
        pcb = Rot([P.ps() for _ in range(1)])
        pys = Rot([P.ps() for _ in range(2)])
        pos = Rot([P.ps() for _ in range(2)])
        for t0 in CHUNKS_F:
            ci = t0 // 128
            is_ctx = t0 >= NL
            do_y = need_ctx or not is_ctx
            xs = xss.next()
            bt = bts.next()
            P.dma("sp", xs[:], k.XS[t0:t0 + 128, :], W=[xs])
            P.dma("sp", bt[:], k.BT[t0:t0 + 128, :], W=[bt])
            xf = xsf.next()
            P.op("dve", "tensor_tensor", [xs, A["sv"]], [xf], out=xf[:].rearrange("p (h e) -> p h e", e=64),
                 in0=xs[:].rearrange("p (h e) -> p h e", e=64), in1=_bc(A["sv"][:, ci, 0:32]), op=ALU.mult)
            if do_y:
                bf = bfs.next()
                cf = cfs.next()
                P.dma("sp", bf[:], k.BF[:, :, t0:t0 + 128].rearrange("c p t -> p c t"), W=[bf])
                P.dma("sp", cf[:], k.CF[:, :, t0:t0 + 128].rearrange("c p t -> p c t"), W=[cf])
                xd = [xdf.next(), xdb.next()]
                for d in range(2):
                    P.op("pool", "tensor_tensor", [xs, A["dtv"]], [xd[d]],
                         out=xd[d][:].rearrange("p (h e) -> p h e", e=64),
                         in0=xs[:].rearrange("p (h e) -> p h e", e=64),
                         in1=_bc(A["dtv"][:, ci, d * 32:(d + 1) * 32]), op=ALU.mult)
                cbp = pcb.next()
                for g in range(4):
                    P.mm(cbp[:, g * 128:(g + 1) * 128], bf[:, g, :], cf[:, g, :], True, True, [bf, cf], [cbp])
                yt = yts.next()
            for g in range(4):
                if do_y:
                    py = pys.next()
                    for quad in range(4):
                        lp = plp.next()
                        prs = [(quad * 2 + q2, d) for q2 in range(2) for d in range(2)]
                        for s_, (hh, d) in enumerate(prs):
                            col = d * 32 + g * 8 + hh
                            tri, ntri, nm2 = (C["triF"], C["ntriF"], C["nm2F"]) if d == 0 else \
                                (C["triB"], C["ntriB"], C["nm2B"])
                            hb = A["hi"][:, ci, col:col + 1].to_broadcast([128, 128])
                            reg = lp[:, s_ * 128:(s_ + 1) * 128]
                            P.mm(reg, hb, tri[:], True, False, [A["hi"], tri], [lp])
                            P.mm(reg, ntri[:], hb, False, False, [A["hi"], ntri], [lp])
                            P.mm(reg, C["negI"][:], nm2[:], False, True, [C["negI"], nm2], [lp])
                        lx = lxs.next()
                        P.op("act", "activation", [lp], [lx], out=lx[:], in_=lp[:, :], func=AF.Exp)
                        wt = wts.next()
                        P.op("dve", "tensor_tensor", [lx, cbp], [wt], out=wt[:].rearrange("p (s i) -> p s i", s=4),
                             in0=lx[:].rearrange("p (s i) -> p s i", s=4),
                             in1=cbp[:, g * 128:(g + 1) * 128].unsqueeze(1).to_broadcast([128, 4, 128]), op=ALU.mult)
                        for s_, (hh, d) in enumerate(prs):
                            h = g * 8 + hh
                            P.mm(py[:, hh * 64:(hh + 1) * 64], wt[:, s_ * 128:(s_ + 1) * 128],
                                 xd[d][:, h * 64:(h + 1) * 64], d == 0, d == 1, [wt, xd[d]], [py])
                    po = pos.next()
                    P.mm(po[:, :], cf[:, g, :], Sfb[:, g * 512:(g + 1) * 512], True, True, [cf, Sgb[g]], [po])
                    tm = tms.next()
                    P.op("dve", "tensor_tensor", [po, A["eacum"]], [tm], out=tm[:].rearrange("p (h e) -> p h e", e=64),
                         in0=po[:, :].rearrange("p (h e) -> p h e", e=64),
                         in1=_bc(A["eacum"][:, ci, g * 8:(g + 1) * 8], 8), op=ALU.mult)
                    P.op("dve", "tensor_tensor", [tm, py], [yt], out=yt[:, g * 512:(g + 1) * 512], in0=tm[:], in1=py[:, :],
                         op=ALU.add)
                pst = pos.next()
                P.mm(pst[:, :], bt[:, g * 128:(g + 1) * 128], xf[:, g * 512:(g + 1) * 512], True, True, [bt, xf], [pst])
                tm2 = tms.next()
                P.op("pool", "tensor_tensor", [Sg[g], A["ealast"]], [tm2], out=tm2[:].rearrange("p (h e) -> p h e", e=64),
                     in0=Sf[:, g * 512:(g + 1) * 512].rearrange("p (h e) -> p h e", e=64),
                     in1=_bc(A["ealast"][:, ci, g * 8:(g + 1) * 8], 8), op=ALU.mult)
                P.op("dve", "tensor_tensor", [tm2, pst], [Sg[g]], out=Sf[:, g * 512:(g + 1) * 512], in0=tm2[:], in1=pst[:, :],
                     op=ALU.add)
                P.op("act", "activation", [Sg[g]], [Sgb[g]], out=Sfb[:, g * 512:(g + 1) * 512],
                     in_=Sf[:, g * 512:(g + 1) * 512], func=AF.Copy)
            if do_y:
                P.store("sp", k.YP[t0:t0 + 128, :], yt[:], R=[yt])


def ph_ssd_b(k, l, need_ctx):
    with Phase(k.nc, "ssdb%d" % l) as P:
        C = ssd_consts(P, k, l)
        pos_banks = [P.ps() for _ in range(4)]
        A = ssd_prep_all(P, k, C, pos_banks[:2])
        epsb = P.sb([128, 1])
        P.op("dve", "memset", [], [epsb], epsb[:], float(EPS))
        dbc = P.sb([128, 32])
        nwb = P.sb([128, 2048])
        P.dma("sp", dbc[:], k.ssd_d_bc[l], W=[dbc])
        P.dma("sp", nwb[:], k.ssd_norm_bc[l], W=[nwb])
        xss = Rot([P.sb([128, 2048]) for _ in range(3)])
        bts = Rot([P.sb([128, 512], BF16) for _ in range(2)])
        cfs = Rot([P.sb([128, 4, 128], BF16) for _ in range(2)])
        yps = Rot([P.sb([128, 2048]) for _ in range(3)])
        zs = Rot([P.sb([128, 2048]) for _ in range(3)])
        xsb = Rot([P.sb([128, 2048], BF16) for _ in range(2)])
        sos = Rot([P.sb([128, 2048], BF16) for _ in range(2)])
        jk = P.sb([128, 512])
        tms = Rot([P.sb([128, 512]) for _ in range(3)])
        ssqs = Rot([P.sb([128, 4]) for _ in range(2)])
        rrs = Rot([P.sb([128, 4]) for _ in range(2)])
        Sb = P.sb([128, 2048])
        Sbb = P.sb([128, 2048], BF16)
        Sg = [Buf() for _ in range(4)]
        Sgb = [Buf() for _ in range(4)]
        P.op("pool", "memset", [], Sg, Sb[:], 0.0)
        P.op("pool", "memset", [], Sgb, Sbb[:], 0.0)
        pos = Rot(pos_banks)
        for t0 in CHUNKS_B:
            is_ctx = t0 >= NL
            do_y = need_ctx or not is_ctx
            xs = xss.next()
            bt = bts.next()
            P.dma("sp", xs[:], k.XS[t0:t0 + 128, :], W=[xs])
            P.dma("sp", bt[:], k.BT[t0:t0 + 128, :], W=[bt])
            ci = t0 // 128
            xb = xsb.next()
            P.op("dve", "tensor_tensor", [xs, A["sv"]], [xb], out=xb[:].rearrange("p (h e) -> p h e", e=64),
                 in0=xs[:].rearrange("p (h e) -> p h e", e=64), in1=_bc(A["sv"][:, ci, 32:64]), op=ALU.mult)
            if do_y:
                cf = cfs.next()
                yp = yps.next()
                z = zs.next()
                P.dma("sp", cf[:], k.CF[:, :, t0:t0 + 128].rearrange("c p t -> p c t"), W=[cf])
                P.dma("sp", yp[:], k.YP[t0:t0 + 128, :], W=[yp])
                P.dma("sp", z[:], k.Z[t0:t0 + 128, :], W=[z])
                P.op("act", "activation", [z], [z], out=z[:], in_=z[:], func=AF.Silu)
                ssq = ssqs.next()
                P.op("pool", "memset", [], [ssq], ssq[:], 0.0)
            for g in range(4):
                gs = slice(g * 512, (g + 1) * 512)
                if do_y:
                    po = pos.next()
                    P.mm(po[:, :], cf[:, g, :], Sbb[:, gs], True, True, [cf, Sgb[g]], [po])
                    tm = tms.next()
                    P.op("dve", "tensor_tensor", [po, A["eacum"]], [tm], out=tm[:].rearrange("p (h e) -> p h e", e=64),
                         in0=po[:, :].rearrange("p (h e) -> p h e", e=64),
                         in1=_bc(A["eacum"][:, ci, 32 + g * 8:32 + (g + 1) * 8], 8), op=ALU.mult)
                    P.op("dve", "tensor_tensor", [tm, yp], [yp], out=yp[:, gs], in0=tm[:], in1=yp[:, gs], op=ALU.add)
                    tm3 = tms.next()
                    P.op("pool", "tensor_tensor", [xs, dbc], [tm3], out=tm3[:].rearrange("p (h e) -> p h e", e=64),
                         in0=xs[:, gs].rearrange("p (h e) -> p h e", e=64), in1=_bc(dbc[:, g * 8:(g + 1) * 8], 8),
                         op=ALU.mult)
                    P.op("pool", "tensor_tensor", [tm3, yp], [yp], out=yp[:, gs], in0=tm3[:], in1=yp[:, gs], op=ALU.add)
                    P.op("dve", "tensor_tensor", [yp, z], [yp], out=yp[:, gs], in0=yp[:, gs], in1=z[:, gs], op=ALU.mult)
                    P.op("act", "activation", [yp, ssq], [jk, ssq], out=jk[:], in_=yp[:, gs], func=AF.Square,
                         accum_out=ssq[:, g:g + 1])
                pst = pos.next()
                P.mm(pst[:, :], bt[:, g * 128:(g + 1) * 128], xb[:, gs], True, True, [bt, xb], [pst])
                tm2 = tms.next()
                P.op("pool", "tensor_tensor", [Sg[g], A["ealast"]], [tm2], out=tm2[:].rearrange("p (h e) -> p h e", e=64),
                     in0=Sb[:, gs].rearrange("p (h e) -> p h e", e=64),
                     in1=_bc(A["ealast"][:, ci, 32 + g * 8:32 + (g + 1) * 8], 8), op=ALU.mult)
                P.op("dve", "tensor_tensor", [tm2, pst], [Sg[g]], out=Sb[:, gs], in0=tm2[:], in1=pst[:, :], op=ALU.add)
                P.op("act", "activation", [Sg[g]], [Sgb[g]], out=Sbb[:, gs], in_=Sb[:, gs], func=AF.Copy)
            if do_y:
                rr = rrs.next()
                P.op("act", "activation", [ssq, epsb], [rr], out=rr[:], in_=ssq[:], func=AF.Sqrt, bias=epsb[:, 0:1],
                     scale=1.0 / 512.0)
                P.op("dve", "reciprocal", [rr], [rr], out=rr[:], in_=rr[:])
                so = sos.next()
                for g in range(4):
                    gs = slice(g * 512, (g + 1) * 512)
                    P.op("dve", "scalar_tensor_tensor", [yp, rr, nwb], [so], out=so[:, gs], in0=yp[:, gs],
                         scalar=rr[:, g:g + 1], in1=nwb[:, gs], op0=ALU.mult, op1=ALU.mult)
                P.store("sp", k.SO[t0:t0 + 128, :], so[:], R=[so])


def ph_merge(k, l, skip_ctx):
    with Phase(k.nc, "mrg%d" % l) as P:
        identb = P.sb([128, 128], BF16)
        P.dma("pool", identb[:], k.ident, W=[identb])
        ms = P.sb([128, 9, NCH, 2])
        P.dma("sp", ms[:].rearrange("p a c v -> p (a c v)"), k.MODS, W=[ms])
        wna = P.sb([128, 8, D], BF16)
        wso = P.sb([128, 16, D], BF16)
        wou = P.sb([128, 8, D], BF16)
        wnb, wsb, wob = Buf(), Buf(), Buf()
        for c in range(8):
            P.dma("pool", wna[:, c, :], k.na_w_o[l].rearrange("(c p) f -> p c f", p=128)[:, c, :], W=[wnb])
        for c in range(16):
            P.dma("pool", wso[:, c, :], k.ssd_w_o[l].rearrange("(c p) f -> p c f", p=128)[:, c, :], W=[wsb])
        for c in range(8):
            P.dma("pool", wou[:, c, :], k.w_out[l].rearrange("(c p) f -> p c f", p=128)[:, c, :], W=[wob])
        xts = Rot([P.sb([128, NCH, 512]) for _ in range(2)])
        gts = Rot([P.sb([128, 16, 512]) for _ in range(1)])
        ain = Rot([P.sb([128, 4, D], BF16) for _ in range(1)])
        sin_ = Rot([P.sb([128, 4, 2048], BF16) for _ in range(1)])
        aof = Rot([P.sb([128, 8, 512], BF16) for _ in range(1)])
        sof = Rot([P.sb([128, 16, 512], BF16) for _ in range(1)])
        mts = Rot([P.sb([128, 8, 512], BF16) for _ in range(1)])
        t1s = Rot([P.sb([128, 512]) for _ in range(2)])
        t2s = Rot([P.sb([128, 512]) for _ in range(2)])
        ptr = Rot([P.ps([128, 1024], BF16) for _ in range(2)])
        pas = Rot([P.ps() for _ in range(2)])
        pss = Rot([P.ps() for _ in range(2)])
        pys = Rot([P.ps() for _ in range(2)])
        cp = Rot(["dve", "act"])
        for (t0, n, v) in (TILES[:8] if skip_ctx else TILES):
            nb = n // 128
            xt = xts.next()
            gt = gts.next()
            ai = ain.next()
            si = sin_.next()
            P.dma("sp", xt[:, :, 0:n], k.XR[:, :, t0:t0 + n].rearrange("c p t -> p c t"), W=[xt])
            P.dma("sp", gt[:, :, 0:n], k.G[:, :, t0:t0 + n].rearrange("c p t -> p c t"), W=[gt])
            P.dma("sp", ai[:, 0:nb, :], k.AO[t0:t0 + n, :].rearrange("(b p) f -> p b f", p=128), W=[ai])
            P.dma("sp", si[:, 0:nb, :], k.SO[t0:t0 + n, :].rearrange("(b p) f -> p b f", p=128), W=[si])
            af = aof.next()
            sf = sof.next()
            for b in range(nb):
                for (src, dstt, nchunk) in ((ai, af, 8), (si, sf, 16)):
                    for c8 in range(nchunk // 8):
                        pt = ptr.next()
                        for j in range(8):
                            c = c8 * 8 + j
                            P.S.add("pe", (lambda e, o=pt[:, j * 128:(j + 1) * 128], i=src[:, b, c * 128:(c + 1) * 128]:
                                           e.transpose(o, i, identb[:])), _bufs([src, identb]), _bufs([pt]))
                        dsta = dstt[:, c8 * 8:(c8 + 1) * 8, b * 128:(b + 1) * 128]
                        srca = pt[:, :].rearrange("p (c t) -> p c t", c=8)
                        if cp.next() == "dve":
                            P.op("dve", "tensor_copy", [pt], [dstt], out=dsta, in_=srca)
                        else:
                            P.op("act", "activation", [pt], [dstt], out=dsta, in_=srca, func=AF.Copy)
            mt = mts.next()
            for dc in range(NCH):
                pa = pas.next()
                for c in range(8):
                    P.mm(pa[:, 0:n], wna[:, c, dc * 128:(dc + 1) * 128], af[:, c, 0:n], c == 0, c == 7, [wnb, af], [pa])
                ps = pss.next()
                for c in range(16):
                    P.mm(ps[:, 0:n], wso[:, c, dc * 128:(dc + 1) * 128], sf[:, c, 0:n], c == 0, c == 15, [wsb, sf], [ps])
                t1 = t1s.next()
                t2 = t2s.next()
                P.op("dve", "tensor_tensor", [pa, gt], [t1], out=t1[:, 0:n], in0=pa[:, 0:n], in1=gt[:, dc, 0:n], op=ALU.mult)
                P.op("dve", "tensor_tensor", [ps, gt], [t2], out=t2[:, 0:n], in0=ps[:, 0:n], in1=gt[:, 8 + dc, 0:n],
                     op=ALU.mult)
                P.op("pool", "tensor_tensor", [t1, t2], [mt], out=mt[:, dc, 0:n], in0=t1[:, 0:n], in1=t2[:, 0:n], op=ALU.add)
            for dc in range(NCH):
                py = pys.next()
                for c in range(8):
                    P.mm(py[:, 0:n], wou[:, c, dc * 128:(dc + 1) * 128], mt[:, c, 0:n], c == 0, c == 7, [wob, mt], [py])
                P.op("dve", "scalar_tensor_tensor", [py, xt, ms], [xt], out=xt[:, dc, 0:n], in0=py[:, 0:n],
                     scalar=ms[:, 5, dc, v:v + 1], in1=xt[:, dc, 0:n], op0=ALU.mult, op1=ALU.add)
            P.store("sp", k.XR[:, :, t0:t0 + n].rearrange("c p t -> p c t"), xt[:, :, 0:n], R=[xt])


def build_program(cfg):
    nc = bass.Bass("TRN2", target_bir_lowering=False)
    k = K()
    k.nc = nc
    k.cfg = cfg
    dbg = cfg.get("debug", ())

    def inp(name, shape):
        return dram(nc, name, shape, F32, kind="ExternalInput")

    k.dbg_copies = []

    def scr(name, shape, dt):
        if name in dbg and dt == BF16:
            a = dram(nc, name + "_bf", shape, dt)
            k.dbg_copies.append((a, dram(nc, name, shape, F32, kind="ExternalOutput")))
            return a
        return dram(nc, name, shape, dt, kind="ExternalOutput" if name in dbg else "Internal")

    k.x = inp("x", [NL, D])
    k.ctx = inp("ctx", [CT, D])
    k.cvec = inp("cvec", [128, NCH, 2])
    k.ident = inp("ident", [128, 128])
    k.ones = inp("ones", [128, 128])
    k.w_ada = inp("w_ada", [DEPTH, D, 9 * D])
    k.b_ada = inp("b_ada", [DEPTH, 128, 72])
    k.norm_ffn1 = inp("norm_ffn1", [DEPTH, 128, NCH])
    k.norm_mix = inp("norm_mix", [DEPTH, 128, NCH])
    k.norm_ffn2 = inp("norm_ffn2", [DEPTH, 128, NCH])
    k.ffn1_w_gate = inp("ffn1_w_gate", [DEPTH, D, DFF])
    k.ffn1_w_up = inp("ffn1_w_up", [DEPTH, D, DFF])
    k.ffn1_w_down = inp("ffn1_w_down", [DEPTH, DFF, D])
    k.ffn2_w_gate = inp("ffn2_w_gate", [DEPTH, D, DFF])
    k.ffn2_w_up = inp("ffn2_w_up", [DEPTH, D, DFF])
    k.ffn2_w_down = inp("ffn2_w_down", [DEPTH, DFF, D])
    k.w_in = inp("w_in", [DEPTH, D, NIN])
    k.q_norm = inp("q_norm", [DEPTH, 128, 1])
    k.k_norm = inp("k_norm", [DEPTH, 128, 1])
    k.blk = inp("blk", [128, 128])
    k.rot = inp("rot", [128, 128])
    k.cosT = inp("cosT", [128, NL])
    k.sinS = inp("sinS", [128, NL])
    k.conv_w = inp("conv_w", [DEPTH, 128, 24, 5])
    k.conv_b = inp("conv_b", [DEPTH, 128, 24])
    k.RB = inp("RB", [DEPTH, 16, 128, 1024])
    k.maskF = inp("maskF", [128, 1024])
    k.maskI = inp("maskI", [128, 1024])
    for nm in ("triF", "triB", "nm2F", "nm2B", "negI", "ntriF", "ntriB"):
        setattr(k, nm, inp(nm, [128, 128]))
    k.dt_bias_bc = inp("dt_bias_bc", [DEPTH, 128, 64])
    k.a_log_bc = inp("a_log_bc", [DEPTH, 128, 64])
    k.ssd_d_bc = inp("ssd_d_bc", [DEPTH, 128, 32])
    k.ssd_norm_bc = inp("ssd_norm_bc", [DEPTH, 128, 2048])
    k.na_w_o = inp("na_w_o", [DEPTH, D, D])
    k.ssd_w_o = inp("ssd_w_o", [DEPTH, 2048, D])
    k.w_out = inp("w_out", [DEPTH, D, D])
    k.out = dram(nc, "out", [NL, D], F32, kind="ExternalOutput")
    k.XR = scr("XR", [NCH, 128, NT], F32)
    k.MODS = scr("MODS", [128, 9 * NCH * 2], F32)
    k.HS = scr("HS", [NCH, 128, NT], BF16)
    k.Q = scr("Q", [8, 128, NT], BF16)
    k.KK = scr("KK", [8, 128, NT], BF16)
    k.V = scr("V", [NT, 16 * 65], BF16)
    k.Z = scr("Z", [NT, 2048], F32)
    k.DT = scr("DT", [NT, 64], F32)
    k.XBC = scr("XBC", [24, 128, NT], F32)
    k.G = scr("G", [16, 128, NT], F32)
    k.XS = scr("XS", [NT, 2048], F32)
    k.BF = scr("BF", [4, 128, NT], BF16)
    k.CF = scr("CF", [4, 128, NT], BF16)
    k.BT = scr("BT", [NT, 512], BF16)
    k.AO = scr("AO", [NT, D], BF16)
    k.YP = scr("YP", [NT, 2048], F32)
    k.SO = scr("SO", [NT, 2048], BF16)

    nl = cfg.get("n_layers", DEPTH)
    stop = cfg.get("stop", None)
    skip = cfg.get("skip", ())
    ph_load_x(k)
    done = False
    for l in range(nl):
        last = (l == DEPTH - 1)
        need_ctx = not last
        plist = [("ada", lambda: ph_adaln(k, l)), ("ffn1", lambda: ph_ffn(k, l, 1)),
                 ("hmix", lambda: ph_hmix(k, l)), ("qkv", lambda: ph_qkv(k, l)), ("zdt", lambda: ph_zdt(k, l)),
                 ("xbc", lambda: ph_proj_fm(k, l, "xbc", OX, 24, k.XBC, False)),
                 ("g", lambda: ph_proj_fm(k, l, "g", OG, 16, k.G, True)),
                 ("conv", lambda: ph_conv(k, l)), ("attn", lambda: ph_attn(k, l, need_ctx)),
                 ("ssda", lambda: ph_ssd_a(k, l, need_ctx)), ("ssdb", lambda: ph_ssd_b(k, l, need_ctx)),
                 ("mrg", lambda: ph_merge(k, l, last)),
                 ("ffn2", lambda: ph_ffn(k, l, 2, skip_ctx=last))]
        for name, fn in plist:
            if name in skip:
                continue
            fn()
            if stop == (l, name):
                done = True
                break
        if done:
            break
    if k.dbg_copies:
        with Phase(nc, "dbg") as P:
            for (src, dst) in k.dbg_copies:
                n0 = src.shape[0]
                step = max(1, n0 // 8)
                for i in range(0, n0, step):
                    P.store("pool", dst[i:i + step], src[i:i + step])
    ph_store_out(k)
    return nc


def _fm(vec, nchunk):
    s = vec.shape[:-1]
    return np.ascontiguousarray(np.swapaxes(vec.reshape(s + (nchunk, 128)), -1, -2))


def prepare_inputs(inputs, b):
    f = np.float32
    m = {}
    m["x"] = np.ascontiguousarray(inputs["x"][b], dtype=f)
    m["ctx"] = np.ascontiguousarray(inputs["ctx"][b], dtype=f)
    cv = np.stack([inputs["c"][b], inputs["c_ctx"]], axis=-1).astype(f)
    m["cvec"] = np.ascontiguousarray(cv.reshape(NCH, 128, 2).transpose(1, 0, 2))
    m["w_ada"] = np.ascontiguousarray(inputs["w_ada"], dtype=f)
    m["b_ada"] = _fm(np.asarray(inputs["b_ada"], dtype=f), 72)
    for nm in ("norm_ffn1", "norm_mix", "norm_ffn2"):
        m[nm] = _fm(np.asarray(inputs[nm], dtype=f), NCH)
    for nm in ("ffn1_w_gate", "ffn1_w_up", "ffn1_w_down", "ffn2_w_gate", "ffn2_w_up", "ffn2_w_down",
               "w_in", "na_w_o", "ssd_w_o", "w_out"):
        m[nm] = np.ascontiguousarray(inputs[nm], dtype=f)
    m["q_norm"] = np.ascontiguousarray(np.tile(np.asarray(inputs["q_norm"], dtype=f), (1, 2))[:, :, None])
    m["k_norm"] = np.ascontiguousarray(np.tile(np.asarray(inputs["k_norm"], dtype=f), (1, 2))[:, :, None])
    m.update(_consts())
    m["conv_w"] = np.ascontiguousarray(
        np.asarray(inputs["ssd_conv_w"], dtype=f).reshape(DEPTH, 5, 24, 128).transpose(0, 3, 2, 1))
    m["conv_b"] = _fm(np.asarray(inputs["ssd_conv_b"], dtype=f), 24)
    m["RB"] = _gather_rpb(np.asarray(inputs["na_rpb"], dtype=f))
    m["dt_bias_bc"] = np.ascontiguousarray(
        np.broadcast_to(np.asarray(inputs["ssd_dt_bias"], dtype=f).reshape(DEPTH, 1, 64), (DEPTH, 128, 64)))
    m["a_log_bc"] = np.ascontiguousarray(
        np.broadcast_to(np.asarray(inputs["ssd_a_log"], dtype=f).reshape(DEPTH, 1, 64), (DEPTH, 128, 64)))
    m["ssd_d_bc"] = np.ascontiguousarray(
        np.broadcast_to(np.asarray(inputs["ssd_d"], dtype=f).reshape(DEPTH, 1, 32), (DEPTH, 128, 32)))
    m["ssd_norm_bc"] = np.ascontiguousarray(
        np.broadcast_to(np.asarray(inputs["ssd_norm"], dtype=f).reshape(DEPTH, 1, 2048), (DEPTH, 128, 2048)))
    return m


_CONSTS = {}


def _rpb_index():
    a = np.arange(2)[:, None, None, None]
    kc = np.arange(64)[None, :, None, None]
    s_ = np.arange(16)[None, None, :, None]
    qc = np.arange(64)[None, None, None, :]
    dr = 7 - s_ + a + 0 * kc + 0 * qc
    dc = kc - qc + 0 * a + 0 * s_
    scol = np.clip(qc - 8, 0, 48)
    colvalid = (kc >= scol) & (kc < scol + 16) & (np.abs(dr) <= 7)
    return dr, dc, colvalid


def _gather_rpb(rpb):
    dr, dc, _ = _rpb_index()
    di = np.clip(dr + 7, 0, 14)
    ci = np.clip(dc + 15, 0, 30)
    out = rpb[:, :, di, ci]
    return np.ascontiguousarray(out.reshape(rpb.shape[0], rpb.shape[1], 128, 1024))


def _consts():
    if _CONSTS:
        return _CONSTS
    f = np.float32
    c = _CONSTS
    idx = np.arange(128)
    c["ident"] = np.eye(128, dtype=f)
    c["ones"] = np.ones((128, 128), dtype=f)
    c["blk"] = (idx[:, None] // 64 == idx[None, :] // 64).astype(f)
    partner = np.where((idx % 64) < 32, idx + 32, idx - 32)
    rot = np.zeros((128, 128), dtype=f)
    rot[partner, idx] = 1.0
    c["rot"] = rot
    t = np.arange(NL)
    row = (t // 64).astype(np.float64)
    col = (t % 64).astype(np.float64)
    inv = 10000.0 ** (-np.arange(16, dtype=np.float64) / 16)
    ang = np.concatenate([row[:, None] * inv, col[:, None] * inv], axis=-1)
    ang = ang.astype(f).astype(np.float64)
    j = idx % 64
    cosT = np.cos(ang[:, j % 32]).T
    sinT = np.sin(ang[:, j % 32]).T
    sign = np.where(j < 32, -1.0, 1.0)[:, None]
    c["cosT"] = np.ascontiguousarray(cosT.astype(f))
    c["sinS"] = np.ascontiguousarray((sinT * sign).astype(f))
    dr, dc, colvalid = _rpb_index()
    c["maskF"] = np.ascontiguousarray(colvalid.astype(f).reshape(128, 1024))
    c["maskI"] = np.ascontiguousarray((colvalid & (dr >= -4) & (dr <= 3)).astype(f).reshape(128, 1024))
    kk = idx[:, None]
    ii = idx[None, :]
    c["triF"] = (kk <= ii).astype(f)
    c["triB"] = (kk >= ii).astype(f)
    c["nm2F"] = (ii < kk).astype(f)
    c["nm2B"] = (ii > kk).astype(f)
    c["negI"] = (-BIG * np.eye(128)).astype(f)
    c["ntriF"] = -c["triF"]
    c["ntriB"] = -c["triB"]
    return c


_CACHE = {}


def run(inputs, cfg, n_cores=8):
    key = repr(sorted(cfg.items()))
    if key not in _CACHE:
        _CACHE[key] = build_program(cfg)
    nc = _CACHE[key]
    shared = None
    in_maps = []
    base = None
    for core in range(n_cores):
        m = prepare_inputs(inputs, core % 4)
        if base is None:
            base = m
        else:
            for kk in m:
                if kk not in ("x", "ctx", "cvec"):
                    m[kk] = base[kk]
        in_maps.append(m)
    res = run_bass_kernel_spmd(nc, in_maps, core_ids=list(range(n_cores)))
    return res


def kernel(**inputs):
    inputs = {k_: np.asarray(v) for k_, v in inputs.items()}
    res = run(inputs, {})
    out = np.stack([res.results[b]["out"] for b in range(4)], axis=0)
    return out.astype(np.float32)
```

```python
import contextlib
import numpy as np
import concourse.bass as bass
import concourse.mybir as mybir
from concourse.bass_utils import run_bass_kernel_spmd

F32 = mybir.dt.float32
BF16 = mybir.dt.bfloat16
ALU = mybir.AluOpType
AF = mybir.ActivationFunctionType

SEM_ROT = 30000
N_DMA_SEMS = 20


class Buf:
    __slots__ = ("name", "last_w", "readers")

    def __init__(self, name=""):
        self.name = name
        self.last_w = None
        self.readers = []


class Op:
    __slots__ = ("eng", "fn", "deps", "ndma", "ticket", "sem", "semval", "signal", "idx")


class Sched:
    ENGS = ("pe", "act", "dve", "pool", "sp")
    uid = 0

    def __init__(self, nc):
        Sched.uid += 1
        self.nc = nc
        self.ops = []
        self.dma_rr = {e: 0 for e in self.ENGS}
        self.dma_last = {}

    def add(self, eng, fn, reads=(), writes=(), ndma=0):
        op = Op()
        op.eng = eng
        op.fn = fn
        op.ndma = ndma
        op.signal = False
        op.ticket = None
        op.sem = None
        op.semval = None
        op.idx = len(self.ops)
        deps = {}
        for b in reads:
            w = b.last_w
            if w is not None:
                deps[w.idx] = w
        for b in writes:
            w = b.last_w
            if w is not None and (w.eng != eng or ndma or w.ndma):
                deps[w.idx] = w
            for r in b.readers:
                if r.eng != eng or ndma or r.ndma:
                    deps[r.idx] = r
        if eng == "pe":
            deps = {k: v for k, v in deps.items() if v.eng != "pe" or v.ndma}
        if ndma:
            slot = self.dma_rr[eng]
            self.dma_rr[eng] = (slot + 1) % N_DMA_SEMS
            prev = self.dma_last.get((eng, slot))
            if prev is not None:
                deps[prev.idx] = prev
            self.dma_last[(eng, slot)] = op
            op.sem = (eng, slot)
        for b in reads:
            b.readers.append(op)
        for b in writes:
            b.last_w = op
            b.readers = []
        op.deps = list(deps.values())
        for d in op.deps:
            d.signal = True
        self.ops.append(op)
        return op

    def emit(self):
        nc = self.nc
        ops = self.ops
        counts = {e: 0 for e in self.ENGS}
        dma_tot = {}
        for op in ops:
            if op.ndma:
                tot = dma_tot.get(op.sem, 0) + 16 * op.ndma
                dma_tot[op.sem] = tot
                op.semval = tot
            elif op.signal:
                counts[op.eng] += 1
                op.ticket = counts[op.eng]
        nsem_eng = {e: max(1, -(-counts[e] // SEM_ROT)) for e in self.ENGS}
        all_sems = []
        with contextlib.ExitStack() as st:
            tick_sems = {}
            for e, n in nsem_eng.items():
                tick_sems[e] = [nc.alloc_semaphore(name="tk%d_%s_%d" % (Sched.uid, e, i)) for i in range(n)]
                all_sems += tick_sems[e]
            dma_sems = {}
            for key in dma_tot:
                dma_sems[key] = nc.alloc_semaphore(name="dm%d_%s_%d" % (Sched.uid, key[0], key[1]))
                all_sems.append(dma_sems[key])
            self.all_sems = all_sems
            block = st.enter_context(nc.Block())

            def tick(op):
                t = op.ticket - 1
                return tick_sems[op.eng][t // SEM_ROT], (t % SEM_ROT) + 1, t // SEM_ROT

            per_eng = {e: [] for e in self.ENGS}
            for op in ops:
                per_eng[op.eng].append(op)

            def run_engine(ename, eng):
                waited = {}
                for op in per_eng[ename]:
                    for d in op.deps:
                        if d.ndma:
                            key = ("d",) + d.sem
                            sem, val = dma_sems[d.sem], d.semval
                        else:
                            sem, val, r = tick(d)
                            key = ("t", d.eng, r)
                        if waited.get(key, 0) >= val:
                            continue
                        waited[key] = val
                        eng.wait_ge(sem, val)
                    ins = op.fn(eng)
                    if op.ndma:
                        ins.then_inc(dma_sems[op.sem], 16)
                    elif op.signal:
                        sem, _, _ = tick(op)
                        ins.then_inc(sem, 1)

            @block.tensor
            def _(e):
                run_engine("pe", e)

            @block.scalar
            def _(e):
                run_engine("act", e)

            @block.vector
            def _(e):
                run_engine("dve", e)

            @block.gpsimd
            def _(e):
                run_engine("pool", e)

            @block.sync
            def _(e):
                run_engine("sp", e)


class T:
    def __init__(self, t, name=""):
        self.t = t
        self.b = Buf(name)

    def __getitem__(self, k):
        return self.t[k]


class TS(T):
    def __init__(self, t, c0, w):
        self.t = t
        self.b = Buf()
        self.c0 = c0
        self.w = w

    def __getitem__(self, k):
        ps, cs = k
        a = self.c0 + (cs.start or 0)
        b = self.c0 + (self.w if cs.stop is None else cs.stop)
        return self.t[ps, a:b]


def _bufs(lst):
    return [x.b if isinstance(x, T) else x for x in lst]


class Rot:
    def __init__(self, items):
        self.items = items
        self.i = 0

    def next(self):
        x = self.items[self.i % len(self.items)]
        self.i += 1
        return x


class Phase:
    uid = 0

    def __init__(self, nc, name):
        self.nc = nc
        self.name = name
        self.S = Sched(nc)
        self.st = contextlib.ExitStack()
        self.stores = []

    def __enter__(self):
        return self

    def __exit__(self, et, ev, tb):
        if et is not None:
            self.st.close()
            return False
        if self.stores:
            self.S.add("sp", lambda e: e.nop(), reads=self.stores)
        with self.nc.named_scope(self.name):
            self.S.emit()
        self.st.close()
        self.nc.all_engine_barrier()
        self.nc.clear_and_free_semaphores(self.S.all_sems)
        self.nc.all_engine_barrier()
        return False

    def _nm(self, p):
        Phase.uid += 1
        return "%s_%s%d" % (self.name, p, Phase.uid)

    def sb(self, shape, dt=F32):
        nm = self._nm("s")
        return T(self.st.enter_context(self.nc.sbuf_tensor(nm, list(shape), dt)), nm)

    def ps(self, shape=(128, 512), dt=F32):
        nm = self._nm("p")
        return T(self.st.enter_context(self.nc.psum_tensor(nm, list(shape), dt)), nm)

    def ps_slots(self, nbanks, w):
        out = []
        for _ in range(nbanks):
            t = self.ps().t
            out += [TS(t, c0, w) for c0 in range(0, 512, w)]
        return out

    def op(self, eng, meth, R, W, *a, **kw):
        return self.S.add(eng, lambda e: getattr(e, meth)(*a, **kw), _bufs(R), _bufs(W))

    def mm(self, out, lhsT, rhs, start, stop, R, W):
        return self.S.add("pe", lambda e: e.matmul(out, lhsT=lhsT, rhs=rhs, start=start, stop=stop),
                          _bufs(R), _bufs(W))

    def dma(self, eng, out, in_, R=(), W=()):
        return self.S.add(eng, lambda e: e.dma_start(out=out, in_=in_), _bufs(R), _bufs(W), ndma=1)

    def store(self, eng, out, in_, R=()):
        b = Buf("st")
        self.stores.append(b)
        return self.S.add(eng, lambda e: e.dma_start(out=out, in_=in_), _bufs(R), [b], ndma=1)


DEPTH = 4
D = 1024
NCH = 8
NL = 4096
CT = 256
NT = NL + CT
DFF = 2816
NFF = 22
NIN = 10304
OQ, OK_, OV, OZ, OX, ODT, OG = 0, 1024, 2048, 3072, 5120, 8192, 8256
EPS = 1e-6
BIG = 32768.0
TILES = [(i * 512, 512, 0) for i in range(8)] + [(NL, CT, 1)]
CHUNKS_F = [NL, NL + 128] + [i * 128 for i in range(32)]
CHUNKS_B = [NL + 128, NL] + [i * 128 for i in range(31, -1, -1)]


class K:
    pass


def dram(nc, name, shape, dt, kind="Internal"):
    return nc.dram_tensor(name, list(shape), dt, kind=kind).ap()


def ph_load_x(k):
    with Phase(k.nc, "ldx") as P:
        ident = P.sb([128, 128]);
        P.dma("sp", ident[:], k.ident, W=[ident])
        xin = Rot([P.sb([128, 4, D]) for _ in range(2)])
        stg = Rot([P.sb([128, NCH, 512]) for _ in range(2)])
        pss = Rot([P.ps() for _ in range(4)])
        cp = Rot(["dve", "act"])
        for (t0, n, v) in TILES:
            nb = n // 128
            xi = xin.next()
            src = k.x[t0:t0 + n, :] if v == 0 else k.ctx[:, :]
            P.dma("sp", xi[:, 0:nb, :], src.rearrange("(b p) d -> p b d", p=128), W=[xi])
            sg = stg.next()
            for b in range(nb):
                for half in range(2):
                    ps = pss.next()
                    for c4 in range(4):
                        c = half * 4 + c4
                        P.mm(ps[:, c4 * 128:(c4 + 1) * 128], xi[:, b, c * 128:(c + 1) * 128], ident[:],
                             True, True, [xi, ident], [ps])
                    e = cp.next()
                    dst = sg[:, half * 4:(half + 1) * 4, b * 128:(b + 1) * 128]
                    srcp = ps[:, :].rearrange("p (c t) -> p c t", c=4)
                    if e == "dve":
                        P.op("dve", "tensor_copy", [ps], [sg], out=dst, in_=srcp)
                    else:
                        P.op("act", "activation", [ps], [sg], out=dst, in_=srcp, func=AF.Copy)
            P.store("sp", k.XR[:, :, t0:t0 + n].rearrange("c p t -> p c t"), sg[:, :, 0:n], R=[sg])


def ph_store_out(k):
    with Phase(k.nc, "sto") as P:
        ident = P.sb([128, 128])
        P.dma("sp", ident[:], k.ident, W=[ident])
        xin = Rot([P.sb([128, NCH, 512]) for _ in range(2)])
        stg = Rot([P.sb([128, 4, D]) for _ in range(2)])
        pss = Rot([P.ps() for _ in range(4)])
        cp = Rot(["dve", "act"])
        for (t0, n, v) in TILES[:8]:
            xi = xin.next()
            P.dma("sp", xi[:], k.XR[:, :, t0:t0 + n].rearrange("c p t -> p c t"), W=[xi])
            sg = stg.next()
            for b in range(4):
                for half in range(2):
                    ps = pss.next()
                    for c4 in range(4):
                        c = half * 4 + c4
                        P.mm(ps[:, c4 * 128:(c4 + 1) * 128], xi[:, c, b * 128:(b + 1) * 128], ident[:],
                             True, True, [xi, ident], [ps])
                    e = cp.next()
                    dst = sg[:, b, half * 512:(half + 1) * 512]
                    if e == "dve":
                        P.op("dve", "tensor_copy", [ps], [sg], out=dst, in_=ps[:, :])
                    else:
                        P.op("act", "activation", [ps], [sg], out=dst, in_=ps[:, :], func=AF.Copy)
            P.store("sp", k.out[t0:t0 + n, :].rearrange("(b p) d -> p b d", p=128), sg[:], R=[sg])


def ph_adaln(k, l):
    with Phase(k.nc, "ada%d" % l) as P:
        cv = P.sb([128, NCH, 2])
        sv = P.sb([128, NCH, 2])
        P.dma("sp", cv[:], k.cvec, W=[cv])
        P.op("act", "activation", [cv], [sv], out=sv[:], in_=cv[:], func=AF.Silu)
        bada = P.sb([128, 72])
        P.dma("sp", bada[:], k.b_ada[l], W=[bada])
        gn = P.sb([128, 3, NCH])
        P.dma("sp", gn[:, 0, :], k.norm_ffn1[l], W=[gn])
        P.dma("sp", gn[:, 1, :], k.norm_mix[l], W=[gn])
        P.dma("sp", gn[:, 2, :], k.norm_ffn2[l], W=[gn])
        wb = Rot([P.sb([128, NCH, 512]) for _ in range(3)])
        pm = P.ps([128, 144])
        wsrc = k.w_ada[l].rearrange("(c p) n -> p c n", p=128)
        for blk in range(18):
            w = wb.next()
            P.dma("sp", w[:], wsrc[:, :, blk * 512:(blk + 1) * 512], W=[w])
            for s in range(4):
                ch = blk * 4 + s
                for kc in range(NCH):
                    P.mm(pm[:, ch * 2:ch * 2 + 2], w[:, kc, s * 128:(s + 1) * 128], sv[:, kc, :],
                         kc == 0, kc == NCH - 1, [w, sv], [pm])
        mod = P.sb([128, 72, 2])
        P.op("dve", "tensor_tensor", [pm, bada], [mod], out=mod[:],
             in0=pm[:, :].rearrange("p (c v) -> p c v", v=2),
             in1=bada[:, :].unsqueeze(2).to_broadcast([128, 72, 2]), op=ALU.add)
        ms = P.sb([128, 9, NCH, 2])
        for i, (gi, scale_idx, shift_idx, gate_idx, gmul) in enumerate(
                [(0, 1, 0, 2, 0.5), (1, 4, 3, 5, 1.0), (2, 7, 6, 8, 0.5)]):
            tmp = P.sb([128, NCH, 2])
            P.op("dve", "tensor_scalar", [mod], [tmp], out=tmp[:], in0=mod[:, scale_idx * 8:scale_idx * 8 + 8, :],
                 scalar1=1.0, scalar2=32.0, op0=ALU.add, op1=ALU.mult)
            P.op("dve", "tensor_tensor", [tmp, gn], [ms], out=ms[:, 3 * i, :, :], in0=tmp[:],
                 in1=gn[:, gi, :].unsqueeze(2).to_broadcast([128, NCH, 2]), op=ALU.mult)
            P.op("dve", "tensor_copy", [mod], [ms], out=ms[:, 3 * i + 1, :, :],
                 in_=mod[:, shift_idx * 8:shift_idx * 8 + 8, :])
            P.op("dve", "tensor_scalar", [mod], [ms], out=ms[:, 3 * i + 2, :, :],
                 in0=mod[:, gate_idx * 8:gate_idx * 8 + 8, :], scalar1=gmul, scalar2=0.0, op0=ALU.mult, op1=ALU.add)
        P.store("sp", k.MODS, ms[:].rearrange("p a c v -> p (a c v)"), R=[ms])


def rsqrt_psum(P, C, ss, r, n, eps_eff, np_=128):
    sqv = C["sqv"].next()
    P.op("act", "activation", [ss, C["epsb"]], [sqv], out=sqv[0:np_, 0:n], in_=ss[0:np_, 0:n], func=AF.Sqrt,
         bias=C["epsb"][0:np_, 0:1], scale=1.0)
    P.op("dve", "reciprocal", [sqv], [r], out=r[0:np_, 0:n], in_=sqv[0:np_, 0:n])


def norm_mod(P, k, C, xt, n, v, slotA, ht):
    ss = C["ps_ss"].next()
    for c in range(NCH):
        sq = C["sq"].next()
        P.op("act", "activation", [xt], [sq], out=sq[:, 0:n], in_=xt[:, c, 0:n], func=AF.Square)
        P.mm(ss[:, 0:n], C["ones"][:], sq[:, 0:n], c == 0, c == NCH - 1, [C["ones"], sq], [ss])
    r = C["r"].next()
    rsqrt_psum(P, C, ss, r, n, float(D * EPS))
    ms = C["ms"]
    for c in range(NCH):
        tm = C["tm"].next()
        P.op("dve", "tensor_tensor", [xt, r], [tm], out=tm[:, 0:n], in0=xt[:, c, 0:n], in1=r[:, 0:n], op=ALU.mult)
        P.op("act", "activation", [tm, ms], [ht], out=ht[:, c, 0:n], in_=tm[:, 0:n], func=AF.Identity,
             scale=ms[:, slotA, c, v:v + 1], bias=ms[:, slotA + 1, c, v:v + 1])


def norm_ctx(P, k):
    C = {}
    C["ones"] = P.sb([128, 128], BF16)
    P.dma("pool", C["ones"][:], k.ones, W=[C["ones"]])
    C["ms"] = P.sb([128, 9, NCH, 2])
    P.dma("sp", C["ms"][:].rearrange("p a c v -> p (a c v)"), k.MODS, W=[C["ms"]])
    C["sq"] = Rot([P.sb([128, 512], BF16) for _ in range(2)])
    C["r"] = Rot([P.sb([128, 512]) for _ in range(1)])
    C["tm"] = Rot([P.sb([128, 512]) for _ in range(2)])
    C["ps_ss"] = Rot([P.ps() for _ in range(1)])
    C["sqv"] = C["tm"]
    C["epsb"] = P.sb([128, 1])
    P.op("dve", "memset", [], [C["epsb"]], C["epsb"][:], float(D * EPS))
    return C


def ph_ffn(k, l, sub, skip_ctx=False):
    wg_d, wu_d, wd_d = (k.ffn1_w_gate, k.ffn1_w_up, k.ffn1_w_down) if sub == 1 else \
        (k.ffn2_w_gate, k.ffn2_w_up, k.ffn2_w_down)
    slotA = 0 if sub == 1 else 6
    with Phase(k.nc, "ffn%d_%d" % (sub, l)) as P:
        C = norm_ctx(P, k)
        wg = P.sb([128, NCH, DFF], BF16)
        wu = P.sb([128, NCH, DFF], BF16)
        wd = P.sb([128, NFF, D], BF16)
        wgb = [Buf(), Buf()]
        wub = [Buf(), Buf()]
        wdb = [Buf(), Buf()]
        gs = wg_d[l].rearrange("(c p) f -> p c f", p=128)
        us = wu_d[l].rearrange("(c p) f -> p c f", p=128)
        ds = wd_d[l].rearrange("(f p) d -> p f d", p=128)
        HF = DFF // 2
        for hf in range(2):
            for c in range(NCH):
                P.dma("pool", wg[:, c, hf * HF:(hf + 1) * HF], gs[:, c, hf * HF:(hf + 1) * HF], W=[wgb[hf]])
            for c in range(NCH):
                P.dma("pool", wu[:, c, hf * HF:(hf + 1) * HF], us[:, c, hf * HF:(hf + 1) * HF], W=[wub[hf]])
        for hf in range(2):
            for f in range(11):
                ff = hf * 11 + f
                P.dma("pool", wd[:, ff, :], ds[:, ff, :], W=[wdb[hf]])
        xts = Rot([P.sb([128, NCH, 512]) for _ in range(2)])
        hts = Rot([P.sb([128, NCH, 512], BF16) for _ in range(1)])
        ats = Rot([P.sb([128, NFF, 512], BF16) for _ in range(1)])
        sgs = Rot([P.sb([128, 512]) for _ in range(2)])
        pgs = Rot([P.ps() for _ in range(2)])
        pus = Rot([P.ps() for _ in range(2)])
        pys = Rot([P.ps() for _ in range(2)])
        tiles = TILES[:8] if skip_ctx else TILES
        for (t0, n, v) in tiles:
            xt = xts.next()
            P.dma("sp", xt[:, :, 0:n], k.XR[:, :, t0:t0 + n].rearrange("c p t -> p c t"), W=[xt])
            ht = hts.next()
            norm_mod(P, k, C, xt, n, v, slotA, ht)
            at = ats.next()
            for f in range(NFF):
                hf = f // 11
                pg = pgs.next()
                pu = pus.next()
                for c in range(NCH):
                    P.mm(pg[:, 0:n], wg[:, c, f * 128:(f + 1) * 128], ht[:, c, 0:n], c == 0, c == NCH - 1,
                         [wgb[hf], ht], [pg])
                for c in range(NCH):
                    P.mm(pu[:, 0:n], wu[:, c, f * 128:(f + 1) * 128], ht[:, c, 0:n], c == 0, c == NCH - 1,
                         [wub[hf], ht], [pu])
                sg = sgs.next()
                P.op("act", "activation", [pg], [sg], out=sg[:, 0:n], in_=pg[:, 0:n], func=AF.Silu)
                P.op("dve", "tensor_tensor", [sg, pu], [at], out=at[:, f, 0:n], in0=sg[:, 0:n], in1=pu[:, 0:n],
                     op=ALU.mult)
            for dc in range(NCH):
                py = pys.next()
                for f in range(NFF):
                    P.mm(py[:, 0:n], wd[:, f, dc * 128:(dc + 1) * 128], at[:, f, 0:n], f == 0, f == NFF - 1,
                         [wdb[f // 11], at], [py])
                P.op("dve", "scalar_tensor_tensor", [py, xt, C["ms"]], [xt], out=xt[:, dc, 0:n], in0=py[:, 0:n],
                     scalar=C["ms"][:, slotA + 2, dc, v:v + 1], in1=xt[:, dc, 0:n], op0=ALU.mult, op1=ALU.add)
            P.store("sp", k.XR[:, :, t0:t0 + n].rearrange("c p t -> p c t"), xt[:, :, 0:n], R=[xt])


def ph_hmix(k, l):
    with Phase(k.nc, "hmix%d" % l) as P:
        C = norm_ctx(P, k)
        xts = Rot([P.sb([128, NCH, 512]) for _ in range(2)])
        hts = Rot([P.sb([128, NCH, 512], BF16) for _ in range(2)])
        for (t0, n, v) in TILES:
            xt = xts.next()
            P.dma("sp", xt[:, :, 0:n], k.XR[:, :, t0:t0 + n].rearrange("c p t -> p c t"), W=[xt])
            ht = hts.next()
            norm_mod(P, k, C, xt, n, v, 3, ht)
            P.store("sp", k.HS[:, :, t0:t0 + n].rearrange("c p t -> p c t"), ht[:, :, 0:n], R=[ht])


def load_w_cols(P, wd, l_ap, c0, ncols, splits=1):
    b = Buf()
    src = l_ap.rearrange("(c p) f -> p c f", p=128)
    for c in range(NCH):
        P.dma("pool", wd[:, c, 0:ncols], src[:, c, c0:c0 + ncols], W=[b])
    return b


def ph_qkv(k, l):
    with Phase(k.nc, "qkv%d" % l) as P:
        w = P.sb([128, NCH, 3072], BF16)
        wb = load_w_cols(P, w, k.w_in[l], 0, 3072)
        blk = P.sb([128, 128], BF16)
        P.dma("pool", blk[:], k.blk, W=[blk])
        rot = P.sb([128, 128])
        P.dma("sp", rot[:], k.rot, W=[rot])
        cosT = P.sb([128, NL])
        sinS = P.sb([128, NL])
        P.dma("sp", cosT[:], k.cosT, W=[cosT])
        P.dma("sp", sinS[:], k.sinS, W=[sinS])
        gq = P.sb([128, 2])
        g8 = P.sb([128, 2])
        P.dma("sp", gq[:, 0:1], k.q_norm[l], W=[gq])
        P.dma("sp", gq[:, 1:2], k.k_norm[l], W=[gq])
        P.op("dve", "tensor_scalar", [gq], [g8], out=g8[:], in0=gq[:], scalar1=8.0, scalar2=0.0, op0=ALU.mult,
             op1=ALU.add)
        C = {"tm": Rot([P.sb([128, 512]) for _ in range(4)]), "epsb": P.sb([128, 1])}
        C["sqv"] = C["tm"]
        P.op("dve", "memset", [], [C["epsb"]], C["epsb"][:], float(64 * EPS))
        hts = Rot([P.sb([128, NCH, 512], BF16) for _ in range(2)])
        qst = Rot([P.sb([128, 8, 512], BF16) for _ in range(2)])
        kst = Rot([P.sb([128, 8, 512], BF16) for _ in range(2)])
        vst = Rot([P.sb([128, 16, 65], BF16) for _ in range(2)])
        for vv in vst.items:
            P.op("pool", "memset", [], [vv], vv[:], 1.0)
        sqs = Rot([P.sb([128, 512], BF16) for _ in range(4)])
        rs = Rot([P.sb([128, 512]) for _ in range(4)])
        qns = Rot([P.sb([128, 512]) for _ in range(4)])
        t1s = Rot([P.sb([128, 512]) for _ in range(4)])
        t2s = Rot([P.sb([128, 512]) for _ in range(4)])
        pqs = Rot([P.ps() for _ in range(3)])
        pss = Rot([P.ps() for _ in range(2)])
        prs = Rot([P.ps() for _ in range(2)])
        pvs = Rot([P.ps() for _ in range(1)])
        for (t0, n, v) in TILES:
            ht = hts.next()
            P.dma("sp", ht[:, :, 0:n], k.HS[:, :, t0:t0 + n].rearrange("c p t -> p c t"), W=[ht])
            stg_q = qst.next()
            stg_k = kst.next()
            items = [(qi, hp) for qi in range(2) for hp in range(8)]
            st = {}

            def s1(it):
                qi, hp = it
                col = qi * 1024 + hp * 128
                pq = pqs.next()
                for c in range(NCH):
                    P.mm(pq[:, 0:n], w[:, c, col:col + 128], ht[:, c, 0:n], c == 0, c == NCH - 1, [wb, ht], [pq])
                sq = sqs.next()
                P.op("act", "activation", [pq], [sq], out=sq[:, 0:n], in_=pq[:, 0:n], func=AF.Square)
                st[it] = dict(pq=pq, sq=sq)

            def s2(it):
                qi, hp = it
                d_ = st[it]
                sg = stg_q if qi == 0 else stg_k
                ss = pss.next()
                P.mm(ss[:, 0:n], blk[:], d_["sq"][:, 0:n], True, True, [blk, d_["sq"]], [ss])
                r = rs.next()
                rsqrt_psum(P, C, ss, r, n, 0.0)
                qn = qns.next()
                P.op("dve", "scalar_tensor_tensor", [d_["pq"], r, g8], [qn], out=qn[:, 0:n], in0=d_["pq"][:, 0:n],
                     scalar=g8[:, qi:qi + 1], in1=r[:, 0:n], op0=ALU.mult, op1=ALU.mult)
                d_["qn"] = qn
                if v == 1:
                    P.op("act", "activation", [qn], [sg], out=sg[:, hp, 0:n], in_=qn[:, 0:n], func=AF.Copy)

            def s3(it):
                qi, hp = it
                d_ = st.pop(it)
                if v == 1:
                    return
                sg = stg_q if qi == 0 else stg_k
                qn = d_["qn"]
                pr = prs.next()
                P.mm(pr[:, 0:n], rot[:], qn[:, 0:n], True, True, [rot, qn], [pr])
                t1 = t1s.next()
                P.op("pool", "tensor_tensor", [qn, cosT], [t1], out=t1[:, 0:n], in0=qn[:, 0:n],
                     in1=cosT[:, t0:t0 + n], op=ALU.mult)
                t2 = t2s.next()
                P.op("dve", "tensor_tensor", [pr, sinS], [t2], out=t2[:, 0:n], in0=pr[:, 0:n],
                     in1=sinS[:, t0:t0 + n], op=ALU.mult)
                P.op("dve", "tensor_tensor", [t1, t2], [sg], out=sg[:, hp, 0:n], in0=t1[:, 0:n], in1=t2[:, 0:n],
                     op=ALU.add)

            for step in range(len(items) + 2):
                if step < len(items):
                    s1(items[step])
                if 1 <= step <= len(items):
                    s2(items[step - 1])
                if step >= 2:
                    s3(items[step - 2])
            P.store("sp", k.Q[:, :, t0:t0 + n].rearrange("c p t -> p c t"), stg_q[:, :, 0:n], R=[stg_q])
            P.store("sp", k.KK[:, :, t0:t0 + n].rearrange("c p t -> p c t"), stg_k[:, :, 0:n], R=[stg_k])
            for b in range(n // 128):
                vs = vst.next()
                for half in range(2):
                    pv = pvs.next()
                    for c in range(NCH):
                        P.mm(pv[:, :], ht[:, c, b * 128:(b + 1) * 128], w[:, c, 2048 + half * 512:2048 + (half + 1) * 512],
                             c == 0, c == NCH - 1, [wb, ht], [pv])
                    P.op("act", "activation", [pv], [vs], out=vs[:, half * 8:(half + 1) * 8, 0:64],
                         in_=pv[:, :].rearrange("p (h e) -> p h e", e=64), func=AF.Copy)
                P.store("sp", k.V[t0 + b * 128:t0 + (b + 1) * 128, :], vs[:].rearrange("p h e -> p (h e)"), R=[vs])


def ph_zdt(k, l):
    with Phase(k.nc, "zdt%d" % l) as P:
        w = P.sb([128, NCH, 2048 + 64], BF16)
        wb = load_w_cols(P, w, k.w_in[l], OZ, 2048)
        wb2 = Buf()
        src = k.w_in[l].rearrange("(c p) f -> p c f", p=128)
        P.dma("pool", w[:, :, 2048:2112], src[:, :, ODT:ODT + 64], W=[wb2])
        hts = Rot([P.sb([128, NCH, 512], BF16) for _ in range(2)])
        zst = Rot([P.sb([128, 2048]) for _ in range(2)])
        dst_ = Rot([P.sb([128, 64]) for _ in range(2)])
        pzs = Rot([P.ps() for _ in range(4)])
        pds = Rot([P.ps([128, 64]) for _ in range(2)])
        cp = Rot(["act", "dve"])
        for (t0, n, v) in TILES:
            ht = hts.next()
            P.dma("pool", ht[:, :, 0:n], k.HS[:, :, t0:t0 + n].rearrange("c p t -> p c t"), W=[ht])
            for b in range(n // 128):
                zs = zst.next()
                for q4 in range(4):
                    pz = pzs.next()
                    for c in range(NCH):
                        P.mm(pz[:, :], ht[:, c, b * 128:(b + 1) * 128], w[:, c, q4 * 512:(q4 + 1) * 512],
                             c == 0, c == NCH - 1, [wb, ht], [pz])
                    if cp.next() == "act":
                        P.op("act", "activation", [pz], [zs], out=zs[:, q4 * 512:(q4 + 1) * 512], in_=pz[:, :],
                             func=AF.Copy)
                    else:
                        P.op("dve", "tensor_copy", [pz], [zs], out=zs[:, q4 * 512:(q4 + 1) * 512], in_=pz[:, :])
                P.store("sp", k.Z[t0 + b * 128:t0 + (b + 1) * 128, :], zs[:], R=[zs])
                pd = pds.next()
                for c in range(NCH):
                    P.mm(pd[:, :], ht[:, c, b * 128:(b + 1) * 128], w[:, c, 2048:2112], c == 0, c == NCH - 1,
                         [wb2, ht], [pd])
                ds = dst_.next()
                P.op("dve", "tensor_copy", [pd], [ds], out=ds[:], in_=pd[:, :])
                P.store("sp", k.DT[t0 + b * 128:t0 + (b + 1) * 128, :], ds[:], R=[ds])


def ph_proj_fm(k, l, name, c0, nchunks, dst, sigmoid):
    with Phase(k.nc, "%s%d" % (name, l)) as P:
        w = P.sb([128, NCH, nchunks * 128], BF16)
        wb = load_w_cols(P, w, k.w_in[l], c0, nchunks * 128)
        hts = Rot([P.sb([128, NCH, 512], BF16) for _ in range(2)])
        stg = Rot([P.sb([128, 8, 512]) for _ in range(2)])
        pps = Rot([P.ps() for _ in range(4)])
        cp = Rot(["act", "dve"])
        for (t0, n, v) in TILES:
            ht = hts.next()
            P.dma("pool", ht[:, :, 0:n], k.HS[:, :, t0:t0 + n].rearrange("c p t -> p c t"), W=[ht])
            for g8 in range(nchunks // 8):
                sg = stg.next()
                for j in range(8):
                    ch = g8 * 8 + j
                    pp = pps.next()
                    for c in range(NCH):
                        P.mm(pp[:, 0:n], w[:, c, ch * 128:(ch + 1) * 128], ht[:, c, 0:n], c == 0, c == NCH - 1,
                             [wb, ht], [pp])
                    if sigmoid:
                        P.op("act", "activation", [pp], [sg], out=sg[:, j, 0:n], in_=pp[:, 0:n], func=AF.Sigmoid)
                    elif cp.next() == "act":
                        P.op("act", "activation", [pp], [sg], out=sg[:, j, 0:n], in_=pp[:, 0:n], func=AF.Copy)
                    else:
                        P.op("dve", "tensor_copy", [pp], [sg], out=sg[:, j, 0:n], in_=pp[:, 0:n])
                P.store("sp", dst[g8 * 8:(g8 + 1) * 8, :, t0:t0 + n].rearrange("c p t -> p c t"), sg[:, :, 0:n], R=[sg])


def ph_conv(k, l):
    with Phase(k.nc, "conv%d" % l) as P:
        ident = P.sb([128, 128])
        P.dma("sp", ident[:], k.ident, W=[ident])
        identb = P.sb([128, 128], BF16)
        P.dma("pool", identb[:], k.ident, W=[identb])
        cw = P.sb([128, 24, 5])
        cb = P.sb([128, 24])
        P.dma("sp", cw[:], k.conv_w[l], W=[cw])
        P.dma("sp", cb[:], k.conv_b[l], W=[cb])
        xins = Rot([P.sb([128, 8, 516]) for _ in range(2)])
        accs = Rot([P.sb([128, 512]) for _ in range(8)])
        us = Rot([P.sb([128, 512]) for _ in range(8)])
        xst = Rot([P.sb([128, 4, 2048]) for _ in range(1)])
        bfs = Rot([P.sb([128, 4, 512], BF16) for _ in range(2)])
        bts = Rot([P.sb([128, 4, 512], BF16) for _ in range(2)])
        pts = Rot([P.ps() for _ in range(3)])
        ptb = Rot([P.ps([128, 512], BF16) for _ in range(2)])
        tapeng = Rot(["dve"])
        cp = Rot(["dve", "act"])
        for (t0, n, v) in TILES:
            seg0, seg1 = (0, NL) if v == 0 else (NL, NT)
            nb = n // 128
            xs_t = xst.next()
            for g in range(3):
                xin = xins.next()
                lo = max(t0 - 2, seg0)
                hi = min(t0 + n + 2, seg1)
                if lo > t0 - 2:
                    P.op("pool", "memset", [], [xin], xin[:, :, 0:2], 0.0)
                if hi < t0 + n + 2:
                    P.op("pool", "memset", [], [xin], xin[:, :, n + 2:n + 4], 0.0)
                P.dma("pool", xin[:, :, lo - (t0 - 2):hi - (t0 - 2)],
                      k.XBC[g * 8:(g + 1) * 8, :, lo:hi].rearrange("c p t -> p c t"), W=[xin])
                if g == 2:
                    bf = bfs.next()
                    cf = bfs.next()
                    bt = bts.next()
                for quad in range(2):
                    ul = []
                    accl = [accs.next() for _ in range(4)]
                    for j4 in range(4):
                        j = quad * 4 + j4
                        ch = g * 8 + j
                        P.op("act", "activation", [xin, cw, cb], [accl[j4]], out=accl[j4][:, 0:n], in_=xin[:, j, 0:n],
                             func=AF.Identity, scale=cw[:, ch, 0:1], bias=cb[:, ch:ch + 1])
                    for kk in range(1, 5):
                        for j4 in range(4):
                            j = quad * 4 + j4
                            ch = g * 8 + j
                            P.op("dve", "scalar_tensor_tensor", [xin, cw, accl[j4]], [accl[j4]], out=accl[j4][:, 0:n],
                                 in0=xin[:, j, kk:kk + n], scalar=cw[:, ch, kk:kk + 1], in1=accl[j4][:, 0:n],
                                 op0=ALU.mult, op1=ALU.add)
                    for j4 in range(4):
                        acc = accl[j4]
                        if g < 2:
                            u = us.next()
                            P.op("act", "activation", [acc], [u], out=u[:, 0:n], in_=acc[:, 0:n], func=AF.Silu)
                            ul.append(u)
                        elif quad == 0:
                            P.op("act", "activation", [acc], [bf], out=bf[:, j4, 0:n], in_=acc[:, 0:n], func=AF.Silu)
                        else:
                            P.op("act", "activation", [acc], [cf], out=cf[:, j4, 0:n], in_=acc[:, 0:n], func=AF.Silu)
                    if g < 2:
                        q16 = g * 2 + quad
                        for b in range(nb):
                            pt = pts.next()
                            for j4 in range(4):
                                P.mm(pt[:, j4 * 128:(j4 + 1) * 128], ul[j4][:, b * 128:(b + 1) * 128], ident[:],
                                     True, True, [ul[j4], ident], [pt])
                            if cp.next() == "dve":
                                P.op("dve", "tensor_copy", [pt], [xs_t], out=xs_t[:, b, q16 * 512:(q16 + 1) * 512],
                                     in_=pt[:, :])
                            else:
                                P.op("act", "activation", [pt], [xs_t], out=xs_t[:, b, q16 * 512:(q16 + 1) * 512],
                                     in_=pt[:, :], func=AF.Copy)
                    elif quad == 0:
                        for b in range(nb):
                            pb = ptb.next()
                            for j4 in range(4):
                                P.S.add("pe", (lambda e, o=pb[:, j4 * 128:(j4 + 1) * 128], i=bf[:, j4, b * 128:(b + 1) * 128]:
                                               e.transpose(o, i, identb[:])), _bufs([bf, identb]), _bufs([pb]))
                            P.op("dve", "tensor_copy", [pb], [bt], out=bt[:, b, :], in_=pb[:, :])
                        P.store("sp", k.BF[:, :, t0:t0 + n].rearrange("c p t -> p c t"), bf[:, :, 0:n], R=[bf])
                        P.store("sp", k.BT[t0:t0 + n, :].rearrange("(b p) f -> p b f", p=128), bt[:, 0:nb, :], R=[bt])
                    else:
                        P.store("sp", k.CF[:, :, t0:t0 + n].rearrange("c p t -> p c t"), cf[:, :, 0:n], R=[cf])
            P.store("sp", k.XS[t0:t0 + n, :].rearrange("(b p) f -> p b f", p=128), xs_t[:, 0:nb, :], R=[xs_t])


def _qrange(j):
    qlo = 0 if j <= 3 else 2 * j - 4
    qhi = 63 if j >= 28 else 2 * j + 5
    return qlo, qhi


def ph_attn(k, l, need_ctx):
    with Phase(k.nc, "attn%d" % l) as P:
        mf = P.sb([128, 1024], BF16)
        mi = P.sb([128, 1024], BF16)
        P.dma("pool", mf[:], k.maskF, W=[mf])
        P.dma("pool", mi[:], k.maskI, W=[mi])
        kts = Rot([P.sb([128, NT], BF16) for _ in range(2)])
        qts = Rot([P.sb([128, NT], BF16) for _ in range(2)])
        vts = Rot([P.sb([128, 34, 2, 65], BF16) for _ in range(2)])
        aos = Rot([P.sb([128, 34, 128], BF16) for _ in range(2)])
        PT = P.sb([128, 32, 768], BF16)
        PTb = [Buf() for _ in range(32)]
        PC = P.sb([128, 2, NT], BF16)
        PCb = [Buf() for _ in range(9)]
        rbs = Rot([P.sb([128, 1024]) for _ in range(2)])
        ees = Rot([P.sb([128, 1024]) for _ in range(2)])
        efs = Rot([P.sb([128, 1024], BF16) for _ in range(2)])
        eis = Rot([P.sb([128, 1024], BF16) for _ in range(2)])
        exs = Rot([P.sb([128, 512]) for _ in range(3)])
        rcs = Rot([P.sb([128, 8]) for _ in range(2)])
        pss = Rot([P.ps() for _ in range(4)])
        pos = Rot([P.ps() for _ in range(3)])
        nqb = 34 if need_ctx else 32
        for hp in range(8):
            kt = kts.next()
            qt = qts.next()
            vt = vts.next()
            ao = aos.next()
            P.dma("sp", kt[:], k.KK[hp], W=[kt])
            P.dma("sp", qt[:], k.Q[hp], W=[qt])
            P.dma("sp", vt[:].rearrange("p b a e -> p b (a e)"),
                  k.V[:, hp * 130:(hp + 1) * 130].rearrange("(b p) f -> p b f", p=128), W=[vt])
            for a in range(2):
                h = hp * 2 + a
                pr = slice(a * 64, a * 64 + 64)
                rb = rbs.next()
                P.dma("sp", rb[:], k.RB[l, h], W=[rb])
                ee = ees.next()
                P.op("act", "activation", [rb], [ee], out=ee[:], in_=rb[:], func=AF.Exp)
                ef = efs.next()
                ei = eis.next()
                P.op("pool", "tensor_tensor", [ee, mf], [ef], out=ef[:], in0=ee[:], in1=mf[:], op=ALU.mult)
                P.op("pool", "tensor_tensor", [ee, mi], [ei], out=ei[:], in0=ee[:], in1=mi[:], op=ALU.mult)
                for j in range(32):
                    qlo, qhi = _qrange(j)
                    nq = (qhi - qlo + 1) * 64
                    off = 0
                    while off < nq:
                        ln = min(512, nq - off)
                        ps = pss.next()
                        P.mm(ps[:, 0:ln], kt[pr, j * 128:(j + 1) * 128], qt[pr, qlo * 64 + off:qlo * 64 + off + ln],
                             True, True, [kt, qt], [ps])
                        ex = exs.next()
                        P.op("act", "activation", [ps], [ex], out=ex[:, 0:ln], in_=ps[:, 0:ln], func=AF.Exp, scale=0.125)
                        r0 = qlo + off // 64
                        r1 = r0 + ln // 64
                        qr = r0
                        while qr < r1:
                            full = (qr <= 3 or qr >= 61)
                            qe = qr
                            while qe < r1 and ((qe <= 3 or qe >= 61) == full):
                                qe += 1
                            tab = ef if full else ei
                            s0 = 7 - 2 * j + qr
                            cnt = (qe - qr) * 64
                            assert 0 <= s0 and s0 * 64 + cnt <= 1024
                            P.op("dve", "tensor_tensor", [ex, tab], [PTb[j]],
                                 out=PT[:, j, (qr - qlo) * 64:(qr - qlo) * 64 + cnt],
                                 in0=ex[:, (qr - r0) * 64:(qr - r0) * 64 + cnt], in1=tab[:, s0 * 64:s0 * 64 + cnt],
                                 op=ALU.mult)
                            qr = qe
                        off += ln
                for ti, (t0, n, v) in enumerate(TILES if need_ctx else TILES[:8]):
                    for cc in range(2):
                        ps = pss.next()
                        P.mm(ps[:, 0:n], kt[pr, NL + cc * 128:NL + (cc + 1) * 128], qt[pr, t0:t0 + n], True, True,
                             [kt, qt], [ps])
                        P.op("act", "activation", [ps], [PCb[ti]], out=PC[:, cc, t0:t0 + n], in_=ps[:, 0:n],
                             func=AF.Exp, scale=0.125)
                for g0 in range(0, nqb, 7):
                    grp = list(range(g0, min(nqb, g0 + 7)))
                    po = pos.next()
                    for si, i in enumerate(grp):
                        mms = []
                        if i < 32:
                            for jj in range(32):
                                qlo, qhi = _qrange(jj)
                                if qlo <= 2 * i and 2 * i + 1 <= qhi:
                                    mms.append((PT[:, jj, (2 * i - qlo) * 64:(2 * i - qlo) * 64 + 128],
                                                vt[:, jj, a, :], PTb[jj]))
                            ti = i // 4
                        else:
                            ti = 8
                        for cc in range(2):
                            mms.append((PC[:, cc, i * 128:(i + 1) * 128], vt[:, 32 + cc, a, :], PCb[ti]))
                        for mi_, (lh, rh, bb) in enumerate(mms):
                            P.mm(po[:, si * 65:(si + 1) * 65], lh, rh, mi_ == 0, mi_ == len(mms) - 1, [bb, vt], [po])
                    ng = len(grp)
                    rc = rcs.next()
                    pov = po[:, 0:ng * 65].rearrange("p (s e) -> p s e", e=65)
                    P.op("dve", "reciprocal", [po], [rc], out=rc[:, 0:ng], in_=pov[:, :, 64])
                    P.op("dve", "tensor_tensor", [po, rc], [ao], out=ao[:, g0:g0 + ng, a * 64:(a + 1) * 64],
                         in0=pov[:, :, 0:64], in1=rc[:, 0:ng].unsqueeze(2).to_broadcast([128, ng, 64]), op=ALU.mult)
            P.store("sp", k.AO[0:nqb * 128, hp * 128:(hp + 1) * 128].rearrange("(b p) f -> p b f", p=128),
                    ao[:, 0:nqb, :], R=[ao])


def ssd_consts(P, k, l):
    C = {}
    for nm, src in [("triF", k.triF), ("triB", k.triB), ("nm2F", k.nm2F), ("nm2B", k.nm2B), ("negI", k.negI),
                    ("ones", k.ones), ("ntriF", k.ntriF), ("ntriB", k.ntriB)]:
        C[nm] = P.sb([128, 128], BF16)
        P.dma("pool", C[nm][:], src, W=[C[nm]])
    C["dtb"] = P.sb([128, 64])
    C["alog"] = P.sb([128, 64])
    C["abc"] = P.sb([128, 64])
    C["one"] = P.sb([128, 1])
    P.op("dve", "memset", [], [C["one"]], C["one"][:], 1.0)
    P.dma("sp", C["dtb"][:], k.dt_bias_bc[l], W=[C["dtb"]])
    P.dma("sp", C["alog"][:], k.a_log_bc[l], W=[C["alog"]])
    P.op("act", "activation", [C["alog"]], [C["abc"]], out=C["abc"][:], in_=C["alog"][:], func=AF.Exp)
    P.op("dve", "tensor_scalar", [C["abc"]], [C["abc"]], out=C["abc"][:], in0=C["abc"][:], scalar1=-1.0, scalar2=0.0,
         op0=ALU.mult, op1=ALU.add)
    return C


NBLK = NT // 128


def ssd_prep_all(P, k, C, pbanks):
    W_ = NBLK * 64

    def big(dt_=F32):
        return P.sb([128, NBLK, 64], dt_)

    def bcv(t):
        return t[:, :].unsqueeze(1).to_broadcast([128, NBLK, 64])

    def flat(t):
        return t[:].rearrange("p c f -> p (c f)")

    tmp = big()
    P.dma("sp", tmp[:], k.DT.rearrange("(c p) f -> p c f", p=128), W=[tmp])
    P.op("dve", "tensor_tensor", [tmp, C["dtb"]], [tmp], out=tmp[:], in0=tmp[:], in1=bcv(C["dtb"]), op=ALU.add)
    P.op("act", "activation", [tmp], [tmp], out=flat(tmp), in_=flat(tmp), func=AF.Exp)
    dtv = big()
    P.op("act", "activation", [tmp, C["one"]], [dtv], out=flat(dtv), in_=flat(tmp), func=AF.Ln, bias=C["one"][:, 0:1],
         scale=1.0)
    P.op("dve", "tensor_tensor", [dtv, C["abc"]], [tmp], out=tmp[:], in0=dtv[:], in1=bcv(C["abc"]), op=ALU.mult)
    hi = big(BF16)
    lo = big(BF16)
    P.op("dve", "tensor_copy", [tmp], [hi], out=hi[:], in_=tmp[:])
    P.op("dve", "tensor_tensor", [tmp, hi], [lo], out=lo[:], in0=tmp[:], in1=hi[:], op=ALU.subtract)
    acum = big()
    alast = big()
    pb = Rot(pbanks)
    for d, tri in ((0, C["triF"]), (1, C["triB"])):
        for c0 in range(0, NBLK, 16):
            c1 = min(NBLK, c0 + 16)
            ncol = (c1 - c0) * 32
            ps = pb.next()
            P.mm(ps[:, 0:ncol], tri[:], hi[:, c0:c1, d * 32:(d + 1) * 32], True, False, [tri, hi], [ps])
            P.mm(ps[:, 0:ncol], tri[:], lo[:, c0:c1, d * 32:(d + 1) * 32], False, True, [tri, lo], [ps])
            P.op("dve", "tensor_copy", [ps], [acum], out=acum[:, c0:c1, d * 32:(d + 1) * 32],
                 in_=ps[:, 0:ncol].rearrange("p (c f) -> p c f", f=32))
    for o0 in range(0, W_, 512):
        o1 = min(W_, o0 + 512)
        ps = pb.next()
        P.mm(ps[:, 0:o1 - o0], C["ones"][:], flat(hi)[:, o0:o1], True, False, [C["ones"], hi], [ps])
        P.mm(ps[:, 0:o1 - o0], C["ones"][:], flat(lo)[:, o0:o1], False, True, [C["ones"], lo], [ps])
        P.op("dve", "tensor_copy", [ps], [alast], out=flat(alast)[:, o0:o1], in_=ps[:, 0:o1 - o0])
    eacum = big()
    P.op("act", "activation", [acum], [eacum], out=flat(eacum), in_=flat(acum), func=AF.Exp)
    sv = big()
    P.op("dve", "tensor_tensor", [alast, acum], [sv], out=sv[:], in0=alast[:], in1=acum[:], op=ALU.subtract)
    P.op("act", "activation", [sv], [sv], out=flat(sv), in_=flat(sv), func=AF.Exp)
    P.op("dve", "tensor_tensor", [sv, dtv], [sv], out=sv[:], in0=sv[:], in1=dtv[:], op=ALU.mult)
    ealast = big()
    P.op("act", "activation", [alast], [ealast], out=flat(ealast), in_=flat(alast), func=AF.Exp)
    return dict(dtv=dtv, hi=hi, eacum=eacum, sv=sv, ealast=ealast)


def _bc(ap32, n=32):
    return ap32.unsqueeze(2).to_broadcast([128, n, 64])


def ph_ssd_a(k, l, need_ctx):
    with Phase(k.nc, "ssda%d" % l) as P:
        C = ssd_consts(P, k, l)
        plp = [P.ps() for _ in range(3)]
        A = ssd_prep_all(P, k, C, plp[:2])
        plp = Rot(plp)
        xss = Rot([P.sb([128, 2048]) for _ in range(3)])
        bts = Rot([P.sb([128, 512], BF16) for _ in range(2)])
        bfs = Rot([P.sb([128, 4, 128], BF16) for _ in range(2)])
        cfs = Rot([P.sb([128, 4, 128], BF16) for _ in range(2)])
        xdf = Rot([P.sb([128, 2048], BF16) for _ in range(2)])
        xdb = Rot([P.sb([128, 2048], BF16) for _ in range(2)])
        xsf = Rot([P.sb([128, 2048], BF16) for _ in range(2)])
        yts = Rot([P.sb([128, 2048]) for _ in range(2)])
        lxs = Rot([P.sb([128, 512]) for _ in range(3)])
        wts = Rot([P.sb([128, 512], BF16) for _ in range(3)])
        tms = Rot([P.sb([128, 512]) for _ in range(4)])
        Sf = P.sb([128, 2048])
        Sfb = P.sb([128, 2048], BF16)
        Sg = [Buf() for _ in range(4)]
        Sgb = [Buf() for _ in range(4)]
        P.op("pool", "memset", [], Sg, Sf[:], 0.0)
        P.op("pool", "memset", [], Sgb, Sfb[:], 0.0)
        pcb = Rot([P.ps() for _ in range(1)])
        pys = Rot([P.ps() for _ in range(2)])
        pos = Rot([P.ps() for _ in range(2)])
        for t0 in CHUNKS_F:
            ci = t0 // 128
            is_ctx = t0 >= NL
            do_y = need_ctx or not is_ctx
            xs = xss.next()
            bt = bts.next()
            P.dma("pool", xs[:], k.XS[t0:t0 + 128, :], W=[xs])
            P.dma("sp", bt[:], k.BT[t0:t0 + 128, :], W=[bt])
            xf = xsf.next()
            P.op("dve", "tensor_tensor", [xs, A["sv"]], [xf], out=xf[:].rearrange("p (h e) -> p h e", e=64),
                 in0=xs[:].rearrange("p (h e) -> p h e", e=64), in1=_bc(A["sv"][:, ci, 0:32]), op=ALU.mult)
            if do_y:
                bf = bfs.next()
                cf = cfs.next()
                P.dma("sp", bf[:], k.BF[:, :, t0:t0 + 128].rearrange("c p t -> p c t"), W=[bf])
                P.dma("sp", cf[:], k.CF[:, :, t0:t0 + 128].rearrange("c p t -> p c t"), W=[cf])
                xd = [xdf.next(), xdb.next()]
                for d in range(2):
                    P.op("pool", "tensor_tensor", [xs, A["dtv"]], [xd[d]],
                         out=xd[d][:].rearrange("p (h e) -> p h e", e=64),
                         in0=xs[:].rearrange("p (h e) -> p h e", e=64),
                         in1=_bc(A["dtv"][:, ci, d * 32:(d + 1) * 32]), op=ALU.mult)
                cbp = pcb.next()
                for g in range(4):
                    P.mm(cbp[:, g * 128:(g + 1) * 128], bf[:, g, :], cf[:, g, :], True, True, [bf, cf], [cbp])
                yt = yts.next()
            def fill(g, quad):
                lp = plp.next()
                prs = [(quad * 2 + q2, d) for q2 in range(2) for d in range(2)]
                for s_, (hh, d) in enumerate(prs):
                    col = d * 32 + g * 8 + hh
                    tri, ntri, nm2 = (C["triF"], C["ntriF"], C["nm2F"]) if d == 0 else \
                        (C["triB"], C["ntriB"], C["nm2B"])
                    hb = A["hi"][:, ci, col:col + 1].to_broadcast([128, 128])
                    reg = lp[:, s_ * 128:(s_ + 1) * 128]
                    P.mm(reg, hb, tri[:], True, False, [A["hi"], tri], [lp])
                    P.mm(reg, ntri[:], hb, False, False, [A["hi"], ntri], [lp])
                    P.mm(reg, C["negI"][:], nm2[:], False, True, [C["negI"], nm2], [lp])
                lx = lxs.next()
                P.op("act", "activation", [lp], [lx], out=lx[:], in_=lp[:, :], func=AF.Exp)
                wt = wts.next()
                P.op("dve", "tensor_tensor", [lx, cbp], [wt], out=wt[:].rearrange("p (s i) -> p s i", s=4),
                     in0=lx[:].rearrange("p (s i) -> p s i", s=4),
                     in1=cbp[:, g * 128:(g + 1) * 128].unsqueeze(1).to_broadcast([128, 4, 128]), op=ALU.mult)
                return wt, prs

            def state_update(g):
                pst = pos.next()
                P.mm(pst[:, :], bt[:, g * 128:(g + 1) * 128], xf[:, g * 512:(g + 1) * 512], True, True, [bt, xf], [pst])
                tm2 = tms.next()
                P.op("pool", "tensor_tensor", [Sg[g], A["ealast"]], [tm2], out=tm2[:].rearrange("p (h e) -> p h e", e=64),
                     in0=Sf[:, g * 512:(g + 1) * 512].rearrange("p (h e) -> p h e", e=64),
                     in1=_bc(A["ealast"][:, ci, g * 8:(g + 1) * 8], 8), op=ALU.mult)
                P.op("dve", "tensor_tensor", [tm2, pst], [Sg[g]], out=Sf[:, g * 512:(g + 1) * 512], in0=tm2[:], in1=pst[:, :],
                     op=ALU.add)
                P.op("act", "activation", [Sg[g]], [Sgb[g]], out=Sfb[:, g * 512:(g + 1) * 512],
                     in_=Sf[:, g * 512:(g + 1) * 512], func=AF.Copy)

            if do_y:
                jobs = [(g, quad) for g in range(4) for quad in range(4)]
                filled = {}
                pyg = {}
                SKEW = 2
                for idx in range(len(jobs) + SKEW):
                    if idx < len(jobs):
                        filled[idx] = fill(*jobs[idx])
                    if idx >= SKEW:
                        g, quad = jobs[idx - SKEW]
                        wt, prs = filled.pop(idx - SKEW)
                        if quad == 0:
                            pyg[g] = pys.next()
                        py = pyg[g]
                        for s_, (hh, d) in enumerate(prs):
                            h = g * 8 + hh
                            P.mm(py[:, hh * 64:(hh + 1) * 64], wt[:, s_ * 128:(s_ + 1) * 128],
                                 xd[d][:, h * 64:(h + 1) * 64], d == 0, d == 1, [wt, xd[d]], [py])
                        if quad == 3:
                            po = pos.next()
                            P.mm(po[:, :], cf[:, g, :], Sfb[:, g * 512:(g + 1) * 512], True, True, [cf, Sgb[g]], [po])
                            tm = tms.next()
                            P.op("dve", "tensor_tensor", [po, A["eacum"]], [tm],
                                 out=tm[:].rearrange("p (h e) -> p h e", e=64),
                                 in0=po[:, :].rearrange("p (h e) -> p h e", e=64),
                                 in1=_bc(A["eacum"][:, ci, g * 8:(g + 1) * 8], 8), op=ALU.mult)
                            P.op("dve", "tensor_tensor", [tm, py], [yt], out=yt[:, g * 512:(g + 1) * 512], in0=tm[:],
                                 in1=py[:, :], op=ALU.add)
                            state_update(g)
            else:
                for g in range(4):
                    state_update(g)
            if do_y:
                P.store("sp", k.YP[t0:t0 + 128, :], yt[:], R=[yt])


def ph_ssd_b(k, l, need_ctx):
    with Phase(k.nc, "ssdb%d" % l) as P:
        C = ssd_consts(P, k, l)
        pos_banks = [P.ps() for _ in range(4)]
        A = ssd_prep_all(P, k, C, pos_banks[:2])
        epsb = P.sb([128, 1])
        P.op("dve", "memset", [], [epsb], epsb[:], float(EPS))
        dbc = P.sb([128, 32])
        nwb = P.sb([128, 2048])
        P.dma("sp", dbc[:], k.ssd_d_bc[l], W=[dbc])
        P.dma("sp", nwb[:], k.ssd_norm_bc[l], W=[nwb])
        xss = Rot([P.sb([128, 2048]) for _ in range(3)])
        bts = Rot([P.sb([128, 512], BF16) for _ in range(2)])
        cfs = Rot([P.sb([128, 4, 128], BF16) for _ in range(2)])
        yps = Rot([P.sb([128, 2048]) for _ in range(3)])
        zs = Rot([P.sb([128, 2048]) for _ in range(3)])
        xsb = Rot([P.sb([128, 2048], BF16) for _ in range(2)])
        sos = Rot([P.sb([128, 2048], BF16) for _ in range(2)])
        jk = P.sb([128, 512])
        tms = Rot([P.sb([128, 512]) for _ in range(3)])
        ssqs = Rot([P.sb([128, 4]) for _ in range(2)])
        rrs = Rot([P.sb([128, 4]) for _ in range(2)])
        Sb = P.sb([128, 2048])
        Sbb = P.sb([128, 2048], BF16)
        Sg = [Buf() for _ in range(4)]
        Sgb = [Buf() for _ in range(4)]
        P.op("pool", "memset", [], Sg, Sb[:], 0.0)
        P.op("pool", "memset", [], Sgb, Sbb[:], 0.0)
        pos = Rot(pos_banks)
        for t0 in CHUNKS_B:
            is_ctx = t0 >= NL
            do_y = need_ctx or not is_ctx
            xs = xss.next()
            bt = bts.next()
            P.dma("sp", xs[:], k.XS[t0:t0 + 128, :], W=[xs])
            P.dma("sp", bt[:], k.BT[t0:t0 + 128, :], W=[bt])
            ci = t0 // 128
            xb = xsb.next()
            P.op("dve", "tensor_tensor", [xs, A["sv"]], [xb], out=xb[:].rearrange("p (h e) -> p h e", e=64),
                 in0=xs[:].rearrange("p (h e) -> p h e", e=64), in1=_bc(A["sv"][:, ci, 32:64]), op=ALU.mult)
            if do_y:
                cf = cfs.next()
                yp = yps.next()
                z = zs.next()
                P.dma("sp", cf[:], k.CF[:, :, t0:t0 + 128].rearrange("c p t -> p c t"), W=[cf])
                P.dma("pool", yp[:], k.YP[t0:t0 + 128, :], W=[yp])
                P.dma("pool", z[:], k.Z[t0:t0 + 128, :], W=[z])
                P.op("act", "activation", [z], [z], out=z[:], in_=z[:], func=AF.Silu)
                ssq = ssqs.next()
                P.op("pool", "memset", [], [ssq], ssq[:], 0.0)
            for g in range(4):
                gs = slice(g * 512, (g + 1) * 512)
                if do_y:
                    po = pos.next()
                    P.mm(po[:, :], cf[:, g, :], Sbb[:, gs], True, True, [cf, Sgb[g]], [po])
                    tm = tms.next()
                    P.op("dve", "tensor_tensor", [po, A["eacum"]], [tm], out=tm[:].rearrange("p (h e) -> p h e", e=64),
                         in0=po[:, :].rearrange("p (h e) -> p h e", e=64),
                         in1=_bc(A["eacum"][:, ci, 32 + g * 8:32 + (g + 1) * 8], 8), op=ALU.mult)
                    P.op("dve", "tensor_tensor", [tm, yp], [yp], out=yp[:, gs], in0=tm[:], in1=yp[:, gs], op=ALU.add)
                    tm3 = tms.next()
                    P.op("pool", "tensor_tensor", [xs, dbc], [tm3], out=tm3[:].rearrange("p (h e) -> p h e", e=64),
                         in0=xs[:, gs].rearrange("p (h e) -> p h e", e=64), in1=_bc(dbc[:, g * 8:(g + 1) * 8], 8),
                         op=ALU.mult)
                    P.op("pool", "tensor_tensor", [tm3, yp], [yp], out=yp[:, gs], in0=tm3[:], in1=yp[:, gs], op=ALU.add)
                    P.op("dve", "tensor_tensor", [yp, z], [yp], out=yp[:, gs], in0=yp[:, gs], in1=z[:, gs], op=ALU.mult)
                    P.op("act", "activation", [yp, ssq], [jk, ssq], out=jk[:], in_=yp[:, gs], func=AF.Square,
                         accum_out=ssq[:, g:g + 1])
                pst = pos.next()
                P.mm(pst[:, :], bt[:, g * 128:(g + 1) * 128], xb[:, gs], True, True, [bt, xb], [pst])
                tm2 = tms.next()
                P.op("pool", "tensor_tensor", [Sg[g], A["ealast"]], [tm2], out=tm2[:].rearrange("p (h e) -> p h e", e=64),
                     in0=Sb[:, gs].rearrange("p (h e) -> p h e", e=64),
                     in1=_bc(A["ealast"][:, ci, 32 + g * 8:32 + (g + 1) * 8], 8), op=ALU.mult)
                P.op("dve", "tensor_tensor", [tm2, pst], [Sg[g]], out=Sb[:, gs], in0=tm2[:], in1=pst[:, :], op=ALU.add)
                P.op("act", "activation", [Sg[g]], [Sgb[g]], out=Sbb[:, gs], in_=Sb[:, gs], func=AF.Copy)
            if do_y:
                rr = rrs.next()
                P.op("act", "activation", [ssq, epsb], [rr], out=rr[:], in_=ssq[:], func=AF.Sqrt, bias=epsb[:, 0:1],
                     scale=1.0 / 512.0)
                P.op("dve", "reciprocal", [rr], [rr], out=rr[:], in_=rr[:])
                so = sos.next()
                for g in range(4):
                    gs = slice(g * 512, (g + 1) * 512)
                    P.op("dve", "scalar_tensor_tensor", [yp, rr, nwb], [so], out=so[:, gs], in0=yp[:, gs],
                         scalar=rr[:, g:g + 1], in1=nwb[:, gs], op0=ALU.mult, op1=ALU.mult)
                P.store("sp", k.SO[t0:t0 + 128, :], so[:], R=[so])


def ph_merge(k, l, skip_ctx):
    with Phase(k.nc, "mrg%d" % l) as P:
        identb = P.sb([128, 128], BF16)
        P.dma("pool", identb[:], k.ident, W=[identb])
        ms = P.sb([128, 9, NCH, 2])
        P.dma("sp", ms[:].rearrange("p a c v -> p (a c v)"), k.MODS, W=[ms])
        wna = P.sb([128, 8, D], BF16)
        wso = P.sb([128, 16, D], BF16)
        wou = P.sb([128, 8, D], BF16)
        wnb, wsb, wob = Buf(), Buf(), Buf()
        for c in range(8):
            P.dma("pool", wna[:, c, :], k.na_w_o[l].rearrange("(c p) f -> p c f", p=128)[:, c, :], W=[wnb])
        for c in range(16):
            P.dma("pool", wso[:, c, :], k.ssd_w_o[l].rearrange("(c p) f -> p c f", p=128)[:, c, :], W=[wsb])
        for c in range(8):
            P.dma("pool", wou[:, c, :], k.w_out[l].rearrange("(c p) f -> p c f", p=128)[:, c, :], W=[wob])
        xts = Rot([P.sb([128, NCH, 512]) for _ in range(2)])
        gts = Rot([P.sb([128, 16, 512]) for _ in range(1)])
        ain = Rot([P.sb([128, 4, D], BF16) for _ in range(1)])
        sin_ = Rot([P.sb([128, 4, 2048], BF16) for _ in range(1)])
        aof = Rot([P.sb([128, 8, 512], BF16) for _ in range(1)])
        sof = Rot([P.sb([128, 16, 512], BF16) for _ in range(1)])
        mts = Rot([P.sb([128, 8, 512], BF16) for _ in range(1)])
        t1s = Rot([P.sb([128, 512]) for _ in range(2)])
        t2s = Rot([P.sb([128, 512]) for _ in range(2)])
        ptr = Rot([P.ps([128, 1024], BF16) for _ in range(2)])
        pas = Rot([P.ps() for _ in range(2)])
        pss = Rot([P.ps() for _ in range(2)])
        pys = Rot([P.ps() for _ in range(2)])
        cp = Rot(["dve", "act"])
        for (t0, n, v) in (TILES[:8] if skip_ctx else TILES):
            nb = n // 128
            xt = xts.next()
            gt = gts.next()
            ai = ain.next()
            si = sin_.next()
            P.dma("sp", xt[:, :, 0:n], k.XR[:, :, t0:t0 + n].rearrange("c p t -> p c t"), W=[xt])
            P.dma("pool", gt[:, :, 0:n], k.G[:, :, t0:t0 + n].rearrange("c p t -> p c t"), W=[gt])
            P.dma("sp", ai[:, 0:nb, :], k.AO[t0:t0 + n, :].rearrange("(b p) f -> p b f", p=128), W=[ai])
            P.dma("sp", si[:, 0:nb, :], k.SO[t0:t0 + n, :].rearrange("(b p) f -> p b f", p=128), W=[si])
            af = aof.next()
            sf = sof.next()
            for b in range(nb):
                for (src, dstt, nchunk) in ((ai, af, 8), (si, sf, 16)):
                    for c8 in range(nchunk // 8):
                        pt = ptr.next()
                        for j in range(8):
                            c = c8 * 8 + j
                            P.S.add("pe", (lambda e, o=pt[:, j * 128:(j + 1) * 128], i=src[:, b, c * 128:(c + 1) * 128]:
                                           e.transpose(o, i, identb[:])), _bufs([src, identb]), _bufs([pt]))
                        dsta = dstt[:, c8 * 8:(c8 + 1) * 8, b * 128:(b + 1) * 128]
                        srca = pt[:, :].rearrange("p (c t) -> p c t", c=8)
                        if cp.next() == "dve":
                            P.op("dve", "tensor_copy", [pt], [dstt], out=dsta, in_=srca)
                        else:
                            P.op("act", "activation", [pt], [dstt], out=dsta, in_=srca, func=AF.Copy)
            mt = mts.next()
            for dc in range(NCH):
                pa = pas.next()
                for c in range(8):
                    P.mm(pa[:, 0:n], wna[:, c, dc * 128:(dc + 1) * 128], af[:, c, 0:n], c == 0, c == 7, [wnb, af], [pa])
                ps = pss.next()
                for c in range(16):
                    P.mm(ps[:, 0:n], wso[:, c, dc * 128:(dc + 1) * 128], sf[:, c, 0:n], c == 0, c == 15, [wsb, sf], [ps])
                t1 = t1s.next()
                t2 = t2s.next()
                P.op("dve", "tensor_tensor", [pa, gt], [t1], out=t1[:, 0:n], in0=pa[:, 0:n], in1=gt[:, dc, 0:n], op=ALU.mult)
                P.op("dve", "tensor_tensor", [ps, gt], [t2], out=t2[:, 0:n], in0=ps[:, 0:n], in1=gt[:, 8 + dc, 0:n],
                     op=ALU.mult)
                P.op("pool", "tensor_tensor", [t1, t2], [mt], out=mt[:, dc, 0:n], in0=t1[:, 0:n], in1=t2[:, 0:n], op=ALU.add)
            for dc in range(NCH):
                py = pys.next()
                for c in range(8):
                    P.mm(py[:, 0:n], wou[:, c, dc * 128:(dc + 1) * 128], mt[:, c, 0:n], c == 0, c == 7, [wob, mt], [py])
                P.op("dve", "scalar_tensor_tensor", [py, xt, ms], [xt], out=xt[:, dc, 0:n], in0=py[:, 0:n],
                     scalar=ms[:, 5, dc, v:v + 1], in1=xt[:, dc, 0:n], op0=ALU.mult, op1=ALU.add)
            P.store("sp", k.XR[:, :, t0:t0 + n].rearrange("c p t -> p c t"), xt[:, :, 0:n], R=[xt])


def build_program(cfg):
    nc = bass.Bass("TRN2", target_bir_lowering=False)
    k = K()
    k.nc = nc
    k.cfg = cfg
    dbg = cfg.get("debug", ())

    def inp(name, shape):
        return dram(nc, name, shape, F32, kind="ExternalInput")

    k.dbg_copies = []

    def scr(name, shape, dt):
        if name in dbg and dt == BF16:
            a = dram(nc, name + "_bf", shape, dt)
            k.dbg_copies.append((a, dram(nc, name, shape, F32, kind="ExternalOutput")))
            return a
        return dram(nc, name, shape, dt, kind="ExternalOutput" if name in dbg else "Internal")

    k.x = inp("x", [NL, D])
    k.ctx = inp("ctx", [CT, D])
    k.cvec = inp("cvec", [128, NCH, 2])
    k.ident = inp("ident", [128, 128])
    k.ones = inp("ones", [128, 128])
    k.w_ada = inp("w_ada", [DEPTH, D, 9 * D])
    k.b_ada = inp("b_ada", [DEPTH, 128, 72])
    k.norm_ffn1 = inp("norm_ffn1", [DEPTH, 128, NCH])
    k.norm_mix = inp("norm_mix", [DEPTH, 128, NCH])
    k.norm_ffn2 = inp("norm_ffn2", [DEPTH, 128, NCH])
    k.ffn1_w_gate = inp("ffn1_w_gate", [DEPTH, D, DFF])
    k.ffn1_w_up = inp("ffn1_w_up", [DEPTH, D, DFF])
    k.ffn1_w_down = inp("ffn1_w_down", [DEPTH, DFF, D])
    k.ffn2_w_gate = inp("ffn2_w_gate", [DEPTH, D, DFF])
    k.ffn2_w_up = inp("ffn2_w_up", [DEPTH, D, DFF])
    k.ffn2_w_down = inp("ffn2_w_down", [DEPTH, DFF, D])
    k.w_in = inp("w_in", [DEPTH, D, NIN])
    k.q_norm = inp("q_norm", [DEPTH, 128, 1])
    k.k_norm = inp("k_norm", [DEPTH, 128, 1])
    k.blk = inp("blk", [128, 128])
    k.rot = inp("rot", [128, 128])
    k.cosT = inp("cosT", [128, NL])
    k.sinS = inp("sinS", [128, NL])
    k.conv_w = inp("conv_w", [DEPTH, 128, 24, 5])
    k.conv_b = inp("conv_b", [DEPTH, 128, 24])
    k.RB = inp("RB", [DEPTH, 16, 128, 1024])
    k.maskF = inp("maskF", [128, 1024])
    k.maskI = inp("maskI", [128, 1024])
    for nm in ("triF", "triB", "nm2F", "nm2B", "negI", "ntriF", "ntriB"):
        setattr(k, nm, inp(nm, [128, 128]))
    k.dt_bias_bc = inp("dt_bias_bc", [DEPTH, 128, 64])
    k.a_log_bc = inp("a_log_bc", [DEPTH, 128, 64])
    k.ssd_d_bc = inp("ssd_d_bc", [DEPTH, 128, 32])
    k.ssd_norm_bc = inp("ssd_norm_bc", [DEPTH, 128, 2048])
    k.na_w_o = inp("na_w_o", [DEPTH, D, D])
    k.ssd_w_o = inp("ssd_w_o", [DEPTH, 2048, D])
    k.w_out = inp("w_out", [DEPTH, D, D])
    k.out = dram(nc, "out", [NL, D], F32, kind="ExternalOutput")
    k.XR = scr("XR", [NCH, 128, NT], F32)
    k.MODS = scr("MODS", [128, 9 * NCH * 2], F32)
    k.HS = scr("HS", [NCH, 128, NT], BF16)
    k.Q = scr("Q", [8, 128, NT], BF16)
    k.KK = scr("KK", [8, 128, NT], BF16)
    k.V = scr("V", [NT, 16 * 65], BF16)
    k.Z = scr("Z", [NT, 2048], F32)
    k.DT = scr("DT", [NT, 64], F32)
    k.XBC = scr("XBC", [24, 128, NT], F32)
    k.G = scr("G", [16, 128, NT], F32)
    k.XS = scr("XS", [NT, 2048], F32)
    k.BF = scr("BF", [4, 128, NT], BF16)
    k.CF = scr("CF", [4, 128, NT], BF16)
    k.BT = scr("BT", [NT, 512], BF16)
    k.AO = scr("AO", [NT, D], BF16)
    k.YP = scr("YP", [NT, 2048], F32)
    k.SO = scr("SO", [NT, 2048], BF16)

    nl = cfg.get("n_layers", DEPTH)
    stop = cfg.get("stop", None)
    skip = cfg.get("skip", ())
    ph_load_x(k)
    done = False
    for l in range(nl):
        last = (l == DEPTH - 1)
        need_ctx = not last
        plist = [("ada", lambda: ph_adaln(k, l)), ("ffn1", lambda: ph_ffn(k, l, 1)),
                 ("hmix", lambda: ph_hmix(k, l)), ("qkv", lambda: ph_qkv(k, l)), ("zdt", lambda: ph_zdt(k, l)),
                 ("xbc", lambda: ph_proj_fm(k, l, "xbc", OX, 24, k.XBC, False)),
                 ("g", lambda: ph_proj_fm(k, l, "g", OG, 16, k.G, True)),
                 ("conv", lambda: ph_conv(k, l)), ("attn", lambda: ph_attn(k, l, need_ctx)),
                 ("ssda", lambda: ph_ssd_a(k, l, need_ctx)), ("ssdb", lambda: ph_ssd_b(k, l, need_ctx)),
                 ("mrg", lambda: ph_merge(k, l, last)),
                 ("ffn2", lambda: ph_ffn(k, l, 2, skip_ctx=last))]
        for name, fn in plist:
            if name in skip:
                continue
            fn()
            if stop == (l, name):
                done = True
                break
        if done:
            break
    if k.dbg_copies:
        with Phase(nc, "dbg") as P:
            for (src, dst) in k.dbg_copies:
                n0 = src.shape[0]
                step = max(1, n0 // 8)
                for i in range(0, n0, step):
                    P.store("pool", dst[i:i + step], src[i:i + step])
    ph_store_out(k)
    return nc


def _fm(vec, nchunk):
    s = vec.shape[:-1]
    return np.ascontiguousarray(np.swapaxes(vec.reshape(s + (nchunk, 128)), -1, -2))


def prepare_inputs(inputs, b):
    f = np.float32
    m = {}
    m["x"] = np.ascontiguousarray(inputs["x"][b], dtype=f)
    m["ctx"] = np.ascontiguousarray(inputs["ctx"][b], dtype=f)
    cv = np.stack([inputs["c"][b], inputs["c_ctx"]], axis=-1).astype(f)
    m["cvec"] = np.ascontiguousarray(cv.reshape(NCH, 128, 2).transpose(1, 0, 2))
    m["w_ada"] = np.ascontiguousarray(inputs["w_ada"], dtype=f)
    m["b_ada"] = _fm(np.asarray(inputs["b_ada"], dtype=f), 72)
    for nm in ("norm_ffn1", "norm_mix", "norm_ffn2"):
        m[nm] = _fm(np.asarray(inputs[nm], dtype=f), NCH)
    for nm in ("ffn1_w_gate", "ffn1_w_up", "ffn1_w_down", "ffn2_w_gate", "ffn2_w_up", "ffn2_w_down",
               "w_in", "na_w_o", "ssd_w_o", "w_out"):
        m[nm] = np.ascontiguousarray(inputs[nm], dtype=f)
    m["q_norm"] = np.ascontiguousarray(np.tile(np.asarray(inputs["q_norm"], dtype=f), (1, 2))[:, :, None])
    m["k_norm"] = np.ascontiguousarray(np.tile(np.asarray(inputs["k_norm"], dtype=f), (1, 2))[:, :, None])
    m.update(_consts())
    m["conv_w"] = np.ascontiguousarray(
        np.asarray(inputs["ssd_conv_w"], dtype=f).reshape(DEPTH, 5, 24, 128).transpose(0, 3, 2, 1))
    m["conv_b"] = _fm(np.asarray(inputs["ssd_conv_b"], dtype=f), 24)
    m["RB"] = _gather_rpb(np.asarray(inputs["na_rpb"], dtype=f))
    m["dt_bias_bc"] = np.ascontiguousarray(
        np.broadcast_to(np.asarray(inputs["ssd_dt_bias"], dtype=f).reshape(DEPTH, 1, 64), (DEPTH, 128, 64)))
    m["a_log_bc"] = np.ascontiguousarray(
        np.broadcast_to(np.asarray(inputs["ssd_a_log"], dtype=f).reshape(DEPTH, 1, 64), (DEPTH, 128, 64)))
    m["ssd_d_bc"] = np.ascontiguousarray(
        np.broadcast_to(np.asarray(inputs["ssd_d"], dtype=f).reshape(DEPTH, 1, 32), (DEPTH, 128, 32)))
    m["ssd_norm_bc"] = np.ascontiguousarray(
        np.broadcast_to(np.asarray(inputs["ssd_norm"], dtype=f).reshape(DEPTH, 1, 2048), (DEPTH, 128, 2048)))
    return m


_CONSTS = {}


def _rpb_index():
    a = np.arange(2)[:, None, None, None]
    kc = np.arange(64)[None, :, None, None]
    s_ = np.arange(16)[None, None, :, None]
    qc = np.arange(64)[None, None, None, :]
    dr = 7 - s_ + a + 0 * kc + 0 * qc
    dc = kc - qc + 0 * a + 0 * s_
    scol = np.clip(qc - 8, 0, 48)
    colvalid = (kc >= scol) & (kc < scol + 16) & (np.abs(dr) <= 7)
    return dr, dc, colvalid


def _gather_rpb(rpb):
    dr, dc, _ = _rpb_index()
    di = np.clip(dr + 7, 0, 14)
    ci = np.clip(dc + 15, 0, 30)
    out = rpb[:, :, di, ci]
    return np.ascontiguousarray(out.reshape(rpb.shape[0], rpb.shape[1], 128, 1024))


def _consts():
    if _CONSTS:
        return _CONSTS
    f = np.float32
    c = _CONSTS
    idx = np.arange(128)
    c["ident"] = np.eye(128, dtype=f)
    c["ones"] = np.ones((128, 128), dtype=f)
    c["blk"] = (idx[:, None] // 64 == idx[None, :] // 64).astype(f)
    partner = np.where((idx % 64) < 32, idx + 32, idx - 32)
    rot = np.zeros((128, 128), dtype=f)
    rot[partner, idx] = 1.0
    c["rot"] = rot
    t = np.arange(NL)
    row = (t // 64).astype(np.float64)
    col = (t % 64).astype(np.float64)
    inv = 10000.0 ** (-np.arange(16, dtype=np.float64) / 16)
    ang = np.concatenate([row[:, None] * inv, col[:, None] * inv], axis=-1)
    ang = ang.astype(f).astype(np.float64)
    j = idx % 64
    cosT = np.cos(ang[:, j % 32]).T
    sinT = np.sin(ang[:, j % 32]).T
    sign = np.where(j < 32, -1.0, 1.0)[:, None]
    c["cosT"] = np.ascontiguousarray(cosT.astype(f))
    c["sinS"] = np.ascontiguousarray((sinT * sign).astype(f))
    dr, dc, colvalid = _rpb_index()
    c["maskF"] = np.ascontiguousarray(colvalid.astype(f).reshape(128, 1024))
    c["maskI"] = np.ascontiguousarray((colvalid & (dr >= -4) & (dr <= 3)).astype(f).reshape(128, 1024))
    kk = idx[:, None]
    ii = idx[None, :]
    c["triF"] = (kk <= ii).astype(f)
    c["triB"] = (kk >= ii).astype(f)
    c["nm2F"] = (ii < kk).astype(f)
    c["nm2B"] = (ii > kk).astype(f)
    c["negI"] = (-BIG * np.eye(128)).astype(f)
    c["ntriF"] = -c["triF"]
    c["ntriB"] = -c["triB"]
    return c


_CACHE = {}


def run(inputs, cfg, n_cores=8):
    key = repr(sorted(cfg.items()))
    if key not in _CACHE:
        _CACHE[key] = build_program(cfg)
    nc = _CACHE[key]
    shared = None
    in_maps = []
    base = None
    for core in range(n_cores):
        m = prepare_inputs(inputs, core % 4)
        if base is None:
            base = m
        else:
            for kk in m:
                if kk not in ("x", "ctx", "cvec"):
                    m[kk] = base[kk]
        in_maps.append(m)
    res = run_bass_kernel_spmd(nc, in_maps, core_ids=list(range(n_cores)))
    return res


def kernel(**inputs):
    inputs = {k_: np.asarray(v) for k_, v in inputs.items()}
    res = run(inputs, {})
    out = np.stack([res.results[b]["out"] for b in range(4)], axis=0)
    return out.astype(np.float32)
```

```python
import contextlib
import numpy as np
import concourse.bass as bass
import concourse.mybir as mybir
from concourse.bass_utils import run_bass_kernel_spmd

F32 = mybir.dt.float32
BF16 = mybir.dt.bfloat16
ALU = mybir.AluOpType
AF = mybir.ActivationFunctionType

SEM_ROT = 30000
N_DMA_SEMS = 20


class Buf:
    __slots__ = ("name", "last_w", "readers")

    def __init__(self, name=""):
        self.name = name
        self.last_w = None
        self.readers = []


class Op:
    __slots__ = ("eng", "fn", "deps", "ndma", "ticket", "sem", "semval", "signal", "idx")


class Sched:
    ENGS = ("pe", "act", "dve", "pool", "sp")
    uid = 0

    def __init__(self, nc):
        Sched.uid += 1
        self.nc = nc
        self.ops = []
        self.dma_rr = {e: 0 for e in self.ENGS}
        self.dma_last = {}

    def add(self, eng, fn, reads=(), writes=(), ndma=0):
        op = Op()
        op.eng = eng
        op.fn = fn
        op.ndma = ndma
        op.signal = False
        op.ticket = None
        op.sem = None
        op.semval = None
        op.idx = len(self.ops)
        deps = {}
        for b in reads:
            w = b.last_w
            if w is not None:
                deps[w.idx] = w
        for b in writes:
            w = b.last_w
            if w is not None and (w.eng != eng or ndma or w.ndma):
                deps[w.idx] = w
            for r in b.readers:
                if r.eng != eng or ndma or r.ndma:
                    deps[r.idx] = r
        if eng == "pe":
            deps = {k: v for k, v in deps.items() if v.eng != "pe" or v.ndma}
        if ndma:
            slot = self.dma_rr[eng]
            self.dma_rr[eng] = (slot + 1) % N_DMA_SEMS
            prev = self.dma_last.get((eng, slot))
            if prev is not None:
                deps[prev.idx] = prev
            self.dma_last[(eng, slot)] = op
            op.sem = (eng, slot)
        for b in reads:
            b.readers.append(op)
        for b in writes:
            b.last_w = op
            b.readers = []
        op.deps = list(deps.values())
        for d in op.deps:
            d.signal = True
        self.ops.append(op)
        return op

    def emit(self):
        nc = self.nc
        ops = self.ops
        counts = {e: 0 for e in self.ENGS}
        dma_tot = {}
        for op in ops:
            if op.ndma:
                tot = dma_tot.get(op.sem, 0) + 16 * op.ndma
                dma_tot[op.sem] = tot
                op.semval = tot
            elif op.signal:
                counts[op.eng] += 1
                op.ticket = counts[op.eng]
        nsem_eng = {e: max(1, -(-counts[e] // SEM_ROT)) for e in self.ENGS}
        all_sems = []
        with contextlib.ExitStack() as st:
            tick_sems = {}
            for e, n in nsem_eng.items():
                tick_sems[e] = [nc.alloc_semaphore(name="tk%d_%s_%d" % (Sched.uid, e, i)) for i in range(n)]
                all_sems += tick_sems[e]
            dma_sems = {}
            for key in dma_tot:
                dma_sems[key] = nc.alloc_semaphore(name="dm%d_%s_%d" % (Sched.uid, key[0], key[1]))
                all_sems.append(dma_sems[key])
            self.all_sems = all_sems
            block = st.enter_context(nc.Block())

            def tick(op):
                t = op.ticket - 1
                return tick_sems[op.eng][t // SEM_ROT], (t % SEM_ROT) + 1, t // SEM_ROT

            per_eng = {e: [] for e in self.ENGS}
            for op in ops:
                per_eng[op.eng].append(op)

            def run_engine(ename, eng):
                waited = {}
                for op in per_eng[ename]:
                    for d in op.deps:
                        if d.ndma:
                            key = ("d",) + d.sem
                            sem, val = dma_sems[d.sem], d.semval
                        else:
                            sem, val, r = tick(d)
                            key = ("t", d.eng, r)
                        if waited.get(key, 0) >= val:
                            continue
                        waited[key] = val
                        eng.wait_ge(sem, val)
                    ins = op.fn(eng)
                    if op.ndma:
                        ins.then_inc(dma_sems[op.sem], 16)
                    elif op.signal:
                        sem, _, _ = tick(op)
                        ins.then_inc(sem, 1)

            @block.tensor
            def _(e):
                run_engine("pe", e)

            @block.scalar
            def _(e):
                run_engine("act", e)

            @block.vector
            def _(e):
                run_engine("dve", e)

            @block.gpsimd
            def _(e):
                run_engine("pool", e)

            @block.sync
            def _(e):
                run_engine("sp", e)


class T:
    def __init__(self, t, name=""):
        self.t = t
        self.b = Buf(name)

    def __getitem__(self, k):
        return self.t[k]


class TS(T):
    def __init__(self, t, c0, w):
        self.t = t
        self.b = Buf()
        self.c0 = c0
        self.w = w

    def __getitem__(self, k):
        ps, cs = k
        a = self.c0 + (cs.start or 0)
        b = self.c0 + (self.w if cs.stop is None else cs.stop)
        return self.t[ps, a:b]


def _bufs(lst):
    return [x.b if isinstance(x, T) else x for x in lst]


class Rot:
    def __init__(self, items):
        self.items = items
        self.i = 0

    def next(self):
        x = self.items[self.i % len(self.items)]
        self.i += 1
        return x


class Phase:
    uid = 0

    def __init__(self, nc, name):
        self.nc = nc
        self.name = name
        self.S = Sched(nc)
        self.st = contextlib.ExitStack()
        self.stores = []

    def __enter__(self):
        return self

    def __exit__(self, et, ev, tb):
        if et is not None:
            self.st.close()
            return False
        if self.stores:
            self.S.add("sp", lambda e: e.nop(), reads=self.stores)
        with self.nc.named_scope(self.name):
            self.S.emit()
        self.st.close()
        self.nc.all_engine_barrier()
        self.nc.clear_and_free_semaphores(self.S.all_sems)
        self.nc.all_engine_barrier()
        return False

    def _nm(self, p):
        Phase.uid += 1
        return "%s_%s%d" % (self.name, p, Phase.uid)

    def sb(self, shape, dt=F32):
        nm = self._nm("s")
        return T(self.st.enter_context(self.nc.sbuf_tensor(nm, list(shape), dt)), nm)

    def ps(self, shape=(128, 512), dt=F32):
        nm = self._nm("p")
        return T(self.st.enter_context(self.nc.psum_tensor(nm, list(shape), dt)), nm)

    def ps_slots(self, nbanks, w):
        out = []
        for _ in range(nbanks):
            t = self.ps().t
            out += [TS(t, c0, w) for c0 in range(0, 512, w)]
        return out

    def op(self, eng, meth, R, W, *a, **kw):
        return self.S.add(eng, lambda e: getattr(e, meth)(*a, **kw), _bufs(R), _bufs(W))

    def mm(self, out, lhsT, rhs, start, stop, R, W):
        return self.S.add("pe", lambda e: e.matmul(out, lhsT=lhsT, rhs=rhs, start=start, stop=stop),
                          _bufs(R), _bufs(W))

    def dma(self, eng, out, in_, R=(), W=()):
        return self.S.add(eng, lambda e: e.dma_start(out=out, in_=in_), _bufs(R), _bufs(W), ndma=1)

    def store(self, eng, out, in_, R=()):
        b = Buf("st")
        self.stores.append(b)
        return self.S.add(eng, lambda e: e.dma_start(out=out, in_=in_), _bufs(R), [b], ndma=1)


DEPTH = 4
D = 1024
NCH = 8
NL = 4096
CT = 256
NT = NL + CT
DFF = 2816
NFF = 22
NIN = 10304
OQ, OK_, OV, OZ, OX, ODT, OG = 0, 1024, 2048, 3072, 5120, 8192, 8256
EPS = 1e-6
BIG = 32768.0
TILES = [(i * 512, 512, 0) for i in range(8)] + [(NL, CT, 1)]
CHUNKS_F = [NL, NL + 128] + [i * 128 for i in range(32)]
CHUNKS_B = [NL + 128, NL] + [i * 128 for i in range(31, -1, -1)]


class K:
    pass


def dram(nc, name, shape, dt, kind="Internal"):
    return nc.dram_tensor(name, list(shape), dt, kind=kind).ap()


def ph_load_x(k):
    with Phase(k.nc, "ldx") as P:
        ident = P.sb([128, 128]);
        P.dma("sp", ident[:], k.ident, W=[ident])
        xin = Rot([P.sb([128, 4, D]) for _ in range(2)])
        stg = Rot([P.sb([128, NCH, 512]) for _ in range(2)])
        pss = Rot([P.ps() for _ in range(4)])
        cp = Rot(["dve", "act"])
        for (t0, n, v) in TILES:
            nb = n // 128
            xi = xin.next()
            src = k.x[t0:t0 + n, :] if v == 0 else k.ctx[:, :]
            P.dma("sp", xi[:, 0:nb, :], src.rearrange("(b p) d -> p b d", p=128), W=[xi])
            sg = stg.next()
            for b in range(nb):
                for half in range(2):
                    ps = pss.next()
                    for c4 in range(4):
                        c = half * 4 + c4
                        P.mm(ps[:, c4 * 128:(c4 + 1) * 128], xi[:, b, c * 128:(c + 1) * 128], ident[:],
                             True, True, [xi, ident], [ps])
                    e = cp.next()
                    dst = sg[:, half * 4:(half + 1) * 4, b * 128:(b + 1) * 128]
                    srcp = ps[:, :].rearrange("p (c t) -> p c t", c=4)
                    if e == "dve":
                        P.op("dve", "tensor_copy", [ps], [sg], out=dst, in_=srcp)
                    else:
                        P.op("act", "activation", [ps], [sg], out=dst, in_=srcp, func=AF.Copy)
            P.store("pool", k.XR[:, :, t0:t0 + n].rearrange("c p t -> p c t"), sg[:, :, 0:n], R=[sg])


def ph_store_out(k):
    with Phase(k.nc, "sto") as P:
        ident = P.sb([128, 128])
        P.dma("sp", ident[:], k.ident, W=[ident])
        xin = Rot([P.sb([128, NCH, 512]) for _ in range(2)])
        stg = Rot([P.sb([128, 4, D]) for _ in range(2)])
        pss = Rot([P.ps() for _ in range(4)])
        cp = Rot(["dve", "act"])
        for (t0, n, v) in TILES[:8]:
            xi = xin.next()
            P.dma("sp", xi[:], k.XR[:, :, t0:t0 + n].rearrange("c p t -> p c t"), W=[xi])
            sg = stg.next()
            for b in range(4):
                for half in range(2):
                    ps = pss.next()
                    for c4 in range(4):
                        c = half * 4 + c4
                        P.mm(ps[:, c4 * 128:(c4 + 1) * 128], xi[:, c, b * 128:(b + 1) * 128], ident[:],
                             True, True, [xi, ident], [ps])
                    e = cp.next()
                    dst = sg[:, b, half * 512:(half + 1) * 512]
                    if e == "dve":
                        P.op("dve", "tensor_copy", [ps], [sg], out=dst, in_=ps[:, :])
                    else:
                        P.op("act", "activation", [ps], [sg], out=dst, in_=ps[:, :], func=AF.Copy)
            P.store("pool", k.out[t0:t0 + n, :].rearrange("(b p) d -> p b d", p=128), sg[:], R=[sg])


def ph_adaln(k, l):
    with Phase(k.nc, "ada%d" % l) as P:
        cv = P.sb([128, NCH, 2])
        sv = P.sb([128, NCH, 2])
        P.dma("sp", cv[:], k.cvec, W=[cv])
        P.op("act", "activation", [cv], [sv], out=sv[:], in_=cv[:], func=AF.Silu)
        bada = P.sb([128, 72])
        P.dma("sp", bada[:], k.b_ada[l], W=[bada])
        gn = P.sb([128, 3, NCH])
        P.dma("sp", gn[:, 0, :], k.norm_ffn1[l], W=[gn])
        P.dma("sp", gn[:, 1, :], k.norm_mix[l], W=[gn])
        P.dma("sp", gn[:, 2, :], k.norm_ffn2[l], W=[gn])
        wb = Rot([P.sb([128, NCH, 512]) for _ in range(3)])
        pm = P.ps([128, 144])
        wsrc = k.w_ada[l].rearrange("(c p) n -> p c n", p=128)
        for blk in range(18):
            w = wb.next()
            P.dma("sp", w[:], wsrc[:, :, blk * 512:(blk + 1) * 512], W=[w])
            for s in range(4):
                ch = blk * 4 + s
                for kc in range(NCH):
                    P.mm(pm[:, ch * 2:ch * 2 + 2], w[:, kc, s * 128:(s + 1) * 128], sv[:, kc, :],
                         kc == 0, kc == NCH - 1, [w, sv], [pm])
        mod = P.sb([128, 72, 2])
        P.op("dve", "tensor_tensor", [pm, bada], [mod], out=mod[:],
             in0=pm[:, :].rearrange("p (c v) -> p c v", v=2),
             in1=bada[:, :].unsqueeze(2).to_broadcast([128, 72, 2]), op=ALU.add)
        ms = P.sb([128, 9, NCH, 2])
        for i, (gi, scale_idx, shift_idx, gate_idx, gmul) in enumerate(
                [(0, 1, 0, 2, 0.5), (1, 4, 3, 5, 1.0), (2, 7, 6, 8, 0.5)]):
            tmp = P.sb([128, NCH, 2])
            P.op("dve", "tensor_scalar", [mod], [tmp], out=tmp[:], in0=mod[:, scale_idx * 8:scale_idx * 8 + 8, :],
                 scalar1=1.0, scalar2=32.0, op0=ALU.add, op1=ALU.mult)
            P.op("dve", "tensor_tensor", [tmp, gn], [ms], out=ms[:, 3 * i, :, :], in0=tmp[:],
                 in1=gn[:, gi, :].unsqueeze(2).to_broadcast([128, NCH, 2]), op=ALU.mult)
            P.op("dve", "tensor_copy", [mod], [ms], out=ms[:, 3 * i + 1, :, :],
                 in_=mod[:, shift_idx * 8:shift_idx * 8 + 8, :])
            P.op("dve", "tensor_scalar", [mod], [ms], out=ms[:, 3 * i + 2, :, :],
                 in0=mod[:, gate_idx * 8:gate_idx * 8 + 8, :], scalar1=gmul, scalar2=0.0, op0=ALU.mult, op1=ALU.add)
        P.store("sp", k.MODS, ms[:].rearrange("p a c v -> p (a c v)"), R=[ms])


def rsqrt_psum(P, C, ss, r, n, eps_eff, np_=128):
    sqv = C["sqv"].next()
    P.op("act", "activation", [ss, C["epsb"]], [sqv], out=sqv[0:np_, 0:n], in_=ss[0:np_, 0:n], func=AF.Sqrt,
         bias=C["epsb"][0:np_, 0:1], scale=1.0)
    P.op("dve", "reciprocal", [sqv], [r], out=r[0:np_, 0:n], in_=sqv[0:np_, 0:n])


def norm_mod(P, k, C, xt, n, v, slotA, ht):
    ss = C["ps_ss"].next()
    for c in range(NCH):
        sq = C["sq"].next()
        P.op("act", "activation", [xt], [sq], out=sq[:, 0:n], in_=xt[:, c, 0:n], func=AF.Square)
        P.mm(ss[:, 0:n], C["ones"][:], sq[:, 0:n], c == 0, c == NCH - 1, [C["ones"], sq], [ss])
    r = C["r"].next()
    rsqrt_psum(P, C, ss, r, n, float(D * EPS))
    ms = C["ms"]
    for c in range(NCH):
        tm = C["tm"].next()
        P.op("dve", "tensor_tensor", [xt, r], [tm], out=tm[:, 0:n], in0=xt[:, c, 0:n], in1=r[:, 0:n], op=ALU.mult)
        P.op("act", "activation", [tm, ms], [ht], out=ht[:, c, 0:n], in_=tm[:, 0:n], func=AF.Identity,
             scale=ms[:, slotA, c, v:v + 1], bias=ms[:, slotA + 1, c, v:v + 1])


def norm_ctx(P, k):
    C = {}
    C["ones"] = P.sb([128, 128], BF16)
    P.dma("pool", C["ones"][:], k.ones, W=[C["ones"]])
    C["ms"] = P.sb([128, 9, NCH, 2])
    P.dma("sp", C["ms"][:].rearrange("p a c v -> p (a c v)"), k.MODS, W=[C["ms"]])
    C["sq"] = Rot([P.sb([128, 512], BF16) for _ in range(2)])
    C["r"] = Rot([P.sb([128, 512]) for _ in range(1)])
    C["tm"] = Rot([P.sb([128, 512]) for _ in range(2)])
    C["ps_ss"] = Rot([P.ps() for _ in range(1)])
    C["sqv"] = C["tm"]
    C["epsb"] = P.sb([128, 1])
    P.op("dve", "memset", [], [C["epsb"]], C["epsb"][:], float(D * EPS))
    return C


def ph_ffn(k, l, sub, skip_ctx=False):
    wg_d, wu_d, wd_d = (k.ffn1_w_gate, k.ffn1_w_up, k.ffn1_w_down) if sub == 1 else \
        (k.ffn2_w_gate, k.ffn2_w_up, k.ffn2_w_down)
    slotA = 0 if sub == 1 else 6
    with Phase(k.nc, "ffn%d_%d" % (sub, l)) as P:
        C = norm_ctx(P, k)
        wg = P.sb([128, NCH, DFF], BF16)
        wu = P.sb([128, NCH, DFF], BF16)
        wd = P.sb([128, NFF, D], BF16)
        wgb = [Buf(), Buf()]
        wub = [Buf(), Buf()]
        wdb = [Buf(), Buf()]
        gs = wg_d[l].rearrange("(c p) f -> p c f", p=128)
        us = wu_d[l].rearrange("(c p) f -> p c f", p=128)
        ds = wd_d[l].rearrange("(f p) d -> p f d", p=128)
        HF = DFF // 2
        for hf in range(2):
            for c in range(NCH):
                P.dma("pool", wg[:, c, hf * HF:(hf + 1) * HF], gs[:, c, hf * HF:(hf + 1) * HF], W=[wgb[hf]])
            for c in range(NCH):
                P.dma("pool", wu[:, c, hf * HF:(hf + 1) * HF], us[:, c, hf * HF:(hf + 1) * HF], W=[wub[hf]])
        for hf in range(2):
            for f in range(11):
                ff = hf * 11 + f
                P.dma("pool", wd[:, ff, :], ds[:, ff, :], W=[wdb[hf]])
        xts = Rot([P.sb([128, NCH, 512]) for _ in range(2)])
        hts = Rot([P.sb([128, NCH, 512], BF16) for _ in range(1)])
        ats = Rot([P.sb([128, NFF, 512], BF16) for _ in range(1)])
        sgs = Rot([P.sb([128, 512]) for _ in range(2)])
        pgs = Rot([P.ps() for _ in range(2)])
        pus = Rot([P.ps() for _ in range(2)])
        pys = Rot([P.ps() for _ in range(2)])
        tiles = TILES[:8] if skip_ctx else TILES
        for (t0, n, v) in tiles:
            xt = xts.next()
            P.dma("sp", xt[:, :, 0:n], k.XR[:, :, t0:t0 + n].rearrange("c p t -> p c t"), W=[xt])
            ht = hts.next()
            norm_mod(P, k, C, xt, n, v, slotA, ht)
            at = ats.next()
            for f in range(NFF):
                hf = f // 11
                pg = pgs.next()
                pu = pus.next()
                for c in range(NCH):
                    P.mm(pg[:, 0:n], wg[:, c, f * 128:(f + 1) * 128], ht[:, c, 0:n], c == 0, c == NCH - 1,
                         [wgb[hf], ht], [pg])
                for c in range(NCH):
                    P.mm(pu[:, 0:n], wu[:, c, f * 128:(f + 1) * 128], ht[:, c, 0:n], c == 0, c == NCH - 1,
                         [wub[hf], ht], [pu])
                sg = sgs.next()
                P.op("act", "activation", [pg], [sg], out=sg[:, 0:n], in_=pg[:, 0:n], func=AF.Silu)
                P.op("dve", "tensor_tensor", [sg, pu], [at], out=at[:, f, 0:n], in0=sg[:, 0:n], in1=pu[:, 0:n],
                     op=ALU.mult)
            for dc in range(NCH):
                py = pys.next()
                for f in range(NFF):
                    P.mm(py[:, 0:n], wd[:, f, dc * 128:(dc + 1) * 128], at[:, f, 0:n], f == 0, f == NFF - 1,
                         [wdb[f // 11], at], [py])
                P.op("dve", "scalar_tensor_tensor", [py, xt, C["ms"]], [xt], out=xt[:, dc, 0:n], in0=py[:, 0:n],
                     scalar=C["ms"][:, slotA + 2, dc, v:v + 1], in1=xt[:, dc, 0:n], op0=ALU.mult, op1=ALU.add)
            P.store("pool", k.XR[:, :, t0:t0 + n].rearrange("c p t -> p c t"), xt[:, :, 0:n], R=[xt])


def ph_hmix(k, l):
    with Phase(k.nc, "hmix%d" % l) as P:
        C = norm_ctx(P, k)
        xts = Rot([P.sb([128, NCH, 512]) for _ in range(2)])
        hts = Rot([P.sb([128, NCH, 512], BF16) for _ in range(2)])
        for (t0, n, v) in TILES:
            xt = xts.next()
            P.dma("sp", xt[:, :, 0:n], k.XR[:, :, t0:t0 + n].rearrange("c p t -> p c t"), W=[xt])
            ht = hts.next()
            norm_mod(P, k, C, xt, n, v, 3, ht)
            P.store("pool", k.HS[:, :, t0:t0 + n].rearrange("c p t -> p c t"), ht[:, :, 0:n], R=[ht])


def load_w_cols(P, wd, l_ap, c0, ncols, splits=1):
    b = Buf()
    src = l_ap.rearrange("(c p) f -> p c f", p=128)
    for c in range(NCH):
        P.dma("pool", wd[:, c, 0:ncols], src[:, c, c0:c0 + ncols], W=[b])
    return b


def ph_qkv(k, l):
    with Phase(k.nc, "qkv%d" % l) as P:
        w = P.sb([128, NCH, 3072], BF16)
        wb = load_w_cols(P, w, k.w_in[l], 0, 3072)
        blk = P.sb([128, 128], BF16)
        P.dma("pool", blk[:], k.blk, W=[blk])
        rot = P.sb([128, 128])
        P.dma("sp", rot[:], k.rot, W=[rot])
        cosT = P.sb([128, NL])
        sinS = P.sb([128, NL])
        P.dma("sp", cosT[:], k.cosT, W=[cosT])
        P.dma("sp", sinS[:], k.sinS, W=[sinS])
        gq = P.sb([128, 2])
        g8 = P.sb([128, 2])
        P.dma("sp", gq[:, 0:1], k.q_norm[l], W=[gq])
        P.dma("sp", gq[:, 1:2], k.k_norm[l], W=[gq])
        P.op("dve", "tensor_scalar", [gq], [g8], out=g8[:], in0=gq[:], scalar1=8.0, scalar2=0.0, op0=ALU.mult,
             op1=ALU.add)
        C = {"tm": Rot([P.sb([128, 512]) for _ in range(4)]), "epsb": P.sb([128, 1])}
        C["sqv"] = C["tm"]
        P.op("dve", "memset", [], [C["epsb"]], C["epsb"][:], float(64 * EPS))
        hts = Rot([P.sb([128, NCH, 512], BF16) for _ in range(2)])
        qst = Rot([P.sb([128, 8, 512], BF16) for _ in range(2)])
        kst = Rot([P.sb([128, 8, 512], BF16) for _ in range(2)])
        vst = Rot([P.sb([128, 16, 65], BF16) for _ in range(2)])
        for vv in vst.items:
            P.op("pool", "memset", [], [vv], vv[:], 1.0)
        sqs = Rot([P.sb([128, 512], BF16) for _ in range(4)])
        rs = Rot([P.sb([128, 512]) for _ in range(4)])
        qns = Rot([P.sb([128, 512]) for _ in range(4)])
        t1s = Rot([P.sb([128, 512]) for _ in range(4)])
        t2s = Rot([P.sb([128, 512]) for _ in range(4)])
        pqs = Rot([P.ps() for _ in range(3)])
        pss = Rot([P.ps() for _ in range(2)])
        prs = Rot([P.ps() for _ in range(2)])
        pvs = Rot([P.ps() for _ in range(1)])
        for (t0, n, v) in TILES:
            ht = hts.next()
            P.dma("sp", ht[:, :, 0:n], k.HS[:, :, t0:t0 + n].rearrange("c p t -> p c t"), W=[ht])
            stg_q = qst.next()
            stg_k = kst.next()
            items = [(qi, hp) for qi in range(2) for hp in range(8)]
            st = {}

            def s1(it):
                qi, hp = it
                col = qi * 1024 + hp * 128
                pq = pqs.next()
                for c in range(NCH):
                    P.mm(pq[:, 0:n], w[:, c, col:col + 128], ht[:, c, 0:n], c == 0, c == NCH - 1, [wb, ht], [pq])
                sq = sqs.next()
                P.op("act", "activation", [pq], [sq], out=sq[:, 0:n], in_=pq[:, 0:n], func=AF.Square)
                st[it] = dict(pq=pq, sq=sq)

            def s2(it):
                qi, hp = it
                d_ = st[it]
                sg = stg_q if qi == 0 else stg_k
                ss = pss.next()
                P.mm(ss[:, 0:n], blk[:], d_["sq"][:, 0:n], True, True, [blk, d_["sq"]], [ss])
                r = rs.next()
                rsqrt_psum(P, C, ss, r, n, 0.0)
                qn = qns.next()
                P.op("dve", "scalar_tensor_tensor", [d_["pq"], r, g8], [qn], out=qn[:, 0:n], in0=d_["pq"][:, 0:n],
                     scalar=g8[:, qi:qi + 1], in1=r[:, 0:n], op0=ALU.mult, op1=ALU.mult)
                d_["qn"] = qn
                if v == 1:
                    P.op("act", "activation", [qn], [sg], out=sg[:, hp, 0:n], in_=qn[:, 0:n], func=AF.Copy)

            def s3(it):
                qi, hp = it
                d_ = st.pop(it)
                if v == 1:
                    return
                sg = stg_q if qi == 0 else stg_k
                qn = d_["qn"]
                pr = prs.next()
                P.mm(pr[:, 0:n], rot[:], qn[:, 0:n], True, True, [rot, qn], [pr])
                t1 = t1s.next()
                P.op("dve", "tensor_tensor", [qn, cosT], [t1], out=t1[:, 0:n], in0=qn[:, 0:n],
                     in1=cosT[:, t0:t0 + n], op=ALU.mult)
                t2 = t2s.next()
                P.op("dve", "tensor_tensor", [pr, sinS], [t2], out=t2[:, 0:n], in0=pr[:, 0:n],
                     in1=sinS[:, t0:t0 + n], op=ALU.mult)
                P.op("dve", "tensor_tensor", [t1, t2], [sg], out=sg[:, hp, 0:n], in0=t1[:, 0:n], in1=t2[:, 0:n],
                     op=ALU.add)

            for step in range(len(items) + 2):
                if step < len(items):
                    s1(items[step])
                if 1 <= step <= len(items):
                    s2(items[step - 1])
                if step >= 2:
                    s3(items[step - 2])
            P.store("pool", k.Q[:, :, t0:t0 + n].rearrange("c p t -> p c t"), stg_q[:, :, 0:n], R=[stg_q])
            P.store("pool", k.KK[:, :, t0:t0 + n].rearrange("c p t -> p c t"), stg_k[:, :, 0:n], R=[stg_k])
            for b in range(n // 128):
                vs = vst.next()
                for half in range(2):
                    pv = pvs.next()
                    for c in range(NCH):
                        P.mm(pv[:, :], ht[:, c, b * 128:(b + 1) * 128], w[:, c, 2048 + half * 512:2048 + (half + 1) * 512],
                             c == 0, c == NCH - 1, [wb, ht], [pv])
                    P.op("act", "activation", [pv], [vs], out=vs[:, half * 8:(half + 1) * 8, 0:64],
                         in_=pv[:, :].rearrange("p (h e) -> p h e", e=64), func=AF.Copy)
                P.store("pool", k.V[t0 + b * 128:t0 + (b + 1) * 128, :], vs[:].rearrange("p h e -> p (h e)"), R=[vs])


def ph_zdt(k, l):
    with Phase(k.nc, "zdt%d" % l) as P:
        w = P.sb([128, NCH, 2048 + 64], BF16)
        wb = load_w_cols(P, w, k.w_in[l], OZ, 2048)
        wb2 = Buf()
        src = k.w_in[l].rearrange("(c p) f -> p c f", p=128)
        P.dma("pool", w[:, :, 2048:2112], src[:, :, ODT:ODT + 64], W=[wb2])
        hts = Rot([P.sb([128, NCH, 512], BF16) for _ in range(2)])
        zst = Rot([P.sb([128, 2048]) for _ in range(2)])
        dst_ = Rot([P.sb([128, 64]) for _ in range(2)])
        pzs = Rot([P.ps() for _ in range(4)])
        pds = Rot([P.ps([128, 64]) for _ in range(2)])
        cp = Rot(["act", "dve"])
        for (t0, n, v) in TILES:
            ht = hts.next()
            P.dma("pool", ht[:, :, 0:n], k.HS[:, :, t0:t0 + n].rearrange("c p t -> p c t"), W=[ht])
            for b in range(n // 128):
                zs = zst.next()
                for q4 in range(4):
                    pz = pzs.next()
                    for c in range(NCH):
                        P.mm(pz[:, :], ht[:, c, b * 128:(b + 1) * 128], w[:, c, q4 * 512:(q4 + 1) * 512],
                             c == 0, c == NCH - 1, [wb, ht], [pz])
                    if cp.next() == "act":
                        P.op("act", "activation", [pz], [zs], out=zs[:, q4 * 512:(q4 + 1) * 512], in_=pz[:, :],
                             func=AF.Copy)
                    else:
                        P.op("dve", "tensor_copy", [pz], [zs], out=zs[:, q4 * 512:(q4 + 1) * 512], in_=pz[:, :])
                P.store("sp", k.Z[t0 + b * 128:t0 + (b + 1) * 128, :], zs[:], R=[zs])
                pd = pds.next()
                for c in range(NCH):
                    P.mm(pd[:, :], ht[:, c, b * 128:(b + 1) * 128], w[:, c, 2048:2112], c == 0, c == NCH - 1,
                         [wb2, ht], [pd])
                ds = dst_.next()
                P.op("dve", "tensor_copy", [pd], [ds], out=ds[:], in_=pd[:, :])
                P.store("sp", k.DT[t0 + b * 128:t0 + (b + 1) * 128, :], ds[:], R=[ds])


def ph_proj_fm(k, l, name, c0, nchunks, dst, sigmoid):
    with Phase(k.nc, "%s%d" % (name, l)) as P:
        w = P.sb([128, NCH, nchunks * 128], BF16)
        wb = load_w_cols(P, w, k.w_in[l], c0, nchunks * 128)
        hts = Rot([P.sb([128, NCH, 512], BF16) for _ in range(2)])
        stg = Rot([P.sb([128, 8, 512]) for _ in range(2)])
        pps = Rot([P.ps() for _ in range(4)])
        cp = Rot(["act", "dve"])
        for (t0, n, v) in TILES:
            ht = hts.next()
            P.dma("pool", ht[:, :, 0:n], k.HS[:, :, t0:t0 + n].rearrange("c p t -> p c t"), W=[ht])
            for g8 in range(nchunks // 8):
                sg = stg.next()
                for j in range(8):
                    ch = g8 * 8 + j
                    pp = pps.next()
                    for c in range(NCH):
                        P.mm(pp[:, 0:n], w[:, c, ch * 128:(ch + 1) * 128], ht[:, c, 0:n], c == 0, c == NCH - 1,
                             [wb, ht], [pp])
                    if sigmoid:
                        P.op("act", "activation", [pp], [sg], out=sg[:, j, 0:n], in_=pp[:, 0:n], func=AF.Sigmoid)
                    elif cp.next() == "act":
                        P.op("act", "activation", [pp], [sg], out=sg[:, j, 0:n], in_=pp[:, 0:n], func=AF.Copy)
                    else:
                        P.op("dve", "tensor_copy", [pp], [sg], out=sg[:, j, 0:n], in_=pp[:, 0:n])
                P.store("sp", dst[g8 * 8:(g8 + 1) * 8, :, t0:t0 + n].rearrange("c p t -> p c t"), sg[:, :, 0:n], R=[sg])


def ph_conv(k, l):
    with Phase(k.nc, "conv%d" % l) as P:
        ident = P.sb([128, 128])
        P.dma("sp", ident[:], k.ident, W=[ident])
        identb = P.sb([128, 128], BF16)
        P.dma("pool", identb[:], k.ident, W=[identb])
        cw = P.sb([128, 24, 5])
        cb = P.sb([128, 24])
        P.dma("sp", cw[:], k.conv_w[l], W=[cw])
        P.dma("sp", cb[:], k.conv_b[l], W=[cb])
        xins = Rot([P.sb([128, 8, 516]) for _ in range(2)])
        accs = Rot([P.sb([128, 512]) for _ in range(8)])
        us = Rot([P.sb([128, 512]) for _ in range(8)])
        xst = Rot([P.sb([128, 4, 2048]) for _ in range(1)])
        bfs = Rot([P.sb([128, 4, 512], BF16) for _ in range(2)])
        bts = Rot([P.sb([128, 4, 512], BF16) for _ in range(2)])
        pts = Rot([P.ps() for _ in range(3)])
        ptb = Rot([P.ps([128, 512], BF16) for _ in range(2)])
        tapeng = Rot(["dve"])
        cp = Rot(["dve", "act"])
        for (t0, n, v) in TILES:
            seg0, seg1 = (0, NL) if v == 0 else (NL, NT)
            nb = n // 128
            xs_t = xst.next()
            for g in range(3):
                xin = xins.next()
                lo = max(t0 - 2, seg0)
                hi = min(t0 + n + 2, seg1)
                if lo > t0 - 2:
                    P.op("pool", "memset", [], [xin], xin[:, :, 0:2], 0.0)
                if hi < t0 + n + 2:
                    P.op("pool", "memset", [], [xin], xin[:, :, n + 2:n + 4], 0.0)
                P.dma("pool", xin[:, :, lo - (t0 - 2):hi - (t0 - 2)],
                      k.XBC[g * 8:(g + 1) * 8, :, lo:hi].rearrange("c p t -> p c t"), W=[xin])
                if g == 2:
                    bf = bfs.next()
                    cf = bfs.next()
                    bt = bts.next()
                for quad in range(2):
                    ul = []
                    accl = [accs.next() for _ in range(4)]
                    for j4 in range(4):
                        j = quad * 4 + j4
                        ch = g * 8 + j
                        P.op("act", "activation", [xin, cw, cb], [accl[j4]], out=accl[j4][:, 0:n], in_=xin[:, j, 0:n],
                             func=AF.Identity, scale=cw[:, ch, 0:1], bias=cb[:, ch:ch + 1])
                    for kk in range(1, 5):
                        for j4 in range(4):
                            j = quad * 4 + j4
                            ch = g * 8 + j
                            P.op("dve", "scalar_tensor_tensor", [xin, cw, accl[j4]], [accl[j4]], out=accl[j4][:, 0:n],
                                 in0=xin[:, j, kk:kk + n], scalar=cw[:, ch, kk:kk + 1], in1=accl[j4][:, 0:n],
                                 op0=ALU.mult, op1=ALU.add)
                    for j4 in range(4):
                        acc = accl[j4]
                        if g < 2:
                            u = us.next()
                            P.op("act", "activation", [acc], [u], out=u[:, 0:n], in_=acc[:, 0:n], func=AF.Silu)
                            ul.append(u)
                        elif quad == 0:
                            P.op("act", "activation", [acc], [bf], out=bf[:, j4, 0:n], in_=acc[:, 0:n], func=AF.Silu)
                        else:
                            P.op("act", "activation", [acc], [cf], out=cf[:, j4, 0:n], in_=acc[:, 0:n], func=AF.Silu)
                    if g < 2:
                        q16 = g * 2 + quad
                        for b in range(nb):
                            pt = pts.next()
                            for j4 in range(4):
                                P.mm(pt[:, j4 * 128:(j4 + 1) * 128], ul[j4][:, b * 128:(b + 1) * 128], ident[:],
                                     True, True, [ul[j4], ident], [pt])
                            if cp.next() == "dve":
                                P.op("dve", "tensor_copy", [pt], [xs_t], out=xs_t[:, b, q16 * 512:(q16 + 1) * 512],
                                     in_=pt[:, :])
                            else:
                                P.op("act", "activation", [pt], [xs_t], out=xs_t[:, b, q16 * 512:(q16 + 1) * 512],
                                     in_=pt[:, :], func=AF.Copy)
                    elif quad == 0:
                        for b in range(nb):
                            pb = ptb.next()
                            for j4 in range(4):
                                P.S.add("pe", (lambda e, o=pb[:, j4 * 128:(j4 + 1) * 128], i=bf[:, j4, b * 128:(b + 1) * 128]:
                                               e.transpose(o, i, identb[:])), _bufs([bf, identb]), _bufs([pb]))
                            P.op("dve", "tensor_copy", [pb], [bt], out=bt[:, b, :], in_=pb[:, :])
                        P.store("sp", k.BF[:, :, t0:t0 + n].rearrange("c p t -> p c t"), bf[:, :, 0:n], R=[bf])
                        P.store("sp", k.BT[t0:t0 + n, :].rearrange("(b p) f -> p b f", p=128), bt[:, 0:nb, :], R=[bt])
                    else:
                        P.store("sp", k.CF[:, :, t0:t0 + n].rearrange("c p t -> p c t"), cf[:, :, 0:n], R=[cf])
            P.store("sp", k.XS[t0:t0 + n, :].rearrange("(b p) f -> p b f", p=128), xs_t[:, 0:nb, :], R=[xs_t])


def _qrange(j):
    qlo = 0 if j <= 3 else 2 * j - 4
    qhi = 63 if j >= 28 else 2 * j + 5
    return qlo, qhi


def ph_attn(k, l, need_ctx):
    with Phase(k.nc, "attn%d" % l) as P:
        mf = P.sb([128, 1024], BF16)
        mi = P.sb([128, 1024], BF16)
        P.dma("pool", mf[:], k.maskF, W=[mf])
        P.dma("pool", mi[:], k.maskI, W=[mi])
        kts = Rot([P.sb([128, NT], BF16) for _ in range(2)])
        qts = Rot([P.sb([128, NT], BF16) for _ in range(2)])
        vts = Rot([P.sb([128, 34, 2, 65], BF16) for _ in range(2)])
        aos = Rot([P.sb([128, 34, 128], BF16) for _ in range(2)])
        PT = P.sb([128, 32, 768], BF16)
        PTb = [Buf() for _ in range(32)]
        PC = P.sb([128, 2, NT], BF16)
        PCb = [Buf() for _ in range(9)]
        rbs = Rot([P.sb([128, 1024]) for _ in range(2)])
        ees = Rot([P.sb([128, 1024]) for _ in range(2)])
        efs = Rot([P.sb([128, 1024], BF16) for _ in range(2)])
        eis = Rot([P.sb([128, 1024], BF16) for _ in range(2)])
        exs = Rot([P.sb([128, 512]) for _ in range(3)])
        rcs = Rot([P.sb([128, 8]) for _ in range(2)])
        pss = Rot([P.ps() for _ in range(4)])
        pos = Rot([P.ps() for _ in range(3)])
        nqb = 34 if need_ctx else 32
        for hp in range(8):
            kt = kts.next()
            qt = qts.next()
            vt = vts.next()
            ao = aos.next()
            P.dma("sp", kt[:], k.KK[hp], W=[kt])
            P.dma("sp", qt[:], k.Q[hp], W=[qt])
            P.dma("sp", vt[:].rearrange("p b a e -> p b (a e)"),
                  k.V[:, hp * 130:(hp + 1) * 130].rearrange("(b p) f -> p b f", p=128), W=[vt])
            for a in range(2):
                h = hp * 2 + a
                pr = slice(a * 64, a * 64 + 64)
                rb = rbs.next()
                P.dma("sp", rb[:], k.RB[l, h], W=[rb])
                ee = ees.next()
                P.op("act", "activation", [rb], [ee], out=ee[:], in_=rb[:], func=AF.Exp)
                ef = efs.next()
                ei = eis.next()
                P.op("dve", "tensor_tensor", [ee, mf], [ef], out=ef[:], in0=ee[:], in1=mf[:], op=ALU.mult)
                P.op("dve", "tensor_tensor", [ee, mi], [ei], out=ei[:], in0=ee[:], in1=mi[:], op=ALU.mult)
                for j in range(32):
                    qlo, qhi = _qrange(j)
                    nq = (qhi - qlo + 1) * 64
                    off = 0
                    while off < nq:
                        ln = min(512, nq - off)
                        ps = pss.next()
                        P.mm(ps[:, 0:ln], kt[pr, j * 128:(j + 1) * 128], qt[pr, qlo * 64 + off:qlo * 64 + off + ln],
                             True, True, [kt, qt], [ps])
                        ex = exs.next()
                        P.op("act", "activation", [ps], [ex], out=ex[:, 0:ln], in_=ps[:, 0:ln], func=AF.Exp, scale=0.125)
                        r0 = qlo + off // 64
                        r1 = r0 + ln // 64
                        qr = r0
                        while qr < r1:
                            full = (qr <= 3 or qr >= 61)
                            qe = qr
                            while qe < r1 and ((qe <= 3 or qe >= 61) == full):
                                qe += 1
                            tab = ef if full else ei
                            s0 = 7 - 2 * j + qr
                            cnt = (qe - qr) * 64
                            assert 0 <= s0 and s0 * 64 + cnt <= 1024
                            P.op("dve", "tensor_tensor", [ex, tab], [PTb[j]],
                                 out=PT[:, j, (qr - qlo) * 64:(qr - qlo) * 64 + cnt],
                                 in0=ex[:, (qr - r0) * 64:(qr - r0) * 64 + cnt], in1=tab[:, s0 * 64:s0 * 64 + cnt],
                                 op=ALU.mult)
                            qr = qe
                        off += ln
                for ti, (t0, n, v) in enumerate(TILES if need_ctx else TILES[:8]):
                    for cc in range(2):
                        ps = pss.next()
                        P.mm(ps[:, 0:n], kt[pr, NL + cc * 128:NL + (cc + 1) * 128], qt[pr, t0:t0 + n], True, True,
                             [kt, qt], [ps])
                        P.op("act", "activation", [ps], [PCb[ti]], out=PC[:, cc, t0:t0 + n], in_=ps[:, 0:n],
                             func=AF.Exp, scale=0.125)
                for g0 in range(0, nqb, 7):
                    grp = list(range(g0, min(nqb, g0 + 7)))
                    po = pos.next()
                    for si, i in enumerate(grp):
                        mms = []
                        if i < 32:
                            for jj in range(32):
                                qlo, qhi = _qrange(jj)
                                if qlo <= 2 * i and 2 * i + 1 <= qhi:
                                    mms.append((PT[:, jj, (2 * i - qlo) * 64:(2 * i - qlo) * 64 + 128],
                                                vt[:, jj, a, :], PTb[jj]))
                            ti = i // 4
                        else:
                            ti = 8
                        for cc in range(2):
                            mms.append((PC[:, cc, i * 128:(i + 1) * 128], vt[:, 32 + cc, a, :], PCb[ti]))
                        for mi_, (lh, rh, bb) in enumerate(mms):
                            P.mm(po[:, si * 65:(si + 1) * 65], lh, rh, mi_ == 0, mi_ == len(mms) - 1, [bb, vt], [po])
                    ng = len(grp)
                    rc = rcs.next()
                    pov = po[:, 0:ng * 65].rearrange("p (s e) -> p s e", e=65)
                    P.op("dve", "reciprocal", [po], [rc], out=rc[:, 0:ng], in_=pov[:, :, 64])
                    P.op("dve", "tensor_tensor", [po, rc], [ao], out=ao[:, g0:g0 + ng, a * 64:(a + 1) * 64],
                         in0=pov[:, :, 0:64], in1=rc[:, 0:ng].unsqueeze(2).to_broadcast([128, ng, 64]), op=ALU.mult)
            P.store("pool", k.AO[0:nqb * 128, hp * 128:(hp + 1) * 128].rearrange("(b p) f -> p b f", p=128),
                    ao[:, 0:nqb, :], R=[ao])


def ssd_consts(P, k, l):
    C = {}
    for nm, src in [("triF", k.triF), ("triB", k.triB), ("nm2F", k.nm2F), ("nm2B", k.nm2B), ("negI", k.negI),
                    ("ones", k.ones), ("ntriF", k.ntriF), ("ntriB", k.ntriB)]:
        C[nm] = P.sb([128, 128], BF16)
        P.dma("pool", C[nm][:], src, W=[C[nm]])
    C["dtb"] = P.sb([128, 64])
    C["alog"] = P.sb([128, 64])
    C["abc"] = P.sb([128, 64])
    C["one"] = P.sb([128, 1])
    P.op("dve", "memset", [], [C["one"]], C["one"][:], 1.0)
    P.dma("sp", C["dtb"][:], k.dt_bias_bc[l], W=[C["dtb"]])
    P.dma("sp", C["alog"][:], k.a_log_bc[l], W=[C["alog"]])
    P.op("act", "activation", [C["alog"]], [C["abc"]], out=C["abc"][:], in_=C["alog"][:], func=AF.Exp)
    P.op("dve", "tensor_scalar", [C["abc"]], [C["abc"]], out=C["abc"][:], in0=C["abc"][:], scalar1=-1.0, scalar2=0.0,
         op0=ALU.mult, op1=ALU.add)
    return C


NBLK = NT // 128


def ssd_prep_all(P, k, C, pbanks):
    W_ = NBLK * 64

    def big(dt_=F32):
        return P.sb([128, NBLK, 64], dt_)

    def bcv(t):
        return t[:, :].unsqueeze(1).to_broadcast([128, NBLK, 64])

    def flat(t):
        return t[:].rearrange("p c f -> p (c f)")

    tmp = big()
    P.dma("sp", tmp[:], k.DT.rearrange("(c p) f -> p c f", p=128), W=[tmp])
    P.op("dve", "tensor_tensor", [tmp, C["dtb"]], [tmp], out=tmp[:], in0=tmp[:], in1=bcv(C["dtb"]), op=ALU.add)
    P.op("act", "activation", [tmp], [tmp], out=flat(tmp), in_=flat(tmp), func=AF.Exp)
    dtv = big()
    P.op("act", "activation", [tmp, C["one"]], [dtv], out=flat(dtv), in_=flat(tmp), func=AF.Ln, bias=C["one"][:, 0:1],
         scale=1.0)
    P.op("dve", "tensor_tensor", [dtv, C["abc"]], [tmp], out=tmp[:], in0=dtv[:], in1=bcv(C["abc"]), op=ALU.mult)
    hi = big(BF16)
    lo = big(BF16)
    P.op("dve", "tensor_copy", [tmp], [hi], out=hi[:], in_=tmp[:])
    P.op("dve", "tensor_tensor", [tmp, hi], [lo], out=lo[:], in0=tmp[:], in1=hi[:], op=ALU.subtract)
    acum = big()
    alast = big()
    pb = Rot(pbanks)
    for d, tri in ((0, C["triF"]), (1, C["triB"])):
        for c0 in range(0, NBLK, 16):
            c1 = min(NBLK, c0 + 16)
            ncol = (c1 - c0) * 32
            ps = pb.next()
            P.mm(ps[:, 0:ncol], tri[:], hi[:, c0:c1, d * 32:(d + 1) * 32], True, False, [tri, hi], [ps])
            P.mm(ps[:, 0:ncol], tri[:], lo[:, c0:c1, d * 32:(d + 1) * 32], False, True, [tri, lo], [ps])
            P.op("dve", "tensor_copy", [ps], [acum], out=acum[:, c0:c1, d * 32:(d + 1) * 32],
                 in_=ps[:, 0:ncol].rearrange("p (c f) -> p c f", f=32))
    for o0 in range(0, W_, 512):
        o1 = min(W_, o0 + 512)
        ps = pb.next()
        P.mm(ps[:, 0:o1 - o0], C["ones"][:], flat(hi)[:, o0:o1], True, False, [C["ones"], hi], [ps])
        P.mm(ps[:, 0:o1 - o0], C["ones"][:], flat(lo)[:, o0:o1], False, True, [C["ones"], lo], [ps])
        P.op("dve", "tensor_copy", [ps], [alast], out=flat(alast)[:, o0:o1], in_=ps[:, 0:o1 - o0])
    eacum = big()
    P.op("act", "activation", [acum], [eacum], out=flat(eacum), in_=flat(acum), func=AF.Exp)
    sv = big()
    P.op("dve", "tensor_tensor", [alast, acum], [sv], out=sv[:], in0=alast[:], in1=acum[:], op=ALU.subtract)
    P.op("act", "activation", [sv], [sv], out=flat(sv), in_=flat(sv), func=AF.Exp)
    P.op("dve", "tensor_tensor", [sv, dtv], [sv], out=sv[:], in0=sv[:], in1=dtv[:], op=ALU.mult)
    ealast = big()
    P.op("act", "activation", [alast], [ealast], out=flat(ealast), in_=flat(alast), func=AF.Exp)
    return dict(dtv=dtv, hi=hi, eacum=eacum, sv=sv, ealast=ealast)


def _bc(ap32, n=32):
    return ap32.unsqueeze(2).to_broadcast([128, n, 64])


def ph_ssd_a(k, l, need_ctx):
    with Phase(k.nc, "ssda%d" % l) as P:
        C = ssd_consts(P, k, l)
        plp = [P.ps() for _ in range(3)]
        A = ssd_prep_all(P, k, C, plp[:2])
        plp = Rot(plp)
        xss = Rot([P.sb([128, 2048]) for _ in range(3)])
        bts = Rot([P.sb([128, 512], BF16) for _ in range(2)])
        bfs = Rot([P.sb([128, 4, 128], BF16) for _ in range(2)])
        cfs = Rot([P.sb([128, 4, 128], BF16) for _ in range(2)])
        xdf = Rot([P.sb([128, 2048], BF16) for _ in range(2)])
        xdb = Rot([P.sb([128, 2048], BF16) for _ in range(2)])
        xsf = Rot([P.sb([128, 2048], BF16) for _ in range(2)])
        yts = Rot([P.sb([128, 2048]) for _ in range(2)])
        lxs = Rot([P.sb([128, 512]) for _ in range(3)])
        wts = Rot([P.sb([128, 512], BF16) for _ in range(3)])
        tms = Rot([P.sb([128, 512]) for _ in range(4)])
        Sf = P.sb([128, 2048])
        Sfb = P.sb([128, 2048], BF16)
        Sg = [Buf() for _ in range(4)]
        Sgb = [Buf() for _ in range(4)]
        P.op("pool", "memset", [], Sg, Sf[:], 0.0)
        P.op("pool", "memset", [], Sgb, Sfb[:], 0.0)
        pcb = Rot([P.ps() for _ in range(1)])
        pys = Rot([P.ps() for _ in range(2)])
        pos = Rot([P.ps() for _ in range(2)])
        for t0 in CHUNKS_F:
            ci = t0 // 128
            is_ctx = t0 >= NL
            do_y = need_ctx or not is_ctx
            xs = xss.next()
            bt = bts.next()
            P.dma("pool", xs[:], k.XS[t0:t0 + 128, :], W=[xs])
            P.dma("sp", bt[:], k.BT[t0:t0 + 128, :], W=[bt])
            xf = xsf.next()
            P.op("dve", "tensor_tensor", [xs, A["sv"]], [xf], out=xf[:].rearrange("p (h e) -> p h e", e=64),
                 in0=xs[:].rearrange("p (h e) -> p h e", e=64), in1=_bc(A["sv"][:, ci, 0:32]), op=ALU.mult)
            if do_y:
                bf = bfs.next()
                cf = cfs.next()
                P.dma("sp", bf[:], k.BF[:, :, t0:t0 + 128].rearrange("c p t -> p c t"), W=[bf])
                P.dma("sp", cf[:], k.CF[:, :, t0:t0 + 128].rearrange("c p t -> p c t"), W=[cf])
                xd = [xdf.next(), xdb.next()]
                for d in range(2):
                    P.op("pool", "tensor_tensor", [xs, A["dtv"]], [xd[d]],
                         out=xd[d][:].rearrange("p (h e) -> p h e", e=64),
                         in0=xs[:].rearrange("p (h e) -> p h e", e=64),
                         in1=_bc(A["dtv"][:, ci, d * 32:(d + 1) * 32]), op=ALU.mult)
                cbp = pcb.next()
                for g in range(4):
                    P.mm(cbp[:, g * 128:(g + 1) * 128], bf[:, g, :], cf[:, g, :], True, True, [bf, cf], [cbp])
                yt = yts.next()
            def fill(g, quad):
                lp = plp.next()
                prs = [(quad * 2 + q2, d) for q2 in range(2) for d in range(2)]
                for s_, (hh, d) in enumerate(prs):
                    col = d * 32 + g * 8 + hh
                    tri, ntri, nm2 = (C["triF"], C["ntriF"], C["nm2F"]) if d == 0 else \
                        (C["triB"], C["ntriB"], C["nm2B"])
                    hb = A["hi"][:, ci, col:col + 1].to_broadcast([128, 128])
                    reg = lp[:, s_ * 128:(s_ + 1) * 128]
                    P.mm(reg, hb, tri[:], True, False, [A["hi"], tri], [lp])
                    P.mm(reg, ntri[:], hb, False, False, [A["hi"], ntri], [lp])
                    P.mm(reg, C["negI"][:], nm2[:], False, True, [C["negI"], nm2], [lp])
                lx = lxs.next()
                P.op("act", "activation", [lp], [lx], out=lx[:], in_=lp[:, :], func=AF.Exp)
                wt = wts.next()
                P.op("dve", "tensor_tensor", [lx, cbp], [wt], out=wt[:].rearrange("p (s i) -> p s i", s=4),
                     in0=lx[:].rearrange("p (s i) -> p s i", s=4),
                     in1=cbp[:, g * 128:(g + 1) * 128].unsqueeze(1).to_broadcast([128, 4, 128]), op=ALU.mult)
                return wt, prs

            def state_update(g):
                pst = pos.next()
                P.mm(pst[:, :], bt[:, g * 128:(g + 1) * 128], xf[:, g * 512:(g + 1) * 512], True, True, [bt, xf], [pst])
                tm2 = tms.next()
                P.op("pool", "tensor_tensor", [Sg[g], A["ealast"]], [tm2], out=tm2[:].rearrange("p (h e) -> p h e", e=64),
                     in0=Sf[:, g * 512:(g + 1) * 512].rearrange("p (h e) -> p h e", e=64),
                     in1=_bc(A["ealast"][:, ci, g * 8:(g + 1) * 8], 8), op=ALU.mult)
                P.op("dve", "tensor_tensor", [tm2, pst], [Sg[g]], out=Sf[:, g * 512:(g + 1) * 512], in0=tm2[:], in1=pst[:, :],
                     op=ALU.add)
                P.op("act", "activation", [Sg[g]], [Sgb[g]], out=Sfb[:, g * 512:(g + 1) * 512],
                     in_=Sf[:, g * 512:(g + 1) * 512], func=AF.Copy)

            if do_y:
                jobs = [(g, quad) for g in range(4) for quad in range(4)]
                filled = {}
                pyg = {}
                SKEW = 2
                for idx in range(len(jobs) + SKEW):
                    if idx < len(jobs):
                        filled[idx] = fill(*jobs[idx])
                    if idx >= SKEW:
                        g, quad = jobs[idx - SKEW]
                        wt, prs = filled.pop(idx - SKEW)
                        if quad == 0:
                            pyg[g] = pys.next()
                        py = pyg[g]
                        for s_, (hh, d) in enumerate(prs):
                            h = g * 8 + hh
                            P.mm(py[:, hh * 64:(hh + 1) * 64], wt[:, s_ * 128:(s_ + 1) * 128],
                                 xd[d][:, h * 64:(h + 1) * 64], d == 0, d == 1, [wt, xd[d]], [py])
                        if quad == 3:
                            po = pos.next()
                            P.mm(po[:, :], cf[:, g, :], Sfb[:, g * 512:(g + 1) * 512], True, True, [cf, Sgb[g]], [po])
                            tm = tms.next()
                            P.op("dve", "tensor_tensor", [po, A["eacum"]], [tm],
                                 out=tm[:].rearrange("p (h e) -> p h e", e=64),
                                 in0=po[:, :].rearrange("p (h e) -> p h e", e=64),
                                 in1=_bc(A["eacum"][:, ci, g * 8:(g + 1) * 8], 8), op=ALU.mult)
                            P.op("dve", "tensor_tensor", [tm, py], [yt], out=yt[:, g * 512:(g + 1) * 512], in0=tm[:],
                                 in1=py[:, :], op=ALU.add)
                            state_update(g)
            else:
                for g in range(4):
                    state_update(g)
            if do_y:
                P.store("sp", k.YP[t0:t0 + 128, :], yt[:], R=[yt])


def ph_ssd_b(k, l, need_ctx):
    with Phase(k.nc, "ssdb%d" % l) as P:
        C = ssd_consts(P, k, l)
        pos_banks = [P.ps() for _ in range(4)]
        A = ssd_prep_all(P, k, C, pos_banks[:2])
        epsb = P.sb([128, 1])
        P.op("dve", "memset", [], [epsb], epsb[:], float(EPS))
        dbc = P.sb([128, 32])
        nwb = P.sb([128, 2048])
        P.dma("sp", dbc[:], k.ssd_d_bc[l], W=[dbc])
        P.dma("sp", nwb[:], k.ssd_norm_bc[l], W=[nwb])
        xss = Rot([P.sb([128, 2048]) for _ in range(3)])
        bts = Rot([P.sb([128, 512], BF16) for _ in range(2)])
        cfs = Rot([P.sb([128, 4, 128], BF16) for _ in range(2)])
        yps = Rot([P.sb([128, 2048]) for _ in range(3)])
        zs = Rot([P.sb([128, 2048]) for _ in range(3)])
        xsb = Rot([P.sb([128, 2048], BF16) for _ in range(2)])
        sos = Rot([P.sb([128, 2048], BF16) for _ in range(2)])
        jk = P.sb([128, 512])
        tms = Rot([P.sb([128, 512]) for _ in range(3)])
        ssqs = Rot([P.sb([128, 4]) for _ in range(2)])
        rrs = Rot([P.sb([128, 4]) for _ in range(2)])
        Sb = P.sb([128, 2048])
        Sbb = P.sb([128, 2048], BF16)
        Sg = [Buf() for _ in range(4)]
        Sgb = [Buf() for _ in range(4)]
        P.op("pool", "memset", [], Sg, Sb[:], 0.0)
        P.op("pool", "memset", [], Sgb, Sbb[:], 0.0)
        pos = Rot(pos_banks)
        for t0 in CHUNKS_B:
            is_ctx = t0 >= NL
            do_y = need_ctx or not is_ctx
            xs = xss.next()
            bt = bts.next()
            P.dma("sp", xs[:], k.XS[t0:t0 + 128, :], W=[xs])
            P.dma("sp", bt[:], k.BT[t0:t0 + 128, :], W=[bt])
            ci = t0 // 128
            xb = xsb.next()
            P.op("dve", "tensor_tensor", [xs, A["sv"]], [xb], out=xb[:].rearrange("p (h e) -> p h e", e=64),
                 in0=xs[:].rearrange("p (h e) -> p h e", e=64), in1=_bc(A["sv"][:, ci, 32:64]), op=ALU.mult)
            if do_y:
                cf = cfs.next()
                yp = yps.next()
                z = zs.next()
                P.dma("sp", cf[:], k.CF[:, :, t0:t0 + 128].rearrange("c p t -> p c t"), W=[cf])
                P.dma("pool", yp[:], k.YP[t0:t0 + 128, :], W=[yp])
                P.dma("pool", z[:], k.Z[t0:t0 + 128, :], W=[z])
                P.op("act", "activation", [z], [z], out=z[:], in_=z[:], func=AF.Silu)
                ssq = ssqs.next()
                P.op("pool", "memset", [], [ssq], ssq[:], 0.0)
            for g in range(4):
                gs = slice(g * 512, (g + 1) * 512)
                if do_y:
                    po = pos.next()
                    P.mm(po[:, :], cf[:, g, :], Sbb[:, gs], True, True, [cf, Sgb[g]], [po])
                    tm = tms.next()
                    P.op("dve", "tensor_tensor", [po, A["eacum"]], [tm], out=tm[:].rearrange("p (h e) -> p h e", e=64),
                         in0=po[:, :].rearrange("p (h e) -> p h e", e=64),
                         in1=_bc(A["eacum"][:, ci, 32 + g * 8:32 + (g + 1) * 8], 8), op=ALU.mult)
                    P.op("dve", "tensor_tensor", [tm, yp], [yp], out=yp[:, gs], in0=tm[:], in1=yp[:, gs], op=ALU.add)
                    tm3 = tms.next()
                    P.op("pool", "tensor_tensor", [xs, dbc], [tm3], out=tm3[:].rearrange("p (h e) -> p h e", e=64),
                         in0=xs[:, gs].rearrange("p (h e) -> p h e", e=64), in1=_bc(dbc[:, g * 8:(g + 1) * 8], 8),
                         op=ALU.mult)
                    P.op("pool", "tensor_tensor", [tm3, yp], [yp], out=yp[:, gs], in0=tm3[:], in1=yp[:, gs], op=ALU.add)
                    P.op("dve", "tensor_tensor", [yp, z], [yp], out=yp[:, gs], in0=yp[:, gs], in1=z[:, gs], op=ALU.mult)
                    P.op("act", "activation", [yp, ssq], [jk, ssq], out=jk[:], in_=yp[:, gs], func=AF.Square,
                         accum_out=ssq[:, g:g + 1])
                pst = pos.next()
                P.mm(pst[:, :], bt[:, g * 128:(g + 1) * 128], xb[:, gs], True, True, [bt, xb], [pst])
                tm2 = tms.next()
                P.op("pool", "tensor_tensor", [Sg[g], A["ealast"]], [tm2], out=tm2[:].rearrange("p (h e) -> p h e", e=64),
                     in0=Sb[:, gs].rearrange("p (h e) -> p h e", e=64),
                     in1=_bc(A["ealast"][:, ci, 32 + g * 8:32 + (g + 1) * 8], 8), op=ALU.mult)
                P.op("dve", "tensor_tensor", [tm2, pst], [Sg[g]], out=Sb[:, gs], in0=tm2[:], in1=pst[:, :], op=ALU.add)
                P.op("act", "activation", [Sg[g]], [Sgb[g]], out=Sbb[:, gs], in_=Sb[:, gs], func=AF.Copy)
            if do_y:
                rr = rrs.next()
                P.op("act", "activation", [ssq, epsb], [rr], out=rr[:], in_=ssq[:], func=AF.Sqrt, bias=epsb[:, 0:1],
                     scale=1.0 / 512.0)
                P.op("dve", "reciprocal", [rr], [rr], out=rr[:], in_=rr[:])
                so = sos.next()
                for g in range(4):
                    gs = slice(g * 512, (g + 1) * 512)
                    P.op("dve", "scalar_tensor_tensor", [yp, rr, nwb], [so], out=so[:, gs], in0=yp[:, gs],
                         scalar=rr[:, g:g + 1], in1=nwb[:, gs], op0=ALU.mult, op1=ALU.mult)
                P.store("sp", k.SO[t0:t0 + 128, :], so[:], R=[so])


def ph_merge(k, l, skip_ctx):
    with Phase(k.nc, "mrg%d" % l) as P:
        identb = P.sb([128, 128], BF16)
        P.dma("pool", identb[:], k.ident, W=[identb])
        ms = P.sb([128, 9, NCH, 2])
        P.dma("sp", ms[:].rearrange("p a c v -> p (a c v)"), k.MODS, W=[ms])
        wna = P.sb([128, 8, D], BF16)
        wso = P.sb([128, 16, D], BF16)
        wou = P.sb([128, 8, D], BF16)
        wnb, wsb, wob = Buf(), Buf(), Buf()
        for c in range(8):
            P.dma("pool", wna[:, c, :], k.na_w_o[l].rearrange("(c p) f -> p c f", p=128)[:, c, :], W=[wnb])
        for c in range(16):
            P.dma("pool", wso[:, c, :], k.ssd_w_o[l].rearrange("(c p) f -> p c f", p=128)[:, c, :], W=[wsb])
        for c in range(8):
            P.dma("pool", wou[:, c, :], k.w_out[l].rearrange("(c p) f -> p c f", p=128)[:, c, :], W=[wob])
        xts = Rot([P.sb([128, NCH, 512]) for _ in range(2)])
        gts = Rot([P.sb([128, 16, 512]) for _ in range(1)])
        ain = Rot([P.sb([128, 4, D], BF16) for _ in range(1)])
        sin_ = Rot([P.sb([128, 4, 2048], BF16) for _ in range(1)])
        aof = Rot([P.sb([128, 8, 512], BF16) for _ in range(1)])
        sof = Rot([P.sb([128, 16, 512], BF16) for _ in range(1)])
        mts = Rot([P.sb([128, 8, 512], BF16) for _ in range(1)])
        t1s = Rot([P.sb([128, 512]) for _ in range(2)])
        t2s = Rot([P.sb([128, 512]) for _ in range(2)])
        ptr = Rot([P.ps([128, 1024], BF16) for _ in range(2)])
        pas = Rot([P.ps() for _ in range(2)])
        pss = Rot([P.ps() for _ in range(2)])
        pys = Rot([P.ps() for _ in range(2)])
        cp = Rot(["dve", "act"])
        for (t0, n, v) in (TILES[:8] if skip_ctx else TILES):
            nb = n // 128
            xt = xts.next()
            gt = gts.next()
            ai = ain.next()
            si = sin_.next()
            P.dma("sp", xt[:, :, 0:n], k.XR[:, :, t0:t0 + n].rearrange("c p t -> p c t"), W=[xt])
            P.dma("sp", gt[:, :, 0:n], k.G[:, :, t0:t0 + n].rearrange("c p t -> p c t"), W=[gt])
            P.dma("sp", ai[:, 0:nb, :], k.AO[t0:t0 + n, :].rearrange("(b p) f -> p b f", p=128), W=[ai])
            P.dma("sp", si[:, 0:nb, :], k.SO[t0:t0 + n, :].rearrange("(b p) f -> p b f", p=128), W=[si])
            af = aof.next()
            sf = sof.next()
            for b in range(nb):
                for (src, dstt, nchunk) in ((ai, af, 8), (si, sf, 16)):
                    for c8 in range(nchunk // 8):
                        pt = ptr.next()
                        for j in range(8):
                            c = c8 * 8 + j
                            P.S.add("pe", (lambda e, o=pt[:, j * 128:(j + 1) * 128], i=src[:, b, c * 128:(c + 1) * 128]:
                                           e.transpose(o, i, identb[:])), _bufs([src, identb]), _bufs([pt]))
                        dsta = dstt[:, c8 * 8:(c8 + 1) * 8, b * 128:(b + 1) * 128]
                        srca = pt[:, :].rearrange("p (c t) -> p c t", c=8)
                        if cp.next() == "dve":
                            P.op("dve", "tensor_copy", [pt], [dstt], out=dsta, in_=srca)
                        else:
                            P.op("act", "activation", [pt], [dstt], out=dsta, in_=srca, func=AF.Copy)
            mt = mts.next()
            for dc in range(NCH):
                pa = pas.next()
                for c in range(8):
                    P.mm(pa[:, 0:n], wna[:, c, dc * 128:(dc + 1) * 128], af[:, c, 0:n], c == 0, c == 7, [wnb, af], [pa])
                ps = pss.next()
                for c in range(16):
                    P.mm(ps[:, 0:n], wso[:, c, dc * 128:(dc + 1) * 128], sf[:, c, 0:n], c == 0, c == 15, [wsb, sf], [ps])
                t1 = t1s.next()
                t2 = t2s.next()
                P.op("dve", "tensor_tensor", [pa, gt], [t1], out=t1[:, 0:n], in0=pa[:, 0:n], in1=gt[:, dc, 0:n], op=ALU.mult)
                P.op("dve", "tensor_tensor", [ps, gt], [t2], out=t2[:, 0:n], in0=ps[:, 0:n], in1=gt[:, 8 + dc, 0:n],
                     op=ALU.mult)
                P.op("dve", "tensor_tensor", [t1, t2], [mt], out=mt[:, dc, 0:n], in0=t1[:, 0:n], in1=t2[:, 0:n], op=ALU.add)
            for dc in range(NCH):
                py = pys.next()
                for c in range(8):
                    P.mm(py[:, 0:n], wou[:, c, dc * 128:(dc + 1) * 128], mt[:, c, 0:n], c == 0, c == 7, [wob, mt], [py])
                P.op("dve", "scalar_tensor_tensor", [py, xt, ms], [xt], out=xt[:, dc, 0:n], in0=py[:, 0:n],
                     scalar=ms[:, 5, dc, v:v + 1], in1=xt[:, dc, 0:n], op0=ALU.mult, op1=ALU.add)
            P.store("pool", k.XR[:, :, t0:t0 + n].rearrange("c p t -> p c t"), xt[:, :, 0:n], R=[xt])


def build_program(cfg):
    nc = bass.Bass("TRN2", target_bir_lowering=False)
    k = K()
    k.nc = nc
    k.cfg = cfg
    dbg = cfg.get("debug", ())

    def inp(name, shape):
        return dram(nc, name, shape, F32, kind="ExternalInput")

    k.dbg_copies = []

    def scr(name, shape, dt):
        if name in dbg and dt == BF16:
            a = dram(nc, name + "_bf", shape, dt)
            k.dbg_copies.append((a, dram(nc, name, shape, F32, kind="ExternalOutput")))
            return a
        return dram(nc, name, shape, dt, kind="ExternalOutput" if name in dbg else "Internal")

    k.x = inp("x", [NL, D])
    k.ctx = inp("ctx", [CT, D])
    k.cvec = inp("cvec", [128, NCH, 2])
    k.ident = inp("ident", [128, 128])
    k.ones = inp("ones", [128, 128])
    k.w_ada = inp("w_ada", [DEPTH, D, 9 * D])
    k.b_ada = inp("b_ada", [DEPTH, 128, 72])
    k.norm_ffn1 = inp("norm_ffn1", [DEPTH, 128, NCH])
    k.norm_mix = inp("norm_mix", [DEPTH, 128, NCH])
    k.norm_ffn2 = inp("norm_ffn2", [DEPTH, 128, NCH])
    k.ffn1_w_gate = inp("ffn1_w_gate", [DEPTH, D, DFF])
    k.ffn1_w_up = inp("ffn1_w_up", [DEPTH, D, DFF])
    k.ffn1_w_down = inp("ffn1_w_down", [DEPTH, DFF, D])
    k.ffn2_w_gate = inp("ffn2_w_gate", [DEPTH, D, DFF])
    k.ffn2_w_up = inp("ffn2_w_up", [DEPTH, D, DFF])
    k.ffn2_w_down = inp("ffn2_w_down", [DEPTH, DFF, D])
    k.w_in = inp("w_in", [DEPTH, D, NIN])
    k.q_norm = inp("q_norm", [DEPTH, 128, 1])
    k.k_norm = inp("k_norm", [DEPTH, 128, 1])
    k.blk = inp("blk", [128, 128])
    k.rot = inp("rot", [128, 128])
    k.cosT = inp("cosT", [128, NL])
    k.sinS = inp("sinS", [128, NL])
    k.conv_w = inp("conv_w", [DEPTH, 128, 24, 5])
    k.conv_b = inp("conv_b", [DEPTH, 128, 24])
    k.RB = inp("RB", [DEPTH, 16, 128, 1024])
    k.maskF = inp("maskF", [128, 1024])
    k.maskI = inp("maskI", [128, 1024])
    for nm in ("triF", "triB", "nm2F", "nm2B", "negI", "ntriF", "ntriB"):
        setattr(k, nm, inp(nm, [128, 128]))
    k.dt_bias_bc = inp("dt_bias_bc", [DEPTH, 128, 64])
    k.a_log_bc = inp("a_log_bc", [DEPTH, 128, 64])
    k.ssd_d_bc = inp("ssd_d_bc", [DEPTH, 128, 32])
    k.ssd_norm_bc = inp("ssd_norm_bc", [DEPTH, 128, 2048])
    k.na_w_o = inp("na_w_o", [DEPTH, D, D])
    k.ssd_w_o = inp("ssd_w_o", [DEPTH, 2048, D])
    k.w_out = inp("w_out", [DEPTH, D, D])
    k.out = dram(nc, "out", [NL, D], F32, kind="ExternalOutput")
    k.XR = scr("XR", [NCH, 128, NT], F32)
    k.MODS = scr("MODS", [128, 9 * NCH * 2], F32)
    k.HS = scr("HS", [NCH, 128, NT], BF16)
    k.Q = scr("Q", [8, 128, NT], BF16)
    k.KK = scr("KK", [8, 128, NT], BF16)
    k.V = scr("V", [NT, 16 * 65], BF16)
    k.Z = scr("Z", [NT, 2048], F32)
    k.DT = scr("DT", [NT, 64], F32)
    k.XBC = scr("XBC", [24, 128, NT], F32)
    k.G = scr("G", [16, 128, NT], F32)
    k.XS = scr("XS", [NT, 2048], F32)
    k.BF = scr("BF", [4, 128, NT], BF16)
    k.CF = scr("CF", [4, 128, NT], BF16)
    k.BT = scr("BT", [NT, 512], BF16)
    k.AO = scr("AO", [NT, D], BF16)
    k.YP = scr("YP", [NT, 2048], F32)
    k.SO = scr("SO", [NT, 2048], BF16)

    nl = cfg.get("n_layers", DEPTH)
    stop = cfg.get("stop", None)
    skip = cfg.get("skip", ())
    ph_load_x(k)
    done = False
    for l in range(nl):
        last = (l == DEPTH - 1)
        need_ctx = not last
        plist = [("ada", lambda: ph_adaln(k, l)), ("ffn1", lambda: ph_ffn(k, l, 1)),
                 ("hmix", lambda: ph_hmix(k, l)), ("qkv", lambda: ph_qkv(k, l)), ("zdt", lambda: ph_zdt(k, l)),
                 ("xbc", lambda: ph_proj_fm(k, l, "xbc", OX, 24, k.XBC, False)),
                 ("g", lambda: ph_proj_fm(k, l, "g", OG, 16, k.G, True)),
                 ("conv", lambda: ph_conv(k, l)), ("attn", lambda: ph_attn(k, l, need_ctx)),
                 ("ssda", lambda: ph_ssd_a(k, l, need_ctx)), ("ssdb", lambda: ph_ssd_b(k, l, need_ctx)),
                 ("mrg", lambda: ph_merge(k, l, last)),
                 ("ffn2", lambda: ph_ffn(k, l, 2, skip_ctx=last))]
        for name, fn in plist:
            if name in skip:
                continue
            fn()
            if stop == (l, name):
                done = True
                break
        if done:
            break
    if k.dbg_copies:
        with Phase(nc, "dbg") as P:
            for (src, dst) in k.dbg_copies:
                n0 = src.shape[0]
                step = max(1, n0 // 8)
                for i in range(0, n0, step):
                    P.store("pool", dst[i:i + step], src[i:i + step])
    ph_store_out(k)
    return nc


def _fm(vec, nchunk):
    s = vec.shape[:-1]
    return np.ascontiguousarray(np.swapaxes(vec.reshape(s + (nchunk, 128)), -1, -2))


def prepare_inputs(inputs, b):
    f = np.float32
    m = {}
    m["x"] = np.ascontiguousarray(inputs["x"][b], dtype=f)
    m["ctx"] = np.ascontiguousarray(inputs["ctx"][b], dtype=f)
    cv = np.stack([inputs["c"][b], inputs["c_ctx"]], axis=-1).astype(f)
    m["cvec"] = np.ascontiguousarray(cv.reshape(NCH, 128, 2).transpose(1, 0, 2))
    m["w_ada"] = np.ascontiguousarray(inputs["w_ada"], dtype=f)
    m["b_ada"] = _fm(np.asarray(inputs["b_ada"], dtype=f), 72)
    for nm in ("norm_ffn1", "norm_mix", "norm_ffn2"):
        m[nm] = _fm(np.asarray(inputs[nm], dtype=f), NCH)
    for nm in ("ffn1_w_gate", "ffn1_w_up", "ffn1_w_down", "ffn2_w_gate", "ffn2_w_up", "ffn2_w_down",
               "w_in", "na_w_o", "ssd_w_o", "w_out"):
        m[nm] = np.ascontiguousarray(inputs[nm], dtype=f)
    m["q_norm"] = np.ascontiguousarray(np.tile(np.asarray(inputs["q_norm"], dtype=f), (1, 2))[:, :, None])
    m["k_norm"] = np.ascontiguousarray(np.tile(np.asarray(inputs["k_norm"], dtype=f), (1, 2))[:, :, None])
    m.update(_consts())
    m["conv_w"] = np.ascontiguousarray(
        np.asarray(inputs["ssd_conv_w"], dtype=f).reshape(DEPTH, 5, 24, 128).transpose(0, 3, 2, 1))
    m["conv_b"] = _fm(np.asarray(inputs["ssd_conv_b"], dtype=f), 24)
    m["RB"] = _gather_rpb(np.asarray(inputs["na_rpb"], dtype=f))
    m["dt_bias_bc"] = np.ascontiguousarray(
        np.broadcast_to(np.asarray(inputs["ssd_dt_bias"], dtype=f).reshape(DEPTH, 1, 64), (DEPTH, 128, 64)))
    m["a_log_bc"] = np.ascontiguousarray(
        np.broadcast_to(np.asarray(inputs["ssd_a_log"], dtype=f).reshape(DEPTH, 1, 64), (DEPTH, 128, 64)))
    m["ssd_d_bc"] = np.ascontiguousarray(
        np.broadcast_to(np.asarray(inputs["ssd_d"], dtype=f).reshape(DEPTH, 1, 32), (DEPTH, 128, 32)))
    m["ssd_norm_bc"] = np.ascontiguousarray(
        np.broadcast_to(np.asarray(inputs["ssd_norm"], dtype=f).reshape(DEPTH, 1, 2048), (DEPTH, 128, 2048)))
    return m


_CONSTS = {}


def _rpb_index():
    a = np.arange(2)[:, None, None, None]
    kc = np.arange(64)[None, :, None, None]
    s_ = np.arange(16)[None, None, :, None]
    qc = np.arange(64)[None, None, None, :]
    dr = 7 - s_ + a + 0 * kc + 0 * qc
    dc = kc - qc + 0 * a + 0 * s_
    scol = np.clip(qc - 8, 0, 48)
    colvalid = (kc >= scol) & (kc < scol + 16) & (np.abs(dr) <= 7)
    return dr, dc, colvalid


def _gather_rpb(rpb):
    dr, dc, _ = _rpb_index()
    di = np.clip(dr + 7, 0, 14)
    ci = np.clip(dc + 15, 0, 30)
    out = rpb[:, :, di, ci]
    return np.ascontiguousarray(out.reshape(rpb.shape[0], rpb.shape[1], 128, 1024))


def _consts():
    if _CONSTS:
        return _CONSTS
    f = np.float32
    c = _CONSTS
    idx = np.arange(128)
    c["ident"] = np.eye(128, dtype=f)
    c["ones"] = np.ones((128, 128), dtype=f)
    c["blk"] = (idx[:, None] // 64 == idx[None, :] // 64).astype(f)
    partner = np.where((idx % 64) < 32, idx + 32, idx - 32)
    rot = np.zeros((128, 128), dtype=f)
    rot[partner, idx] = 1.0
    c["rot"] = rot
    t = np.arange(NL)
    row = (t // 64).astype(np.float64)
    col = (t % 64).astype(np.float64)
    inv = 10000.0 ** (-np.arange(16, dtype=np.float64) / 16)
    ang = np.concatenate([row[:, None] * inv, col[:, None] * inv], axis=-1)
    ang = ang.astype(f).astype(np.float64)
    j = idx % 64
    cosT = np.cos(ang[:, j % 32]).T
    sinT = np.sin(ang[:, j % 32]).T
    sign = np.where(j < 32, -1.0, 1.0)[:, None]
    c["cosT"] = np.ascontiguousarray(cosT.astype(f))
    c["sinS"] = np.ascontiguousarray((sinT * sign).astype(f))
    dr, dc, colvalid = _rpb_index()
    c["maskF"] = np.ascontiguousarray(colvalid.astype(f).reshape(128, 1024))
    c["maskI"] = np.ascontiguousarray((colvalid & (dr >= -4) & (dr <= 3)).astype(f).reshape(128, 1024))
    kk = idx[:, None]
    ii = idx[None, :]
    c["triF"] = (kk <= ii).astype(f)
    c["triB"] = (kk >= ii).astype(f)
    c["nm2F"] = (ii < kk).astype(f)
    c["nm2B"] = (ii > kk).astype(f)
    c["negI"] = (-BIG * np.eye(128)).astype(f)
    c["ntriF"] = -c["triF"]
    c["ntriB"] = -c["triB"]
    return c


_CACHE = {}


def run(inputs, cfg, n_cores=8):
    key = repr(sorted(cfg.items()))
    if key not in _CACHE:
        _CACHE[key] = build_program(cfg)
    nc = _CACHE[key]
    shared = None
    in_maps = []
    base = None
    for core in range(n_cores):
        m = prepare_inputs(inputs, core % 4)
        if base is None:
            base = m
        else:
            for kk in m:
                if kk not in ("x", "ctx", "cvec"):
                    m[kk] = base[kk]
        in_maps.append(m)
    res = run_bass_kernel_spmd(nc, in_maps, core_ids=list(range(n_cores)))
    return res


def kernel(**inputs):
    inputs = {k_: np.asarray(v) for k_, v in inputs.items()}
    res = run(inputs, {})
    out = np.stack([res.results[b]["out"] for b in range(4)], axis=0)
    return out.astype(np.float32)
```

```python
import contextlib
import numpy as np
import concourse.bass as bass
import concourse.mybir as mybir
from concourse.bass_utils import run_bass_kernel_spmd

F32 = mybir.dt.float32
BF16 = mybir.dt.bfloat16
ALU = mybir.AluOpType
AF = mybir.ActivationFunctionType

SEM_ROT = 30000
N_DMA_SEMS = 20


class Buf:
    __slots__ = ("name", "last_w", "readers")

    def __init__(self, name=""):
        self.name = name
        self.last_w = None
        self.readers = []


class Op:
    __slots__ = ("eng", "fn", "deps", "ndma", "ticket", "sem", "semval", "signal", "idx")


class Sched:
    ENGS = ("pe", "act", "dve", "pool", "sp")
    uid = 0

    def __init__(self, nc):
        Sched.uid += 1
        self.nc = nc
        self.ops = []
        self.dma_rr = {e: 0 for e in self.ENGS}
        self.dma_last = {}

    def add(self, eng, fn, reads=(), writes=(), ndma=0):
        op = Op()
        op.eng = eng
        op.fn = fn
        op.ndma = ndma
        op.signal = False
        op.ticket = None
        op.sem = None
        op.semval = None
        op.idx = len(self.ops)
        deps = {}
        for b in reads:
            w = b.last_w
            if w is not None:
                deps[w.idx] = w
        for b in writes:
            w = b.last_w
            if w is not None and (w.eng != eng or ndma or w.ndma):
                deps[w.idx] = w
            for r in b.readers:
                if r.eng != eng or ndma or r.ndma:
                    deps[r.idx] = r
        if eng == "pe":
            deps = {k: v for k, v in deps.items() if v.eng != "pe" or v.ndma}
        if ndma:
            slot = self.dma_rr[eng]
            self.dma_rr[eng] = (slot + 1) % N_DMA_SEMS
            prev = self.dma_last.get((eng, slot))
            if prev is not None:
                deps[prev.idx] = prev
            self.dma_last[(eng, slot)] = op
            op.sem = (eng, slot)
        for b in reads:
            b.readers.append(op)
        for b in writes:
            b.last_w = op
            b.readers = []
        op.deps = list(deps.values())
        for d in op.deps:
            d.signal = True
        self.ops.append(op)
        return op

    def emit(self):
        nc = self.nc
        ops = self.ops
        counts = {e: 0 for e in self.ENGS}
        dma_tot = {}
        for op in ops:
            if op.ndma:
                tot = dma_tot.get(op.sem, 0) + 16 * op.ndma
                dma_tot[op.sem] = tot
                op.semval = tot
            elif op.signal:
                counts[op.eng] += 1
                op.ticket = counts[op.eng]
        nsem_eng = {e: max(1, -(-counts[e] // SEM_ROT)) for e in self.ENGS}
        all_sems = []
        with contextlib.ExitStack() as st:
            tick_sems = {}
            for e, n in nsem_eng.items():
                tick_sems[e] = [nc.alloc_semaphore(name="tk%d_%s_%d" % (Sched.uid, e, i)) for i in range(n)]
                all_sems += tick_sems[e]
            dma_sems = {}
            for key in dma_tot:
                dma_sems[key] = nc.alloc_semaphore(name="dm%d_%s_%d" % (Sched.uid, key[0], key[1]))
                all_sems.append(dma_sems[key])
            self.all_sems = all_sems
            block = st.enter_context(nc.Block())

            def tick(op):
                t = op.ticket - 1
                return tick_sems[op.eng][t // SEM_ROT], (t % SEM_ROT) + 1, t // SEM_ROT

            per_eng = {e: [] for e in self.ENGS}
            for op in ops:
                per_eng[op.eng].append(op)

            def run_engine(ename, eng):
                waited = {}
                for op in per_eng[ename]:
                    for d in op.deps:
                        if d.ndma:
                            key = ("d",) + d.sem
                            sem, val = dma_sems[d.sem], d.semval
                        else:
                            sem, val, r = tick(d)
                            key = ("t", d.eng, r)
                        if waited.get(key, 0) >= val:
                            continue
                        waited[key] = val
                        eng.wait_ge(sem, val)
                    ins = op.fn(eng)
                    if op.ndma:
                        ins.then_inc(dma_sems[op.sem], 16)
                    elif op.signal:
                        sem, _, _ = tick(op)
                        ins.then_inc(sem, 1)

            @block.tensor
            def _(e):
                run_engine("pe", e)

            @block.scalar
            def _(e):
                run_engine("act", e)

            @block.vector
            def _(e):
                run_engine("dve", e)

            @block.gpsimd
            def _(e):
                run_engine("pool", e)

            @block.sync
            def _(e):
                run_engine("sp", e)


class T:
    def __init__(self, t, name=""):
        self.t = t
        self.b = Buf(name)

    def __getitem__(self, k):
        return self.t[k]


class TS(T):
    def __init__(self, t, c0, w):
        self.t = t
        self.b = Buf()
        self.c0 = c0
        self.w = w

    def __getitem__(self, k):
        ps, cs = k
        a = self.c0 + (cs.start or 0)
        b = self.c0 + (self.w if cs.stop is None else cs.stop)
        return self.t[ps, a:b]


def _bufs(lst):
    return [x.b if isinstance(x, T) else x for x in lst]


class Rot:
    def __init__(self, items):
        self.items = items
        self.i = 0

    def next(self):
        x = self.items[self.i % len(self.items)]
        self.i += 1
        return x


class Phase:
    uid = 0

    def __init__(self, nc, name):
        self.nc = nc
        self.name = name
        self.S = Sched(nc)
        self.st = contextlib.ExitStack()
        self.stores = []

    def __enter__(self):
        return self

    def __exit__(self, et, ev, tb):
        if et is not None:
            self.st.close()
            return False
        if self.stores:
            self.S.add("sp", lambda e: e.nop(), reads=self.stores)
        with self.nc.named_scope(self.name):
            self.S.emit()
        self.st.close()
        self.nc.all_engine_barrier()
        self.nc.clear_and_free_semaphores(self.S.all_sems)
        self.nc.all_engine_barrier()
        return False

    def _nm(self, p):
        Phase.uid += 1
        return "%s_%s%d" % (self.name, p, Phase.uid)

    def sb(self, shape, dt=F32):
        nm = self._nm("s")
        return T(self.st.enter_context(self.nc.sbuf_tensor(nm, list(shape), dt)), nm)

    def ps(self, shape=(128, 512), dt=F32):
        nm = self._nm("p")
        return T(self.st.enter_context(self.nc.psum_tensor(nm, list(shape), dt)), nm)

    def ps_slots(self, nbanks, w):
        out = []
        for _ in range(nbanks):
            t = self.ps().t
            out += [TS(t, c0, w) for c0 in range(0, 512, w)]
        return out

    def op(self, eng, meth, R, W, *a, **kw):
        return self.S.add(eng, lambda e: getattr(e, meth)(*a, **kw), _bufs(R), _bufs(W))

    def mm(self, out, lhsT, rhs, start, stop, R, W):
        return self.S.add("pe", lambda e: e.matmul(out, lhsT=lhsT, rhs=rhs, start=start, stop=stop),
                          _bufs(R), _bufs(W))

    def dma(self, eng, out, in_, R=(), W=()):
        return self.S.add(eng, lambda e: e.dma_start(out=out, in_=in_), _bufs(R), _bufs(W), ndma=1)

    def store(self, eng, out, in_, R=()):
        b = Buf("st")
        self.stores.append(b)
        return self.S.add(eng, lambda e: e.dma_start(out=out, in_=in_), _bufs(R), [b], ndma=1)


DEPTH = 4
D = 1024
NCH = 8
NL = 4096
CT = 256
NT = NL + CT
DFF = 2816
NFF = 22
NIN = 10304
OQ, OK_, OV, OZ, OX, ODT, OG = 0, 1024, 2048, 3072, 5120, 8192, 8256
EPS = 1e-6
BIG = 32768.0
TILES = [(i * 512, 512, 0) for i in range(8)] + [(NL, CT, 1)]
CHUNKS_F = [NL, NL + 128] + [i * 128 for i in range(32)]
CHUNKS_B = [NL + 128, NL] + [i * 128 for i in range(31, -1, -1)]


class K:
    pass


def dram(nc, name, shape, dt, kind="Internal"):
    return nc.dram_tensor(name, list(shape), dt, kind=kind).ap()


def ph_load_x(k):
    with Phase(k.nc, "ldx") as P:
        ident = P.sb([128, 128]);
        P.dma("sp", ident[:], k.ident, W=[ident])
        xin = Rot([P.sb([128, 4, D]) for _ in range(2)])
        stg = Rot([P.sb([128, NCH, 512]) for _ in range(2)])
        pss = Rot([P.ps() for _ in range(4)])
        cp = Rot(["dve", "act"])
        for (t0, n, v) in TILES:
            nb = n // 128
            xi = xin.next()
            src = k.x[t0:t0 + n, :] if v == 0 else k.ctx[:, :]
            P.dma("sp", xi[:, 0:nb, :], src.rearrange("(b p) d -> p b d", p=128), W=[xi])
            sg = stg.next()
            for b in range(nb):
                for half in range(2):
                    ps = pss.next()
                    for c4 in range(4):
                        c = half * 4 + c4
                        P.mm(ps[:, c4 * 128:(c4 + 1) * 128], xi[:, b, c * 128:(c + 1) * 128], ident[:],
                             True, True, [xi, ident], [ps])
                    e = cp.next()
                    dst = sg[:, half * 4:(half + 1) * 4, b * 128:(b + 1) * 128]
                    srcp = ps[:, :].rearrange("p (c t) -> p c t", c=4)
                    if e == "dve":
                        P.op("dve", "tensor_copy", [ps], [sg], out=dst, in_=srcp)
                    else:
                        P.op("act", "activation", [ps], [sg], out=dst, in_=srcp, func=AF.Copy)
            P.store("pool", k.XR[:, :, t0:t0 + n].rearrange("c p t -> p c t"), sg[:, :, 0:n], R=[sg])


def ph_store_out(k):
    with Phase(k.nc, "sto") as P:
        ident = P.sb([128, 128])
        P.dma("sp", ident[:], k.ident, W=[ident])
        xin = Rot([P.sb([128, NCH, 512]) for _ in range(2)])
        stg = Rot([P.sb([128, 4, D]) for _ in range(2)])
        pss = Rot([P.ps() for _ in range(4)])
        cp = Rot(["dve", "act"])
        for (t0, n, v) in TILES[:8]:
            xi = xin.next()
            P.dma("sp", xi[:], k.XR[:, :, t0:t0 + n].rearrange("c p t -> p c t"), W=[xi])
            sg = stg.next()
            for b in range(4):
                for half in range(2):
                    ps = pss.next()
                    for c4 in range(4):
                        c = half * 4 + c4
                        P.mm(ps[:, c4 * 128:(c4 + 1) * 128], xi[:, c, b * 128:(b + 1) * 128], ident[:],
                             True, True, [xi, ident], [ps])
                    e = cp.next()
                    dst = sg[:, b, half * 512:(half + 1) * 512]
                    if e == "dve":
                        P.op("dve", "tensor_copy", [ps], [sg], out=dst, in_=ps[:, :])
                    else:
                        P.op("act", "activation", [ps], [sg], out=dst, in_=ps[:, :], func=AF.Copy)
            P.store("pool", k.out[t0:t0 + n, :].rearrange("(b p) d -> p b d", p=128), sg[:], R=[sg])


def ph_adaln(k, l):
    with Phase(k.nc, "ada%d" % l) as P:
        cv = P.sb([128, NCH, 2])
        sv = P.sb([128, NCH, 2])
        P.dma("sp", cv[:], k.cvec, W=[cv])
        P.op("act", "activation", [cv], [sv], out=sv[:], in_=cv[:], func=AF.Silu)
        bada = P.sb([128, 72])
        P.dma("sp", bada[:], k.b_ada[l], W=[bada])
        gn = P.sb([128, 3, NCH])
        P.dma("sp", gn[:, 0, :], k.norm_ffn1[l], W=[gn])
        P.dma("sp", gn[:, 1, :], k.norm_mix[l], W=[gn])
        P.dma("sp", gn[:, 2, :], k.norm_ffn2[l], W=[gn])
        wb = Rot([P.sb([128, NCH, 512]) for _ in range(3)])
        pm = P.ps([128, 144])
        wsrc = k.w_ada[l].rearrange("(c p) n -> p c n", p=128)
        for blk in range(18):
            w = wb.next()
            P.dma("sp", w[:], wsrc[:, :, blk * 512:(blk + 1) * 512], W=[w])
            for s in range(4):
                ch = blk * 4 + s
                for kc in range(NCH):
                    P.mm(pm[:, ch * 2:ch * 2 + 2], w[:, kc, s * 128:(s + 1) * 128], sv[:, kc, :],
                         kc == 0, kc == NCH - 1, [w, sv], [pm])
        mod = P.sb([128, 72, 2])
        P.op("dve", "tensor_tensor", [pm, bada], [mod], out=mod[:],
             in0=pm[:, :].rearrange("p (c v) -> p c v", v=2),
             in1=bada[:, :].unsqueeze(2).to_broadcast([128, 72, 2]), op=ALU.add)
        ms = P.sb([128, 9, NCH, 2])
        for i, (gi, scale_idx, shift_idx, gate_idx, gmul) in enumerate(
                [(0, 1, 0, 2, 0.5), (1, 4, 3, 5, 1.0), (2, 7, 6, 8, 0.5)]):
            tmp = P.sb([128, NCH, 2])
            P.op("dve", "tensor_scalar", [mod], [tmp], out=tmp[:], in0=mod[:, scale_idx * 8:scale_idx * 8 + 8, :],
                 scalar1=1.0, scalar2=32.0, op0=ALU.add, op1=ALU.mult)
            P.op("dve", "tensor_tensor", [tmp, gn], [ms], out=ms[:, 3 * i, :, :], in0=tmp[:],
                 in1=gn[:, gi, :].unsqueeze(2).to_broadcast([128, NCH, 2]), op=ALU.mult)
            P.op("dve", "tensor_copy", [mod], [ms], out=ms[:, 3 * i + 1, :, :],
                 in_=mod[:, shift_idx * 8:shift_idx * 8 + 8, :])
            P.op("dve", "tensor_scalar", [mod], [ms], out=ms[:, 3 * i + 2, :, :],
                 in0=mod[:, gate_idx * 8:gate_idx * 8 + 8, :], scalar1=gmul, scalar2=0.0, op0=ALU.mult, op1=ALU.add)
        P.store("sp", k.MODS, ms[:].rearrange("p a c v -> p (a c v)"), R=[ms])


def rsqrt_psum(P, C, ss, r, n, eps_eff, np_=128):
    sqv = C["sqv"].next()
    P.op("act", "activation", [ss, C["epsb"]], [sqv], out=sqv[0:np_, 0:n], in_=ss[0:np_, 0:n], func=AF.Sqrt,
         bias=C["epsb"][0:np_, 0:1], scale=1.0)
    P.op("dve", "reciprocal", [sqv], [r], out=r[0:np_, 0:n], in_=sqv[0:np_, 0:n])


def norm_mod(P, k, C, xt, n, v, slotA, ht):
    ss = C["ps_ss"].next()
    for c in range(NCH):
        sq = C["sq"].next()
        P.op("act", "activation", [xt], [sq], out=sq[:, 0:n], in_=xt[:, c, 0:n], func=AF.Square)
        P.mm(ss[:, 0:n], C["ones"][:], sq[:, 0:n], c == 0, c == NCH - 1, [C["ones"], sq], [ss])
    r = C["r"].next()
    rsqrt_psum(P, C, ss, r, n, float(D * EPS))
    ms = C["ms"]
    for c in range(NCH):
        tm = C["tm"].next()
        P.op("dve", "tensor_tensor", [xt, r], [tm], out=tm[:, 0:n], in0=xt[:, c, 0:n], in1=r[:, 0:n], op=ALU.mult)
        P.op("act", "activation", [tm, ms], [ht], out=ht[:, c, 0:n], in_=tm[:, 0:n], func=AF.Identity,
             scale=ms[:, slotA, c, v:v + 1], bias=ms[:, slotA + 1, c, v:v + 1])


def norm_ctx(P, k):
    C = {}
    C["ones"] = P.sb([128, 128], BF16)
    P.dma("pool", C["ones"][:], k.ones, W=[C["ones"]])
    C["ms"] = P.sb([128, 9, NCH, 2])
    P.dma("sp", C["ms"][:].rearrange("p a c v -> p (a c v)"), k.MODS, W=[C["ms"]])
    C["sq"] = Rot([P.sb([128, 512], BF16) for _ in range(2)])
    C["r"] = Rot([P.sb([128, 512]) for _ in range(1)])
    C["tm"] = Rot([P.sb([128, 512]) for _ in range(2)])
    C["ps_ss"] = Rot([P.ps() for _ in range(1)])
    C["sqv"] = C["tm"]
    C["epsb"] = P.sb([128, 1])
    P.op("dve", "memset", [], [C["epsb"]], C["epsb"][:], float(D * EPS))
    return C


def ph_ffn(k, l, sub, skip_ctx=False):
    wg_d, wu_d, wd_d = (k.ffn1_w_gate, k.ffn1_w_up, k.ffn1_w_down) if sub == 1 else \
        (k.ffn2_w_gate, k.ffn2_w_up, k.ffn2_w_down)
    slotA = 0 if sub == 1 else 6
    with Phase(k.nc, "ffn%d_%d" % (sub, l)) as P:
        C = norm_ctx(P, k)
        wg = P.sb([128, NCH, DFF], BF16)
        wu = P.sb([128, NCH, DFF], BF16)
        wd = P.sb([128, NFF, D], BF16)
        wgb = [Buf(), Buf()]
        wub = [Buf(), Buf()]
        wdb = [Buf(), Buf()]
        gs = wg_d[l].rearrange("(c p) f -> p c f", p=128)
        us = wu_d[l].rearrange("(c p) f -> p c f", p=128)
        ds = wd_d[l].rearrange("(f p) d -> p f d", p=128)
        HF = DFF // 2
        for hf in range(2):
            for c in range(NCH):
                P.dma("pool", wg[:, c, hf * HF:(hf + 1) * HF], gs[:, c, hf * HF:(hf + 1) * HF], W=[wgb[hf]])
            for c in range(NCH):
                P.dma("pool", wu[:, c, hf * HF:(hf + 1) * HF], us[:, c, hf * HF:(hf + 1) * HF], W=[wub[hf]])
        for hf in range(2):
            for f in range(11):
                ff = hf * 11 + f
                P.dma("pool", wd[:, ff, :], ds[:, ff, :], W=[wdb[hf]])
        xts = Rot([P.sb([128, NCH, 512]) for _ in range(2)])
        hts = Rot([P.sb([128, NCH, 512], BF16) for _ in range(1)])
        ats = Rot([P.sb([128, NFF, 512], BF16) for _ in range(1)])
        sgs = Rot([P.sb([128, 512]) for _ in range(2)])
        pgs = Rot([P.ps() for _ in range(2)])
        pus = Rot([P.ps() for _ in range(2)])
        pys = Rot([P.ps() for _ in range(2)])
        tiles = TILES[:8] if skip_ctx else TILES
        for (t0, n, v) in tiles:
            xt = xts.next()
            P.dma("sp", xt[:, :, 0:n], k.XR[:, :, t0:t0 + n].rearrange("c p t -> p c t"), W=[xt])
            ht = hts.next()
            norm_mod(P, k, C, xt, n, v, slotA, ht)
            at = ats.next()
            for f in range(NFF):
                hf = f // 11
                pg = pgs.next()
                pu = pus.next()
                for c in range(NCH):
                    P.mm(pg[:, 0:n], wg[:, c, f * 128:(f + 1) * 128], ht[:, c, 0:n], c == 0, c == NCH - 1,
                         [wgb[hf], ht], [pg])
                for c in range(NCH):
                    P.mm(pu[:, 0:n], wu[:, c, f * 128:(f + 1) * 128], ht[:, c, 0:n], c == 0, c == NCH - 1,
                         [wub[hf], ht], [pu])
                sg = sgs.next()
                P.op("act", "activation", [pg], [sg], out=sg[:, 0:n], in_=pg[:, 0:n], func=AF.Silu)
                P.op("dve", "tensor_tensor", [sg, pu], [at], out=at[:, f, 0:n], in0=sg[:, 0:n], in1=pu[:, 0:n],
                     op=ALU.mult)
            for dc in range(NCH):
                py = pys.next()
                for f in range(NFF):
                    P.mm(py[:, 0:n], wd[:, f, dc * 128:(dc + 1) * 128], at[:, f, 0:n], f == 0, f == NFF - 1,
                         [wdb[f // 11], at], [py])
                P.op("dve", "scalar_tensor_tensor", [py, xt, C["ms"]], [xt], out=xt[:, dc, 0:n], in0=py[:, 0:n],
                     scalar=C["ms"][:, slotA + 2, dc, v:v + 1], in1=xt[:, dc, 0:n], op0=ALU.mult, op1=ALU.add)
            P.store("pool", k.XR[:, :, t0:t0 + n].rearrange("c p t -> p c t"), xt[:, :, 0:n], R=[xt])


def ph_hmix(k, l):
    with Phase(k.nc, "hmix%d" % l) as P:
        C = norm_ctx(P, k)
        xts = Rot([P.sb([128, NCH, 512]) for _ in range(2)])
        hts = Rot([P.sb([128, NCH, 512], BF16) for _ in range(2)])
        for (t0, n, v) in TILES:
            xt = xts.next()
            P.dma("sp", xt[:, :, 0:n], k.XR[:, :, t0:t0 + n].rearrange("c p t -> p c t"), W=[xt])
            ht = hts.next()
            norm_mod(P, k, C, xt, n, v, 3, ht)
            P.store("pool", k.HS[:, :, t0:t0 + n].rearrange("c p t -> p c t"), ht[:, :, 0:n], R=[ht])


def load_w_cols(P, wd, l_ap, c0, ncols, splits=1):
    b = Buf()
    src = l_ap.rearrange("(c p) f -> p c f", p=128)
    for c in range(NCH):
        P.dma("pool", wd[:, c, 0:ncols], src[:, c, c0:c0 + ncols], W=[b])
    return b


def ph_qkv(k, l):
    with Phase(k.nc, "qkv%d" % l) as P:
        w = P.sb([128, NCH, 3072], BF16)
        wb = load_w_cols(P, w, k.w_in[l], 0, 3072)
        blk = P.sb([128, 128], BF16)
        P.dma("pool", blk[:], k.blk, W=[blk])
        rot = P.sb([128, 128])
        P.dma("sp", rot[:], k.rot, W=[rot])
        cosT = P.sb([128, NL])
        sinS = P.sb([128, NL])
        P.dma("sp", cosT[:], k.cosT, W=[cosT])
        P.dma("sp", sinS[:], k.sinS, W=[sinS])
        gq = P.sb([128, 2])
        g8 = P.sb([128, 2])
        P.dma("sp", gq[:, 0:1], k.q_norm[l], W=[gq])
        P.dma("sp", gq[:, 1:2], k.k_norm[l], W=[gq])
        P.op("dve", "tensor_scalar", [gq], [g8], out=g8[:], in0=gq[:], scalar1=8.0, scalar2=0.0, op0=ALU.mult,
             op1=ALU.add)
        C = {"tm": Rot([P.sb([128, 512]) for _ in range(4)]), "epsb": P.sb([128, 1])}
        C["sqv"] = C["tm"]
        P.op("dve", "memset", [], [C["epsb"]], C["epsb"][:], float(64 * EPS))
        hts = Rot([P.sb([128, NCH, 512], BF16) for _ in range(2)])
        qst = Rot([P.sb([128, 8, 512], BF16) for _ in range(2)])
        kst = Rot([P.sb([128, 8, 512], BF16) for _ in range(2)])
        vst = Rot([P.sb([128, 16, 65], BF16) for _ in range(2)])
        for vv in vst.items:
            P.op("pool", "memset", [], [vv], vv[:], 1.0)
        sqs = Rot([P.sb([128, 512], BF16) for _ in range(4)])
        rs = Rot([P.sb([128, 512]) for _ in range(4)])
        qns = Rot([P.sb([128, 512]) for _ in range(4)])
        t1s = Rot([P.sb([128, 512]) for _ in range(4)])
        t2s = Rot([P.sb([128, 512]) for _ in range(4)])
        pqs = Rot([P.ps() for _ in range(3)])
        pss = Rot([P.ps() for _ in range(2)])
        prs = Rot([P.ps() for _ in range(2)])
        pvs = Rot([P.ps() for _ in range(1)])
        for (t0, n, v) in TILES:
            ht = hts.next()
            P.dma("sp", ht[:, :, 0:n], k.HS[:, :, t0:t0 + n].rearrange("c p t -> p c t"), W=[ht])
            stg_q = qst.next()
            stg_k = kst.next()
            items = [(qi, hp) for qi in range(2) for hp in range(8)]
            st = {}

            def s1(it):
                qi, hp = it
                col = qi * 1024 + hp * 128
                pq = pqs.next()
                for c in range(NCH):
                    P.mm(pq[:, 0:n], w[:, c, col:col + 128], ht[:, c, 0:n], c == 0, c == NCH - 1, [wb, ht], [pq])
                sq = sqs.next()
                P.op("act", "activation", [pq], [sq], out=sq[:, 0:n], in_=pq[:, 0:n], func=AF.Square)
                st[it] = dict(pq=pq, sq=sq)

            def s2(it):
                qi, hp = it
                d_ = st[it]
                sg = stg_q if qi == 0 else stg_k
                ss = pss.next()
                P.mm(ss[:, 0:n], blk[:], d_["sq"][:, 0:n], True, True, [blk, d_["sq"]], [ss])
                r = rs.next()
                rsqrt_psum(P, C, ss, r, n, 0.0)
                qn = qns.next()
                P.op("dve", "scalar_tensor_tensor", [d_["pq"], r, g8], [qn], out=qn[:, 0:n], in0=d_["pq"][:, 0:n],
                     scalar=g8[:, qi:qi + 1], in1=r[:, 0:n], op0=ALU.mult, op1=ALU.mult)
                d_["qn"] = qn
                if v == 1:
                    P.op("act", "activation", [qn], [sg], out=sg[:, hp, 0:n], in_=qn[:, 0:n], func=AF.Copy)

            def s3(it):
                qi, hp = it
                d_ = st.pop(it)
                if v == 1:
                    return
                sg = stg_q if qi == 0 else stg_k
                qn = d_["qn"]
                pr = prs.next()
                P.mm(pr[:, 0:n], rot[:], qn[:, 0:n], True, True, [rot, qn], [pr])
                t1 = t1s.next()
                P.op("dve", "tensor_tensor", [qn, cosT], [t1], out=t1[:, 0:n], in0=qn[:, 0:n],
                     in1=cosT[:, t0:t0 + n], op=ALU.mult)
                t2 = t2s.next()
                P.op("dve", "tensor_tensor", [pr, sinS], [t2], out=t2[:, 0:n], in0=pr[:, 0:n],
                     in1=sinS[:, t0:t0 + n], op=ALU.mult)
                P.op("dve", "tensor_tensor", [t1, t2], [sg], out=sg[:, hp, 0:n], in0=t1[:, 0:n], in1=t2[:, 0:n],
                     op=ALU.add)

            for step in range(len(items) + 2):
                if step < len(items):
                    s1(items[step])
                if 1 <= step <= len(items):
                    s2(items[step - 1])
                if step >= 2:
                    s3(items[step - 2])
            P.store("pool", k.Q[:, :, t0:t0 + n].rearrange("c p t -> p c t"), stg_q[:, :, 0:n], R=[stg_q])
            P.store("pool", k.KK[:, :, t0:t0 + n].rearrange("c p t -> p c t"), stg_k[:, :, 0:n], R=[stg_k])
            for b in range(n // 128):
                vs = vst.next()
                for half in range(2):
                    pv = pvs.next()
                    for c in range(NCH):
                        P.mm(pv[:, :], ht[:, c, b * 128:(b + 1) * 128], w[:, c, 2048 + half * 512:2048 + (half + 1) * 512],
                             c == 0, c == NCH - 1, [wb, ht], [pv])
                    P.op("act", "activation", [pv], [vs], out=vs[:, half * 8:(half + 1) * 8, 0:64],
                         in_=pv[:, :].rearrange("p (h e) -> p h e", e=64), func=AF.Copy)
                P.store("pool", k.V[t0 + b * 128:t0 + (b + 1) * 128, :], vs[:].rearrange("p h e -> p (h e)"), R=[vs])


def ph_zdt(k, l):
    with Phase(k.nc, "zdt%d" % l) as P:
        w = P.sb([128, NCH, 2048 + 64], BF16)
        wb = load_w_cols(P, w, k.w_in[l], OZ, 2048)
        wb2 = Buf()
        src = k.w_in[l].rearrange("(c p) f -> p c f", p=128)
        P.dma("pool", w[:, :, 2048:2112], src[:, :, ODT:ODT + 64], W=[wb2])
        hts = Rot([P.sb([128, NCH, 512], BF16) for _ in range(2)])
        zst = Rot([P.sb([128, 2048]) for _ in range(2)])
        dst_ = Rot([P.sb([128, 64]) for _ in range(2)])
        pzs = Rot([P.ps() for _ in range(4)])
        pds = Rot([P.ps([128, 64]) for _ in range(2)])
        cp = Rot(["act", "dve"])
        for (t0, n, v) in TILES:
            ht = hts.next()
            P.dma("pool", ht[:, :, 0:n], k.HS[:, :, t0:t0 + n].rearrange("c p t -> p c t"), W=[ht])
            for b in range(n // 128):
                zs = zst.next()
                for q4 in range(4):
                    pz = pzs.next()
                    for c in range(NCH):
                        P.mm(pz[:, :], ht[:, c, b * 128:(b + 1) * 128], w[:, c, q4 * 512:(q4 + 1) * 512],
                             c == 0, c == NCH - 1, [wb, ht], [pz])
                    if cp.next() == "act":
                        P.op("act", "activation", [pz], [zs], out=zs[:, q4 * 512:(q4 + 1) * 512], in_=pz[:, :],
                             func=AF.Copy)
                    else:
                        P.op("dve", "tensor_copy", [pz], [zs], out=zs[:, q4 * 512:(q4 + 1) * 512], in_=pz[:, :])
                P.store("sp", k.Z[t0 + b * 128:t0 + (b + 1) * 128, :], zs[:], R=[zs])
                pd = pds.next()
                for c in range(NCH):
                    P.mm(pd[:, :], ht[:, c, b * 128:(b + 1) * 128], w[:, c, 2048:2112], c == 0, c == NCH - 1,
                         [wb2, ht], [pd])
                ds = dst_.next()
                P.op("dve", "tensor_copy", [pd], [ds], out=ds[:], in_=pd[:, :])
                P.store("sp", k.DT[t0 + b * 128:t0 + (b + 1) * 128, :], ds[:], R=[ds])


def ph_proj_fm(k, l, name, c0, nchunks, dst, sigmoid):
    with Phase(k.nc, "%s%d" % (name, l)) as P:
        w = P.sb([128, NCH, nchunks * 128], BF16)
        wb = load_w_cols(P, w, k.w_in[l], c0, nchunks * 128)
        hts = Rot([P.sb([128, NCH, 512], BF16) for _ in range(2)])
        stg = Rot([P.sb([128, 8, 512]) for _ in range(2)])
        pps = Rot([P.ps() for _ in range(4)])
        cp = Rot(["act", "dve"])
        for (t0, n, v) in TILES:
            ht = hts.next()
            P.dma("pool", ht[:, :, 0:n], k.HS[:, :, t0:t0 + n].rearrange("c p t -> p c t"), W=[ht])
            for g8 in range(nchunks // 8):
                sg = stg.next()
                for j in range(8):
                    ch = g8 * 8 + j
                    pp = pps.next()
                    for c in range(NCH):
                        P.mm(pp[:, 0:n], w[:, c, ch * 128:(ch + 1) * 128], ht[:, c, 0:n], c == 0, c == NCH - 1,
                             [wb, ht], [pp])
                    if sigmoid:
                        P.op("act", "activation", [pp], [sg], out=sg[:, j, 0:n], in_=pp[:, 0:n], func=AF.Sigmoid)
                    elif cp.next() == "act":
                        P.op("act", "activation", [pp], [sg], out=sg[:, j, 0:n], in_=pp[:, 0:n], func=AF.Copy)
                    else:
                        P.op("dve", "tensor_copy", [pp], [sg], out=sg[:, j, 0:n], in_=pp[:, 0:n])
                P.store("sp", dst[g8 * 8:(g8 + 1) * 8, :, t0:t0 + n].rearrange("c p t -> p c t"), sg[:, :, 0:n], R=[sg])


def ph_conv(k, l):
    with Phase(k.nc, "conv%d" % l) as P:
        ident = P.sb([128, 128])
        P.dma("sp", ident[:], k.ident, W=[ident])
        identb = P.sb([128, 128], BF16)
        P.dma("pool", identb[:], k.ident, W=[identb])
        cw = P.sb([128, 24, 5])
        cb = P.sb([128, 24])
        P.dma("sp", cw[:], k.conv_w[l], W=[cw])
        P.dma("sp", cb[:], k.conv_b[l], W=[cb])
        xins = Rot([P.sb([128, 8, 516]) for _ in range(2)])
        accs = Rot([P.sb([128, 512]) for _ in range(8)])
        us = Rot([P.sb([128, 512]) for _ in range(8)])
        xst = Rot([P.sb([128, 4, 2048]) for _ in range(1)])
        bfs = Rot([P.sb([128, 4, 512], BF16) for _ in range(2)])
        bts = Rot([P.sb([128, 4, 512], BF16) for _ in range(2)])
        pts = Rot([P.ps() for _ in range(3)])
        ptb = Rot([P.ps([128, 512], BF16) for _ in range(2)])
        tapeng = Rot(["dve"])
        cp = Rot(["dve", "act"])
        for (t0, n, v) in TILES:
            seg0, seg1 = (0, NL) if v == 0 else (NL, NT)
            nb = n // 128
            xs_t = xst.next()
            for g in range(3):
                xin = xins.next()
                lo = max(t0 - 2, seg0)
                hi = min(t0 + n + 2, seg1)
                if lo > t0 - 2:
                    P.op("pool", "memset", [], [xin], xin[:, :, 0:2], 0.0)
                if hi < t0 + n + 2:
                    P.op("pool", "memset", [], [xin], xin[:, :, n + 2:n + 4], 0.0)
                P.dma("pool", xin[:, :, lo - (t0 - 2):hi - (t0 - 2)],
                      k.XBC[g * 8:(g + 1) * 8, :, lo:hi].rearrange("c p t -> p c t"), W=[xin])
                if g == 2:
                    bf = bfs.next()
                    cf = bfs.next()
                    bt = bts.next()
                for quad in range(2):
                    ul = []
                    accl = [accs.next() for _ in range(4)]
                    for j4 in range(4):
                        j = quad * 4 + j4
                        ch = g * 8 + j
                        P.op("act", "activation", [xin, cw, cb], [accl[j4]], out=accl[j4][:, 0:n], in_=xin[:, j, 0:n],
                             func=AF.Identity, scale=cw[:, ch, 0:1], bias=cb[:, ch:ch + 1])
                    for kk in range(1, 5):
                        for j4 in range(4):
                            j = quad * 4 + j4
                            ch = g * 8 + j
                            P.op("dve", "scalar_tensor_tensor", [xin, cw, accl[j4]], [accl[j4]], out=accl[j4][:, 0:n],
                                 in0=xin[:, j, kk:kk + n], scalar=cw[:, ch, kk:kk + 1], in1=accl[j4][:, 0:n],
                                 op0=ALU.mult, op1=ALU.add)
                    for j4 in range(4):
                        acc = accl[j4]
                        if g < 2:
                            u = us.next()
                            P.op("act", "activation", [acc], [u], out=u[:, 0:n], in_=acc[:, 0:n], func=AF.Silu)
                            ul.append(u)
                        elif quad == 0:
                            P.op("act", "activation", [acc], [bf], out=bf[:, j4, 0:n], in_=acc[:, 0:n], func=AF.Silu)
                        else:
                            P.op("act", "activation", [acc], [cf], out=cf[:, j4, 0:n], in_=acc[:, 0:n], func=AF.Silu)
                    if g < 2:
                        q16 = g * 2 + quad
                        for b in range(nb):
                            pt = pts.next()
                            for j4 in range(4):
                                P.mm(pt[:, j4 * 128:(j4 + 1) * 128], ul[j4][:, b * 128:(b + 1) * 128], ident[:],
                                     True, True, [ul[j4], ident], [pt])
                            if cp.next() == "dve":
                                P.op("dve", "tensor_copy", [pt], [xs_t], out=xs_t[:, b, q16 * 512:(q16 + 1) * 512],
                                     in_=pt[:, :])
                            else:
                                P.op("act", "activation", [pt], [xs_t], out=xs_t[:, b, q16 * 512:(q16 + 1) * 512],
                                     in_=pt[:, :], func=AF.Copy)
                    elif quad == 0:
                        for b in range(nb):
                            pb = ptb.next()
                            for j4 in range(4):
                                P.S.add("pe", (lambda e, o=pb[:, j4 * 128:(j4 + 1) * 128], i=bf[:, j4, b * 128:(b + 1) * 128]:
                                               e.transpose(o, i, identb[:])), _bufs([bf, identb]), _bufs([pb]))
                            P.op("dve", "tensor_copy", [pb], [bt], out=bt[:, b, :], in_=pb[:, :])
                        P.store("sp", k.BF[:, :, t0:t0 + n].rearrange("c p t -> p c t"), bf[:, :, 0:n], R=[bf])
                        P.store("sp", k.BT[t0:t0 + n, :].rearrange("(b p) f -> p b f", p=128), bt[:, 0:nb, :], R=[bt])
                    else:
                        P.store("sp", k.CF[:, :, t0:t0 + n].rearrange("c p t -> p c t"), cf[:, :, 0:n], R=[cf])
            P.store("sp", k.XS[t0:t0 + n, :].rearrange("(b p) f -> p b f", p=128), xs_t[:, 0:nb, :], R=[xs_t])


def _qrange(j):
    qlo = 0 if j <= 3 else 2 * j - 4
    qhi = 63 if j >= 28 else 2 * j + 5
    return qlo, qhi


def ph_attn(k, l, need_ctx):
    with Phase(k.nc, "attn%d" % l) as P:
        mf = P.sb([128, 1024], BF16)
        mi = P.sb([128, 1024], BF16)
        P.dma("pool", mf[:], k.maskF, W=[mf])
        P.dma("pool", mi[:], k.maskI, W=[mi])
        kts = Rot([P.sb([128, NT], BF16) for _ in range(2)])
        qts = Rot([P.sb([128, NT], BF16) for _ in range(2)])
        vts = Rot([P.sb([128, 34, 2, 65], BF16) for _ in range(2)])
        aos = Rot([P.sb([128, 34, 128], BF16) for _ in range(2)])
        PT = P.sb([128, 32, 768], BF16)
        PTb = [Buf() for _ in range(32)]
        PC = P.sb([128, 2, NT], BF16)
        PCb = [Buf() for _ in range(9)]
        rbs = Rot([P.sb([128, 1024]) for _ in range(2)])
        ees = Rot([P.sb([128, 1024]) for _ in range(2)])
        efs = Rot([P.sb([128, 1024], BF16) for _ in range(2)])
        eis = Rot([P.sb([128, 1024], BF16) for _ in range(2)])
        exs = Rot([P.sb([128, 512]) for _ in range(3)])
        rcs = Rot([P.sb([128, 8]) for _ in range(2)])
        pss = Rot([P.ps() for _ in range(4)])
        pos = Rot([P.ps() for _ in range(3)])
        nqb = 34 if need_ctx else 32
        for hp in range(8):
            kt = kts.next()
            qt = qts.next()
            vt = vts.next()
            ao = aos.next()
            P.dma("sp", kt[:], k.KK[hp], W=[kt])
            P.dma("sp", qt[:], k.Q[hp], W=[qt])
            P.dma("sp", vt[:].rearrange("p b a e -> p b (a e)"),
                  k.V[:, hp * 130:(hp + 1) * 130].rearrange("(b p) f -> p b f", p=128), W=[vt])
            for a in range(2):
                h = hp * 2 + a
                pr = slice(a * 64, a * 64 + 64)
                rb = rbs.next()
                P.dma("sp", rb[:], k.RB[l, h], W=[rb])
                ee = ees.next()
                P.op("act", "activation", [rb], [ee], out=ee[:], in_=rb[:], func=AF.Exp)
                ef = efs.next()
                ei = eis.next()
                P.op("dve", "tensor_tensor", [ee, mf], [ef], out=ef[:], in0=ee[:], in1=mf[:], op=ALU.mult)
                P.op("dve", "tensor_tensor", [ee, mi], [ei], out=ei[:], in0=ee[:], in1=mi[:], op=ALU.mult)
                for j in range(32):
                    qlo, qhi = _qrange(j)
                    nq = (qhi - qlo + 1) * 64
                    off = 0
                    while off < nq:
                        ln = min(512, nq - off)
                        ps = pss.next()
                        P.mm(ps[:, 0:ln], kt[pr, j * 128:(j + 1) * 128], qt[pr, qlo * 64 + off:qlo * 64 + off + ln],
                             True, True, [kt, qt], [ps])
                        ex = exs.next()
                        P.op("act", "activation", [ps], [ex], out=ex[:, 0:ln], in_=ps[:, 0:ln], func=AF.Exp, scale=0.125)
                        r0 = qlo + off // 64
                        r1 = r0 + ln // 64
                        qr = r0
                        while qr < r1:
                            full = (qr <= 3 or qr >= 61)
                            qe = qr
                            while qe < r1 and ((qe <= 3 or qe >= 61) == full):
                                qe += 1
                            tab = ef if full else ei
                            s0 = 7 - 2 * j + qr
                            cnt = (qe - qr) * 64
                            assert 0 <= s0 and s0 * 64 + cnt <= 1024
                            P.op("dve", "tensor_tensor", [ex, tab], [PTb[j]],
                                 out=PT[:, j, (qr - qlo) * 64:(qr - qlo) * 64 + cnt],
                                 in0=ex[:, (qr - r0) * 64:(qr - r0) * 64 + cnt], in1=tab[:, s0 * 64:s0 * 64 + cnt],
                                 op=ALU.mult)
                            qr = qe
                        off += ln
                for ti, (t0, n, v) in enumerate(TILES if need_ctx else TILES[:8]):
                    for cc in range(2):
                        ps = pss.next()
                        P.mm(ps[:, 0:n], kt[pr, NL + cc * 128:NL + (cc + 1) * 128], qt[pr, t0:t0 + n], True, True,
                             [kt, qt], [ps])
                        P.op("act", "activation", [ps], [PCb[ti]], out=PC[:, cc, t0:t0 + n], in_=ps[:, 0:n],
                             func=AF.Exp, scale=0.125)
                for g0 in range(0, nqb, 7):
                    grp = list(range(g0, min(nqb, g0 + 7)))
                    po = pos.next()
                    for si, i in enumerate(grp):
                        mms = []
                        if i < 32:
                            for jj in range(32):
                                qlo, qhi = _qrange(jj)
                                if qlo <= 2 * i and 2 * i + 1 <= qhi:
                                    mms.append((PT[:, jj, (2 * i - qlo) * 64:(2 * i - qlo) * 64 + 128],
                                                vt[:, jj, a, :], PTb[jj]))
                            ti = i // 4
                        else:
                            ti = 8
                        for cc in range(2):
                            mms.append((PC[:, cc, i * 128:(i + 1) * 128], vt[:, 32 + cc, a, :], PCb[ti]))
                        for mi_, (lh, rh, bb) in enumerate(mms):
                            P.mm(po[:, si * 65:(si + 1) * 65], lh, rh, mi_ == 0, mi_ == len(mms) - 1, [bb, vt], [po])
                    ng = len(grp)
                    rc = rcs.next()
                    pov = po[:, 0:ng * 65].rearrange("p (s e) -> p s e", e=65)
                    P.op("dve", "reciprocal", [po], [rc], out=rc[:, 0:ng], in_=pov[:, :, 64])
                    P.op("dve", "tensor_tensor", [po, rc], [ao], out=ao[:, g0:g0 + ng, a * 64:(a + 1) * 64],
                         in0=pov[:, :, 0:64], in1=rc[:, 0:ng].unsqueeze(2).to_broadcast([128, ng, 64]), op=ALU.mult)
            P.store("pool", k.AO[0:nqb * 128, hp * 128:(hp + 1) * 128].rearrange("(b p) f -> p b f", p=128),
                    ao[:, 0:nqb, :], R=[ao])


def ssd_consts(P, k, l):
    C = {}
    for nm, src in [("triF", k.triF), ("triB", k.triB), ("nm2F", k.nm2F), ("nm2B", k.nm2B), ("negI", k.negI),
                    ("ones", k.ones), ("ntriF", k.ntriF), ("ntriB", k.ntriB)]:
        C[nm] = P.sb([128, 128], BF16)
        P.dma("pool", C[nm][:], src, W=[C[nm]])
    C["dtb"] = P.sb([128, 64])
    C["alog"] = P.sb([128, 64])
    C["abc"] = P.sb([128, 64])
    C["one"] = P.sb([128, 1])
    P.op("dve", "memset", [], [C["one"]], C["one"][:], 1.0)
    P.dma("sp", C["dtb"][:], k.dt_bias_bc[l], W=[C["dtb"]])
    P.dma("sp", C["alog"][:], k.a_log_bc[l], W=[C["alog"]])
    P.op("act", "activation", [C["alog"]], [C["abc"]], out=C["abc"][:], in_=C["alog"][:], func=AF.Exp)
    P.op("dve", "tensor_scalar", [C["abc"]], [C["abc"]], out=C["abc"][:], in0=C["abc"][:], scalar1=-1.0, scalar2=0.0,
         op0=ALU.mult, op1=ALU.add)
    return C


NBLK = NT // 128


def ssd_prep_all(P, k, C, pbanks):
    W_ = NBLK * 64

    def big(dt_=F32):
        return P.sb([128, NBLK, 64], dt_)

    def bcv(t):
        return t[:, :].unsqueeze(1).to_broadcast([128, NBLK, 64])

    def flat(t):
        return t[:].rearrange("p c f -> p (c f)")

    tmp = big()
    P.dma("sp", tmp[:], k.DT.rearrange("(c p) f -> p c f", p=128), W=[tmp])
    P.op("dve", "tensor_tensor", [tmp, C["dtb"]], [tmp], out=tmp[:], in0=tmp[:], in1=bcv(C["dtb"]), op=ALU.add)
    P.op("act", "activation", [tmp], [tmp], out=flat(tmp), in_=flat(tmp), func=AF.Exp)
    dtv = big()
    P.op("act", "activation", [tmp, C["one"]], [dtv], out=flat(dtv), in_=flat(tmp), func=AF.Ln, bias=C["one"][:, 0:1],
         scale=1.0)
    P.op("dve", "tensor_tensor", [dtv, C["abc"]], [tmp], out=tmp[:], in0=dtv[:], in1=bcv(C["abc"]), op=ALU.mult)
    hi = big(BF16)
    lo = big(BF16)
    P.op("dve", "tensor_copy", [tmp], [hi], out=hi[:], in_=tmp[:])
    P.op("dve", "tensor_tensor", [tmp, hi], [lo], out=lo[:], in0=tmp[:], in1=hi[:], op=ALU.subtract)
    acum = big()
    alast = big()
    pb = Rot(pbanks)
    for d, tri in ((0, C["triF"]), (1, C["triB"])):
        for c0 in range(0, NBLK, 16):
            c1 = min(NBLK, c0 + 16)
            ncol = (c1 - c0) * 32
            ps = pb.next()
            P.mm(ps[:, 0:ncol], tri[:], hi[:, c0:c1, d * 32:(d + 1) * 32], True, False, [tri, hi], [ps])
            P.mm(ps[:, 0:ncol], tri[:], lo[:, c0:c1, d * 32:(d + 1) * 32], False, True, [tri, lo], [ps])
            P.op("dve", "tensor_copy", [ps], [acum], out=acum[:, c0:c1, d * 32:(d + 1) * 32],
                 in_=ps[:, 0:ncol].rearrange("p (c f) -> p c f", f=32))
    for o0 in range(0, W_, 512):
        o1 = min(W_, o0 + 512)
        ps = pb.next()
        P.mm(ps[:, 0:o1 - o0], C["ones"][:], flat(hi)[:, o0:o1], True, False, [C["ones"], hi], [ps])
        P.mm(ps[:, 0:o1 - o0], C["ones"][:], flat(lo)[:, o0:o1], False, True, [C["ones"], lo], [ps])
        P.op("dve", "tensor_copy", [ps], [alast], out=flat(alast)[:, o0:o1], in_=ps[:, 0:o1 - o0])
    eacum = big()
    P.op("act", "activation", [acum], [eacum], out=flat(eacum), in_=flat(acum), func=AF.Exp)
    sv = big()
    P.op("dve", "tensor_tensor", [alast, acum], [sv], out=sv[:], in0=alast[:], in1=acum[:], op=ALU.subtract)
    P.op("act", "activation", [sv], [sv], out=flat(sv), in_=flat(sv), func=AF.Exp)
    P.op("dve", "tensor_tensor", [sv, dtv], [sv], out=sv[:], in0=sv[:], in1=dtv[:], op=ALU.mult)
    ealast = big()
    P.op("act", "activation", [alast], [ealast], out=flat(ealast), in_=flat(alast), func=AF.Exp)
    return dict(dtv=dtv, hi=hi, eacum=eacum, sv=sv, ealast=ealast)


def _bc(ap32, n=32):
    return ap32.unsqueeze(2).to_broadcast([128, n, 64])


def ph_ssd_a(k, l, need_ctx):
    with Phase(k.nc, "ssda%d" % l) as P:
        C = ssd_consts(P, k, l)
        plp = [P.ps() for _ in range(3)]
        A = ssd_prep_all(P, k, C, plp[:2])
        plp = Rot(plp)
        xss = Rot([P.sb([128, 2048]) for _ in range(3)])
        bts = Rot([P.sb([128, 512], BF16) for _ in range(2)])
        bfs = Rot([P.sb([128, 4, 128], BF16) for _ in range(2)])
        cfs = Rot([P.sb([128, 4, 128], BF16) for _ in range(2)])
        xdf = Rot([P.sb([128, 2048], BF16) for _ in range(2)])
        xdb = Rot([P.sb([128, 2048], BF16) for _ in range(2)])
        xsf = Rot([P.sb([128, 2048], BF16) for _ in range(2)])
        yts = Rot([P.sb([128, 2048]) for _ in range(2)])
        lxs = Rot([P.sb([128, 512]) for _ in range(3)])
        wts = Rot([P.sb([128, 512], BF16) for _ in range(3)])
        tms = Rot([P.sb([128, 512]) for _ in range(4)])
        Sf = P.sb([128, 2048])
        Sfb = P.sb([128, 2048], BF16)
        Sg = [Buf() for _ in range(4)]
        Sgb = [Buf() for _ in range(4)]
        P.op("pool", "memset", [], Sg, Sf[:], 0.0)
        P.op("pool", "memset", [], Sgb, Sfb[:], 0.0)
        pcb = Rot([P.ps() for _ in range(1)])
        pys = Rot([P.ps() for _ in range(2)])
        pos = Rot([P.ps() for _ in range(2)])
        for t0 in CHUNKS_F:
            ci = t0 // 128
            is_ctx = t0 >= NL
            do_y = need_ctx or not is_ctx
            xs = xss.next()
            bt = bts.next()
            P.dma("pool", xs[:], k.XS[t0:t0 + 128, :], W=[xs])
            P.dma("pool", bt[:], k.BT[t0:t0 + 128, :], W=[bt])
            xf = xsf.next()
            P.op("dve", "tensor_tensor", [xs, A["sv"]], [xf], out=xf[:].rearrange("p (h e) -> p h e", e=64),
                 in0=xs[:].rearrange("p (h e) -> p h e", e=64), in1=_bc(A["sv"][:, ci, 0:32]), op=ALU.mult)
            if do_y:
                bf = bfs.next()
                cf = cfs.next()
                P.dma("pool", bf[:], k.BF[:, :, t0:t0 + 128].rearrange("c p t -> p c t"), W=[bf])
                P.dma("pool", cf[:], k.CF[:, :, t0:t0 + 128].rearrange("c p t -> p c t"), W=[cf])
                xd = [xdf.next(), xdb.next()]
                for d in range(2):
                    P.op("dve", "tensor_tensor", [xs, A["dtv"]], [xd[d]],
                         out=xd[d][:].rearrange("p (h e) -> p h e", e=64),
                         in0=xs[:].rearrange("p (h e) -> p h e", e=64),
                         in1=_bc(A["dtv"][:, ci, d * 32:(d + 1) * 32]), op=ALU.mult)
                cbp = pcb.next()
                for g in range(4):
                    P.mm(cbp[:, g * 128:(g + 1) * 128], bf[:, g, :], cf[:, g, :], True, True, [bf, cf], [cbp])
                yt = yts.next()
            def fill(g, quad):
                lp = plp.next()
                prs = [(quad * 2 + q2, d) for q2 in range(2) for d in range(2)]
                for s_, (hh, d) in enumerate(prs):
                    col = d * 32 + g * 8 + hh
                    tri, ntri, nm2 = (C["triF"], C["ntriF"], C["nm2F"]) if d == 0 else \
                        (C["triB"], C["ntriB"], C["nm2B"])
                    hb = A["hi"][:, ci, col:col + 1].to_broadcast([128, 128])
                    reg = lp[:, s_ * 128:(s_ + 1) * 128]
                    P.mm(reg, hb, tri[:], True, False, [A["hi"], tri], [lp])
                    P.mm(reg, ntri[:], hb, False, False, [A["hi"], ntri], [lp])
                    P.mm(reg, C["negI"][:], nm2[:], False, True, [C["negI"], nm2], [lp])
                lx = lxs.next()
                P.op("act", "activation", [lp], [lx], out=lx[:], in_=lp[:, :], func=AF.Exp)
                wt = wts.next()
                P.op("dve", "tensor_tensor", [lx, cbp], [wt], out=wt[:].rearrange("p (s i) -> p s i", s=4),
                     in0=lx[:].rearrange("p (s i) -> p s i", s=4),
                     in1=cbp[:, g * 128:(g + 1) * 128].unsqueeze(1).to_broadcast([128, 4, 128]), op=ALU.mult)
                return wt, prs

            def state_update(g):
                pst = pos.next()
                P.mm(pst[:, :], bt[:, g * 128:(g + 1) * 128], xf[:, g * 512:(g + 1) * 512], True, True, [bt, xf], [pst])
                tm2 = tms.next()
                P.op("dve", "tensor_tensor", [Sg[g], A["ealast"]], [tm2], out=tm2[:].rearrange("p (h e) -> p h e", e=64),
                     in0=Sf[:, g * 512:(g + 1) * 512].rearrange("p (h e) -> p h e", e=64),
                     in1=_bc(A["ealast"][:, ci, g * 8:(g + 1) * 8], 8), op=ALU.mult)
                P.op("dve", "tensor_tensor", [tm2, pst], [Sg[g]], out=Sf[:, g * 512:(g + 1) * 512], in0=tm2[:], in1=pst[:, :],
                     op=ALU.add)
                P.op("act", "activation", [Sg[g]], [Sgb[g]], out=Sfb[:, g * 512:(g + 1) * 512],
                     in_=Sf[:, g * 512:(g + 1) * 512], func=AF.Copy)

            if do_y:
                jobs = [(g, quad) for g in range(4) for quad in range(4)]
                filled = {}
                pyg = {}
                SKEW = 2
                for idx in range(len(jobs) + SKEW):
                    if idx < len(jobs):
                        filled[idx] = fill(*jobs[idx])
                    if idx >= SKEW:
                        g, quad = jobs[idx - SKEW]
                        wt, prs = filled.pop(idx - SKEW)
                        if quad == 0:
                            pyg[g] = pys.next()
                        py = pyg[g]
                        for s_, (hh, d) in enumerate(prs):
                            h = g * 8 + hh
                            P.mm(py[:, hh * 64:(hh + 1) * 64], wt[:, s_ * 128:(s_ + 1) * 128],
                                 xd[d][:, h * 64:(h + 1) * 64], d == 0, d == 1, [wt, xd[d]], [py])
                        if quad == 3:
                            po = pos.next()
                            P.mm(po[:, :], cf[:, g, :], Sfb[:, g * 512:(g + 1) * 512], True, True, [cf, Sgb[g]], [po])
                            tm = tms.next()
                            P.op("dve", "tensor_tensor", [po, A["eacum"]], [tm],
                                 out=tm[:].rearrange("p (h e) -> p h e", e=64),
                                 in0=po[:, :].rearrange("p (h e) -> p h e", e=64),
                                 in1=_bc(A["eacum"][:, ci, g * 8:(g + 1) * 8], 8), op=ALU.mult)
                            P.op("dve", "tensor_tensor", [tm, py], [yt], out=yt[:, g * 512:(g + 1) * 512], in0=tm[:],
                                 in1=py[:, :], op=ALU.add)
                            state_update(g)
            else:
                for g in range(4):
                    state_update(g)
            if do_y:
                P.store("sp", k.YP[t0:t0 + 128, :], yt[:], R=[yt])


def ph_ssd_b(k, l, need_ctx):
    with Phase(k.nc, "ssdb%d" % l) as P:
        C = ssd_consts(P, k, l)
        pos_banks = [P.ps() for _ in range(4)]
        A = ssd_prep_all(P, k, C, pos_banks[:2])
        epsb = P.sb([128, 1])
        P.op("dve", "memset", [], [epsb], epsb[:], float(EPS))
        dbc = P.sb([128, 32])
        nwb = P.sb([128, 2048])
        P.dma("sp", dbc[:], k.ssd_d_bc[l], W=[dbc])
        P.dma("sp", nwb[:], k.ssd_norm_bc[l], W=[nwb])
        xss = Rot([P.sb([128, 2048]) for _ in range(3)])
        bts = Rot([P.sb([128, 512], BF16) for _ in range(2)])
        cfs = Rot([P.sb([128, 4, 128], BF16) for _ in range(2)])
        yps = Rot([P.sb([128, 2048]) for _ in range(3)])
        zs = Rot([P.sb([128, 2048]) for _ in range(3)])
        xsb = Rot([P.sb([128, 2048], BF16) for _ in range(2)])
        sos = Rot([P.sb([128, 2048], BF16) for _ in range(2)])
        jk = P.sb([128, 512])
        tms = Rot([P.sb([128, 512]) for _ in range(3)])
        ssqs = Rot([P.sb([128, 4]) for _ in range(2)])
        rrs = Rot([P.sb([128, 4]) for _ in range(2)])
        Sb = P.sb([128, 2048])
        Sbb = P.sb([128, 2048], BF16)
        Sg = [Buf() for _ in range(4)]
        Sgb = [Buf() for _ in range(4)]
        P.op("pool", "memset", [], Sg, Sb[:], 0.0)
        P.op("pool", "memset", [], Sgb, Sbb[:], 0.0)
        pos = Rot(pos_banks)
        for t0 in CHUNKS_B:
            is_ctx = t0 >= NL
            do_y = need_ctx or not is_ctx
            xs = xss.next()
            bt = bts.next()
            P.dma("pool", xs[:], k.XS[t0:t0 + 128, :], W=[xs])
            P.dma("pool", bt[:], k.BT[t0:t0 + 128, :], W=[bt])
            ci = t0 // 128
            xb = xsb.next()
            P.op("dve", "tensor_tensor", [xs, A["sv"]], [xb], out=xb[:].rearrange("p (h e) -> p h e", e=64),
                 in0=xs[:].rearrange("p (h e) -> p h e", e=64), in1=_bc(A["sv"][:, ci, 32:64]), op=ALU.mult)
            if do_y:
                cf = cfs.next()
                yp = yps.next()
                z = zs.next()
                P.dma("pool", cf[:], k.CF[:, :, t0:t0 + 128].rearrange("c p t -> p c t"), W=[cf])
                P.dma("pool", yp[:], k.YP[t0:t0 + 128, :], W=[yp])
                P.dma("pool", z[:], k.Z[t0:t0 + 128, :], W=[z])
                P.op("act", "activation", [z], [z], out=z[:], in_=z[:], func=AF.Silu)
                ssq = ssqs.next()
                P.op("dve", "memset", [], [ssq], ssq[:], 0.0)
            for g in range(4):
                gs = slice(g * 512, (g + 1) * 512)
                if do_y:
                    po = pos.next()
                    P.mm(po[:, :], cf[:, g, :], Sbb[:, gs], True, True, [cf, Sgb[g]], [po])
                    tm = tms.next()
                    P.op("dve", "tensor_tensor", [po, A["eacum"]], [tm], out=tm[:].rearrange("p (h e) -> p h e", e=64),
                         in0=po[:, :].rearrange("p (h e) -> p h e", e=64),
                         in1=_bc(A["eacum"][:, ci, 32 + g * 8:32 + (g + 1) * 8], 8), op=ALU.mult)
                    P.op("dve", "tensor_tensor", [tm, yp], [yp], out=yp[:, gs], in0=tm[:], in1=yp[:, gs], op=ALU.add)
                    tm3 = tms.next()
                    P.op("dve", "tensor_tensor", [xs, dbc], [tm3], out=tm3[:].rearrange("p (h e) -> p h e", e=64),
                         in0=xs[:, gs].rearrange("p (h e) -> p h e", e=64), in1=_bc(dbc[:, g * 8:(g + 1) * 8], 8),
                         op=ALU.mult)
                    P.op("dve", "tensor_tensor", [tm3, yp], [yp], out=yp[:, gs], in0=tm3[:], in1=yp[:, gs], op=ALU.add)
                    P.op("dve", "tensor_tensor", [yp, z], [yp], out=yp[:, gs], in0=yp[:, gs], in1=z[:, gs], op=ALU.mult)
                    P.op("act", "activation", [yp, ssq], [jk, ssq], out=jk[:], in_=yp[:, gs], func=AF.Square,
                         accum_out=ssq[:, g:g + 1])
                pst = pos.next()
                P.mm(pst[:, :], bt[:, g * 128:(g + 1) * 128], xb[:, gs], True, True, [bt, xb], [pst])
                tm2 = tms.next()
                P.op("dve", "tensor_tensor", [Sg[g], A["ealast"]], [tm2], out=tm2[:].rearrange("p (h e) -> p h e", e=64),
                     in0=Sb[:, gs].rearrange("p (h e) -> p h e", e=64),
                     in1=_bc(A["ealast"][:, ci, 32 + g * 8:32 + (g + 1) * 8], 8), op=ALU.mult)
                P.op("dve", "tensor_tensor", [tm2, pst], [Sg[g]], out=Sb[:, gs], in0=tm2[:], in1=pst[:, :], op=ALU.add)
                P.op("act", "activation", [Sg[g]], [Sgb[g]], out=Sbb[:, gs], in_=Sb[:, gs], func=AF.Copy)
            if do_y:
                rr = rrs.next()
                P.op("act", "activation", [ssq, epsb], [rr], out=rr[:], in_=ssq[:], func=AF.Sqrt, bias=epsb[:, 0:1],
                     scale=1.0 / 512.0)
                P.op("dve", "reciprocal", [rr], [rr], out=rr[:], in_=rr[:])
                so = sos.next()
                for g in range(4):
                    gs = slice(g * 512, (g + 1) * 512)
                    P.op("dve", "scalar_tensor_tensor", [yp, rr, nwb], [so], out=so[:, gs], in0=yp[:, gs],
                         scalar=rr[:, g:g + 1], in1=nwb[:, gs], op0=ALU.mult, op1=ALU.mult)
                P.store("sp", k.SO[t0:t0 + 128, :], so[:], R=[so])


def ph_merge(k, l, skip_ctx):
    with Phase(k.nc, "mrg%d" % l) as P:
        identb = P.sb([128, 128], BF16)
        P.dma("pool", identb[:], k.ident, W=[identb])
        ms = P.sb([128, 9, NCH, 2])
        P.dma("sp", ms[:].rearrange("p a c v -> p (a c v)"), k.MODS, W=[ms])
        wna = P.sb([128, 8, D], BF16)
        wso = P.sb([128, 16, D], BF16)
        wou = P.sb([128, 8, D], BF16)
        wnb, wsb, wob = Buf(), Buf(), Buf()
        for c in range(8):
            P.dma("pool", wna[:, c, :], k.na_w_o[l].rearrange("(c p) f -> p c f", p=128)[:, c, :], W=[wnb])
        for c in range(16):
            P.dma("pool", wso[:, c, :], k.ssd_w_o[l].rearrange("(c p) f -> p c f", p=128)[:, c, :], W=[wsb])
        for c in range(8):
            P.dma("pool", wou[:, c, :], k.w_out[l].rearrange("(c p) f -> p c f", p=128)[:, c, :], W=[wob])
        xts = Rot([P.sb([128, NCH, 512]) for _ in range(2)])
        gts = Rot([P.sb([128, 16, 512]) for _ in range(1)])
        ain = Rot([P.sb([128, 4, D], BF16) for _ in range(1)])
        sin_ = Rot([P.sb([128, 4, 2048], BF16) for _ in range(1)])
        aof = Rot([P.sb([128, 8, 512], BF16) for _ in range(1)])
        sof = Rot([P.sb([128, 16, 512], BF16) for _ in range(1)])
        mts = Rot([P.sb([128, 8, 512], BF16) for _ in range(1)])
        t1s = Rot([P.sb([128, 512]) for _ in range(2)])
        t2s = Rot([P.sb([128, 512]) for _ in range(2)])
        ptr = Rot([P.ps([128, 1024], BF16) for _ in range(2)])
        pas = Rot([P.ps() for _ in range(2)])
        pss = Rot([P.ps() for _ in range(2)])
        pys = Rot([P.ps() for _ in range(2)])
        cp = Rot(["dve", "act"])
        for (t0, n, v) in (TILES[:8] if skip_ctx else TILES):
            nb = n // 128
            xt = xts.next()
            gt = gts.next()
            ai = ain.next()
            si = sin_.next()
            P.dma("sp", xt[:, :, 0:n], k.XR[:, :, t0:t0 + n].rearrange("c p t -> p c t"), W=[xt])
            P.dma("sp", gt[:, :, 0:n], k.G[:, :, t0:t0 + n].rearrange("c p t -> p c t"), W=[gt])
            P.dma("sp", ai[:, 0:nb, :], k.AO[t0:t0 + n, :].rearrange("(b p) f -> p b f", p=128), W=[ai])
            P.dma("sp", si[:, 0:nb, :], k.SO[t0:t0 + n, :].rearrange("(b p) f -> p b f", p=128), W=[si])
            af = aof.next()
            sf = sof.next()
            for b in range(nb):
                for (src, dstt, nchunk) in ((ai, af, 8), (si, sf, 16)):
                    for c8 in range(nchunk // 8):
                        pt = ptr.next()
                        for j in range(8):
                            c = c8 * 8 + j
                            P.S.add("pe", (lambda e, o=pt[:, j * 128:(j + 1) * 128], i=src[:, b, c * 128:(c + 1) * 128]:
                                           e.transpose(o, i, identb[:])), _bufs([src, identb]), _bufs([pt]))
                        dsta = dstt[:, c8 * 8:(c8 + 1) * 8, b * 128:(b + 1) * 128]
                        srca = pt[:, :].rearrange("p (c t) -> p c t", c=8)
                        if cp.next() == "dve":
                            P.op("dve", "tensor_copy", [pt], [dstt], out=dsta, in_=srca)
                        else:
                            P.op("act", "activation", [pt], [dstt], out=dsta, in_=srca, func=AF.Copy)
            mt = mts.next()
            for dc in range(NCH):
                pa = pas.next()
                for c in range(8):
                    P.mm(pa[:, 0:n], wna[:, c, dc * 128:(dc + 1) * 128], af[:, c, 0:n], c == 0, c == 7, [wnb, af], [pa])
                ps = pss.next()
                for c in range(16):
                    P.mm(ps[:, 0:n], wso[:, c, dc * 128:(dc + 1) * 128], sf[:, c, 0:n], c == 0, c == 15, [wsb, sf], [ps])
                t1 = t1s.next()
                t2 = t2s.next()
                P.op("dve", "tensor_tensor", [pa, gt], [t1], out=t1[:, 0:n], in0=pa[:, 0:n], in1=gt[:, dc, 0:n], op=ALU.mult)
                P.op("dve", "tensor_tensor", [ps, gt], [t2], out=t2[:, 0:n], in0=ps[:, 0:n], in1=gt[:, 8 + dc, 0:n],
                     op=ALU.mult)
                P.op("dve", "tensor_tensor", [t1, t2], [mt], out=mt[:, dc, 0:n], in0=t1[:, 0:n], in1=t2[:, 0:n], op=ALU.add)
            for dc in range(NCH):
                py = pys.next()
                for c in range(8):
                    P.mm(py[:, 0:n], wou[:, c, dc * 128:(dc + 1) * 128], mt[:, c, 0:n], c == 0, c == 7, [wob, mt], [py])
                P.op("dve", "scalar_tensor_tensor", [py, xt, ms], [xt], out=xt[:, dc, 0:n], in0=py[:, 0:n],
                     scalar=ms[:, 5, dc, v:v + 1], in1=xt[:, dc, 0:n], op0=ALU.mult, op1=ALU.add)
            P.store("pool", k.XR[:, :, t0:t0 + n].rearrange("c p t -> p c t"), xt[:, :, 0:n], R=[xt])


def build_program(cfg):
    nc = bass.Bass("TRN2", target_bir_lowering=False)
    k = K()
    k.nc = nc
    k.cfg = cfg
    dbg = cfg.get("debug", ())

    def inp(name, shape):
        return dram(nc, name, shape, F32, kind="ExternalInput")

    k.dbg_copies = []

    def scr(name, shape, dt):
        if name in dbg and dt == BF16:
            a = dram(nc, name + "_bf", shape, dt)
            k.dbg_copies.append((a, dram(nc, name, shape, F32, kind="ExternalOutput")))
            return a
        return dram(nc, name, shape, dt, kind="ExternalOutput" if name in dbg else "Internal")

    k.x = inp("x", [NL, D])
    k.ctx = inp("ctx", [CT, D])
    k.cvec = inp("cvec", [128, NCH, 2])
    k.ident = inp("ident", [128, 128])
    k.ones = inp("ones", [128, 128])
    k.w_ada = inp("w_ada", [DEPTH, D, 9 * D])
    k.b_ada = inp("b_ada", [DEPTH, 128, 72])
    k.norm_ffn1 = inp("norm_ffn1", [DEPTH, 128, NCH])
    k.norm_mix = inp("norm_mix", [DEPTH, 128, NCH])
    k.norm_ffn2 = inp("norm_ffn2", [DEPTH, 128, NCH])
    k.ffn1_w_gate = inp("ffn1_w_gate", [DEPTH, D, DFF])
    k.ffn1_w_up = inp("ffn1_w_up", [DEPTH, D, DFF])
    k.ffn1_w_down = inp("ffn1_w_down", [DEPTH, DFF, D])
    k.ffn2_w_gate = inp("ffn2_w_gate", [DEPTH, D, DFF])
    k.ffn2_w_up = inp("ffn2_w_up", [DEPTH, D, DFF])
    k.ffn2_w_down = inp("ffn2_w_down", [DEPTH, DFF, D])
    k.w_in = inp("w_in", [DEPTH, D, NIN])
    k.q_norm = inp("q_norm", [DEPTH, 128, 1])
    k.k_norm = inp("k_norm", [DEPTH, 128, 1])
    k.blk = inp("blk", [128, 128])
    k.rot = inp("rot", [128, 128])
    k.cosT = inp("cosT", [128, NL])
    k.sinS = inp("sinS", [128, NL])
    k.conv_w = inp("conv_w", [DEPTH, 128, 24, 5])
    k.conv_b = inp("conv_b", [DEPTH, 128, 24])
    k.RB = inp("RB", [DEPTH, 16, 128, 1024])
    k.maskF = inp("maskF", [128, 1024])
    k.maskI = inp("maskI", [128, 1024])
    for nm in ("triF", "triB", "nm2F", "nm2B", "negI", "ntriF", "ntriB"):
        setattr(k, nm, inp(nm, [128, 128]))
    k.dt_bias_bc = inp("dt_bias_bc", [DEPTH, 128, 64])
    k.a_log_bc = inp("a_log_bc", [DEPTH, 128, 64])
    k.ssd_d_bc = inp("ssd_d_bc", [DEPTH, 128, 32])
    k.ssd_norm_bc = inp("ssd_norm_bc", [DEPTH, 128, 2048])
    k.na_w_o = inp("na_w_o", [DEPTH, D, D])
    k.ssd_w_o = inp("ssd_w_o", [DEPTH, 2048, D])
    k.w_out = inp("w_out", [DEPTH, D, D])
    k.out = dram(nc, "out", [NL, D], F32, kind="ExternalOutput")
    k.XR = scr("XR", [NCH, 128, NT], F32)
    k.MODS = scr("MODS", [128, 9 * NCH * 2], F32)
    k.HS = scr("HS", [NCH, 128, NT], BF16)
    k.Q = scr("Q", [8, 128, NT], BF16)
    k.KK = scr("KK", [8, 128, NT], BF16)
    k.V = scr("V", [NT, 16 * 65], BF16)
    k.Z = scr("Z", [NT, 2048], F32)
    k.DT = scr("DT", [NT, 64], F32)
    k.XBC = scr("XBC", [24, 128, NT], F32)
    k.G = scr("G", [16, 128, NT], F32)
    k.XS = scr("XS", [NT, 2048], F32)
    k.BF = scr("BF", [4, 128, NT], BF16)
    k.CF = scr("CF", [4, 128, NT], BF16)
    k.BT = scr("BT", [NT, 512], BF16)
    k.AO = scr("AO", [NT, D], BF16)
    k.YP = scr("YP", [NT, 2048], F32)
    k.SO = scr("SO", [NT, 2048], BF16)

    nl = cfg.get("n_layers", DEPTH)
    stop = cfg.get("stop", None)
    skip = cfg.get("skip", ())
    ph_load_x(k)
    done = False
    for l in range(nl):
        last = (l == DEPTH - 1)
        need_ctx = not last
        plist = [("ada", lambda: ph_adaln(k, l)), ("ffn1", lambda: ph_ffn(k, l, 1)),
                 ("hmix", lambda: ph_hmix(k, l)), ("qkv", lambda: ph_qkv(k, l)), ("zdt", lambda: ph_zdt(k, l)),
                 ("xbc", lambda: ph_proj_fm(k, l, "xbc", OX, 24, k.XBC, False)),
                 ("g", lambda: ph_proj_fm(k, l, "g", OG, 16, k.G, True)),
                 ("conv", lambda: ph_conv(k, l)), ("attn", lambda: ph_attn(k, l, need_ctx)),
                 ("ssda", lambda: ph_ssd_a(k, l, need_ctx)), ("ssdb", lambda: ph_ssd_b(k, l, need_ctx)),
                 ("mrg", lambda: ph_merge(k, l, last)),
                 ("ffn2", lambda: ph_ffn(k, l, 2, skip_ctx=last))]
        for name, fn in plist:
            if name in skip:
                continue
            fn()
            if stop == (l, name):
                done = True
                break
        if done:
            break
    if k.dbg_copies:
        with Phase(nc, "dbg") as P:
            for (src, dst) in k.dbg_copies:
                n0 = src.shape[0]
                step = max(1, n0 // 8)
                for i in range(0, n0, step):
                    P.store("pool", dst[i:i + step], src[i:i + step])
    ph_store_out(k)
    return nc


def _fm(vec, nchunk):
    s = vec.shape[:-1]
    return np.ascontiguousarray(np.swapaxes(vec.reshape(s + (nchunk, 128)), -1, -2))


def prepare_inputs(inputs, b):
    f = np.float32
    m = {}
    m["x"] = np.ascontiguousarray(inputs["x"][b], dtype=f)
    m["ctx"] = np.ascontiguousarray(inputs["ctx"][b], dtype=f)
    cv = np.stack([inputs["c"][b], inputs["c_ctx"]], axis=-1).astype(f)
    m["cvec"] = np.ascontiguousarray(cv.reshape(NCH, 128, 2).transpose(1, 0, 2))
    m["w_ada"] = np.ascontiguousarray(inputs["w_ada"], dtype=f)
    m["b_ada"] = _fm(np.asarray(inputs["b_ada"], dtype=f), 72)
    for nm in ("norm_ffn1", "norm_mix", "norm_ffn2"):
        m[nm] = _fm(np.asarray(inputs[nm], dtype=f), NCH)
    for nm in ("ffn1_w_gate", "ffn1_w_up", "ffn1_w_down", "ffn2_w_gate", "ffn2_w_up", "ffn2_w_down",
               "w_in", "na_w_o", "ssd_w_o", "w_out"):
        m[nm] = np.ascontiguousarray(inputs[nm], dtype=f)
    m["q_norm"] = np.ascontiguousarray(np.tile(np.asarray(inputs["q_norm"], dtype=f), (1, 2))[:, :, None])
    m["k_norm"] = np.ascontiguousarray(np.tile(np.asarray(inputs["k_norm"], dtype=f), (1, 2))[:, :, None])
    m.update(_consts())
    m["conv_w"] = np.ascontiguousarray(
        np.asarray(inputs["ssd_conv_w"], dtype=f).reshape(DEPTH, 5, 24, 128).transpose(0, 3, 2, 1))
    m["conv_b"] = _fm(np.asarray(inputs["ssd_conv_b"], dtype=f), 24)
    m["RB"] = _gather_rpb(np.asarray(inputs["na_rpb"], dtype=f))
    m["dt_bias_bc"] = np.ascontiguousarray(
        np.broadcast_to(np.asarray(inputs["ssd_dt_bias"], dtype=f).reshape(DEPTH, 1, 64), (DEPTH, 128, 64)))
    m["a_log_bc"] = np.ascontiguousarray(
        np.broadcast_to(np.asarray(inputs["ssd_a_log"], dtype=f).reshape(DEPTH, 1, 64), (DEPTH, 128, 64)))
    m["ssd_d_bc"] = np.ascontiguousarray(
        np.broadcast_to(np.asarray(inputs["ssd_d"], dtype=f).reshape(DEPTH, 1, 32), (DEPTH, 128, 32)))
    m["ssd_norm_bc"] = np.ascontiguousarray(
        np.broadcast_to(np.asarray(inputs["ssd_norm"], dtype=f).reshape(DEPTH, 1, 2048), (DEPTH, 128, 2048)))
    return m


_CONSTS = {}


def _rpb_index():
    a = np.arange(2)[:, None, None, None]
    kc = np.arange(64)[None, :, None, None]
    s_ = np.arange(16)[None, None, :, None]
    qc = np.arange(64)[None, None, None, :]
    dr = 7 - s_ + a + 0 * kc + 0 * qc
    dc = kc - qc + 0 * a + 0 * s_
    scol = np.clip(qc - 8, 0, 48)
    colvalid = (kc >= scol) & (kc < scol + 16) & (np.abs(dr) <= 7)
    return dr, dc, colvalid


def _gather_rpb(rpb):
    dr, dc, _ = _rpb_index()
    di = np.clip(dr + 7, 0, 14)
    ci = np.clip(dc + 15, 0, 30)
    out = rpb[:, :, di, ci]
    return np.ascontiguousarray(out.reshape(rpb.shape[0], rpb.shape[1], 128, 1024))


def _consts():
    if _CONSTS:
        return _CONSTS
    f = np.float32
    c = _CONSTS
    idx = np.arange(128)
    c["ident"] = np.eye(128, dtype=f)
    c["ones"] = np.ones((128, 128), dtype=f)
    c["blk"] = (idx[:, None] // 64 == idx[None, :] // 64).astype(f)
    partner = np.where((idx % 64) < 32, idx + 32, idx - 32)
    rot = np.zeros((128, 128), dtype=f)
    rot[partner, idx] = 1.0
    c["rot"] = rot
    t = np.arange(NL)
    row = (t // 64).astype(np.float64)
    col = (t % 64).astype(np.float64)
    inv = 10000.0 ** (-np.arange(16, dtype=np.float64) / 16)
    ang = np.concatenate([row[:, None] * inv, col[:, None] * inv], axis=-1)
    ang = ang.astype(f).astype(np.float64)
    j = idx % 64
    cosT = np.cos(ang[:, j % 32]).T
    sinT = np.sin(ang[:, j % 32]).T
    sign = np.where(j < 32, -1.0, 1.0)[:, None]
    c["cosT"] = np.ascontiguousarray(cosT.astype(f))
    c["sinS"] = np.ascontiguousarray((sinT * sign).astype(f))
    dr, dc, colvalid = _rpb_index()
    c["maskF"] = np.ascontiguousarray(colvalid.astype(f).reshape(128, 1024))
    c["maskI"] = np.ascontiguousarray((colvalid & (dr >= -4) & (dr <= 3)).astype(f).reshape(128, 1024))
    kk = idx[:, None]
    ii = idx[None, :]
    c["triF"] = (kk <= ii).astype(f)
    c["triB"] = (kk >= ii).astype(f)
    c["nm2F"] = (ii < kk).astype(f)
    c["nm2B"] = (ii > kk).astype(f)
    c["negI"] = (-BIG * np.eye(128)).astype(f)
    c["ntriF"] = -c["triF"]
    c["ntriB"] = -c["triB"]
    return c


_CACHE = {}


def run(inputs, cfg, n_cores=8):
    key = repr(sorted(cfg.items()))
    if key not in _CACHE:
        _CACHE[key] = build_program(cfg)
    nc = _CACHE[key]
    shared = None
    in_maps = []
    base = None
    for core in range(n_cores):
        m = prepare_inputs(inputs, core % 4)
        if base is None:
            base = m
        else:
            for kk in m:
                if kk not in ("x", "ctx", "cvec"):
                    m[kk] = base[kk]
        in_maps.append(m)
    res = run_bass_kernel_spmd(nc, in_maps, core_ids=list(range(n_cores)))
    return res


def kernel(**inputs):
    inputs = {k_: np.asarray(v) for k_, v in inputs.items()}
    res = run(inputs, {})
    out = np.stack([res.results[b]["out"] for b in range(4)], axis=0)
    return out.astype(np.float32)
```
